# Optimizing a Trainium2 kernel written in Bass

```python
import jax, jax.numpy as jnp
from jax import lax
import numpy as np

D_MODEL = 1024
BATCH = 8
SEQ = 4096
DEPTH = 2

N_A_LAYERS = DEPTH // 2
N_B_LAYERS = DEPTH - N_A_LAYERS

HEAD_DIM = 64
EPS = 1e-6
ROPE_THETA = 10000.0

MLSTM_HEADS = 4
MLSTM_DV = (3 * D_MODEL // 4) // MLSTM_HEADS
MLSTM_DQK = MLSTM_DV // 2
MLSTM_QK_W = MLSTM_HEADS * MLSTM_DQK
MLSTM_V_W = MLSTM_HEADS * MLSTM_DV
MLSTM_CHUNK = 64
GATE_SOFTCAP = 15.0
M_INIT = -1e30

SWA_Q_HEADS = (3 * D_MODEL // 4) // HEAD_DIM
SWA_KV_HEADS = 4
SWA_Q_W = SWA_Q_HEADS * HEAD_DIM
SWA_KV_W = SWA_KV_HEADS * HEAD_DIM
SWA_WINDOW = 128

MEM_TOKENS = 256
MEM_HEADS = 4
MEM_HEAD_DIM = (D_MODEL // 4) // MEM_HEADS
MEM_W = MEM_HEADS * MEM_HEAD_DIM

A_IN_W = 2 * MLSTM_QK_W + 2 * MLSTM_V_W + 2 * MLSTM_HEADS + MEM_W
B_IN_W = SWA_Q_W + MEM_W
MIX_W = MLSTM_V_W + MEM_W

D_FF = ((8 * D_MODEL // 3 + 255) // 256) * 256

kernel_name = "yoco_mlstm_swa_sink_hybrid"


def rmsnorm(x, g):
    xf = x.astype(jnp.float32)
    y = xf * lax.rsqrt(jnp.mean(xf * xf, axis=-1, keepdims=True) + EPS)
    return (y * g.astype(jnp.float32)).astype(x.dtype)


def rope_tables(positions):
    inv = 1.0 / (ROPE_THETA ** (jnp.arange(0, HEAD_DIM, 2, dtype=jnp.float32) / HEAD_DIM))
    ang = positions.astype(jnp.float32)[..., None] * inv
    return jnp.cos(ang), jnp.sin(ang)


def apply_rope(t, cos, sin):
    t1, t2 = jnp.split(t.astype(jnp.float32), 2, axis=-1)
    c = cos[:, :, None, :]
    s = sin[:, :, None, :]
    return jnp.concatenate([t1 * c - t2 * s, t2 * c + t1 * s], axis=-1).astype(t.dtype)


def mlstm_chunkwise(q, k, v, i_pre, f_pre):
    f32 = jnp.float32
    B_, S_, H, dqk = q.shape
    L = MLSTM_CHUNK
    NC = S_ // L

    def chunks(t):
        return t.astype(f32).reshape(B_, NC, L, H, -1).transpose(0, 3, 1, 2, 4)

    qc = chunks(q)
    kc = chunks(k) * (dqk ** -0.5)
    vc = chunks(v)
    ig = i_pre.astype(f32).reshape(B_, NC, L, H).transpose(0, 3, 1, 2)
    logf = jax.nn.log_sigmoid(f_pre.astype(f32)).reshape(B_, NC, L, H).transpose(0, 3, 1, 2)
    b = jnp.cumsum(logf, axis=-1)
    g = b[..., -1]
    a = g[..., None] - b + ig

    def step(carry, inp):
        C, n, m = carry
        g_c, a_c, k_c, v_c = inp
        m_new = jnp.maximum(g_c + m, jnp.max(a_c, axis=-1))
        decay = jnp.exp(g_c + m - m_new)
        w = jnp.exp(a_c - m_new[..., None])
        C_new = decay[..., None, None] * C + jnp.einsum('bhl,bhlv,bhlk->bhvk', w, v_c, k_c)
        n_new = decay[..., None] * n + jnp.einsum('bhl,bhlk->bhk', w, k_c)
        return (C_new, n_new, m_new), (C, n, m)

    dv = v.shape[-1]
    init = (jnp.zeros((B_, H, dv, dqk), f32), jnp.zeros((B_, H, dqk), f32), jnp.full((B_, H), M_INIT, f32))
    xs = (g.transpose(2, 0, 1), a.transpose(2, 0, 1, 3), kc.transpose(2, 0, 1, 3, 4), vc.transpose(2, 0, 1, 3, 4))
    _, (C_st, n_st, m_st) = lax.scan(step, init, xs)
    C_st = C_st.transpose(1, 2, 0, 3, 4)
    n_st = n_st.transpose(1, 2, 0, 3)
    m_st = m_st.transpose(1, 2, 0)

    causal = jnp.tril(jnp.ones((L, L), dtype=bool))
    dmat = jnp.where(causal, b[..., :, None] - b[..., None, :] + ig[..., None, :], -jnp.inf)
    inter_log = b + m_st[..., None]
    m_row = jnp.maximum(inter_log, jnp.max(dmat, axis=-1))
    scores = jnp.einsum('bhcjd,bhcsd->bhcjs', qc, kc) * jnp.exp(dmat - m_row[..., None])
    inter_w = jnp.exp(inter_log - m_row)
    num = (jnp.einsum('bhcjs,bhcsv->bhcjv', scores, vc)
           + inter_w[..., None] * jnp.einsum('bhcjk,bhcvk->bhcjv', qc, C_st))
    den = jnp.sum(scores, axis=-1) + inter_w * jnp.einsum('bhcjk,bhck->bhcj', qc, n_st)
    h = num / jnp.maximum(jnp.abs(den), jnp.exp(-m_row))[..., None]
    return h.transpose(0, 2, 3, 1, 4).reshape(B_, S_, H, dv)


def mlstm_mixer(h, w_in, b_gates, g_out):
    B_, S_, _ = h.shape
    p = h @ w_in
    q, k, v, o, gates, mq = jnp.split(
        p, [MLSTM_QK_W, 2 * MLSTM_QK_W, 2 * MLSTM_QK_W + MLSTM_V_W, 2 * MLSTM_QK_W + 2 * MLSTM_V_W,
            2 * MLSTM_QK_W + 2 * MLSTM_V_W + 2 * MLSTM_HEADS], axis=-1)
    gates = gates.astype(jnp.float32) + b_gates.astype(jnp.float32)
    gates = GATE_SOFTCAP * jnp.tanh(gates / GATE_SOFTCAP)
    i_pre, f_pre = jnp.split(gates, 2, axis=-1)
    hm = mlstm_chunkwise(q.reshape(B_, S_, MLSTM_HEADS, MLSTM_DQK), k.reshape(B_, S_, MLSTM_HEADS, MLSTM_DQK),
                         v.reshape(B_, S_, MLSTM_HEADS, MLSTM_DV), i_pre, f_pre)
    hm = hm * lax.rsqrt(jnp.mean(hm * hm, axis=-1, keepdims=True) + EPS)
    hm = hm * g_out.astype(jnp.float32).reshape(MLSTM_HEADS, MLSTM_DV)
    hm = jax.nn.sigmoid(o.astype(jnp.float32)) * hm.reshape(B_, S_, MLSTM_V_W)
    return hm.astype(h.dtype), mq.reshape(B_, S_, MEM_HEADS, MEM_HEAD_DIM)


def swa_sink_attention(q, k, v, sinks):
    B_, S_, Hq, dh = q.shape
    Hkv = k.shape[2]
    G = Hq // Hkv
    W = SWA_WINDOW
    NB = S_ // W
    qb = q.reshape(B_, NB, W, Hkv, G, dh)

    def band(t):
        tb = jnp.pad(t, ((0, 0), (W, 0), (0, 0), (0, 0))).reshape(B_, NB + 1, W, Hkv, dh)
        return jnp.concatenate([tb[:, :-1], tb[:, 1:]], axis=2)

    kband = band(k)
    vband = band(v)
    s = jnp.einsum('bnqhgd,bnkhd->bnhgqk', qb, kband).astype(jnp.float32) * (dh ** -0.5)
    qpos = jnp.arange(W)[:, None] + W
    kpos = jnp.arange(2 * W)[None, :]
    diff = qpos - kpos
    in_band = (diff >= 0) & (diff < W)
    blk = jnp.arange(NB)[:, None, None]
    valid = in_band[None] & (blk * W + kpos[None] - W >= 0)
    s = jnp.where(valid[None, :, None, None, :, :], s, -jnp.inf)
    sink = sinks.astype(jnp.float32).reshape(Hkv, G)[None, None, :, :, None, None]
    m = jnp.maximum(jnp.max(s, axis=-1, keepdims=True), sink)
    p = jnp.exp(s - m)
    p = p / (jnp.sum(p, axis=-1, keepdims=True) + jnp.exp(sink - m))
    o = jnp.einsum('bnhgqk,bnkhd->bnqhgd', p.astype(v.dtype), vband)
    return o.reshape(B_, S_, Hq * dh)


def swa_mixer(h, w_in, sinks, k_sh, v_sh, cos, sin):
    B_, S_, _ = h.shape
    p = h @ w_in
    q, mq = jnp.split(p, [SWA_Q_W], axis=-1)
    q = apply_rope(q.reshape(B_, S_, SWA_Q_HEADS, HEAD_DIM), cos, sin)
    o = swa_sink_attention(q, k_sh, v_sh, sinks)
    return o, mq.reshape(B_, S_, MEM_HEADS, MEM_HEAD_DIM)


def memory_attention(mq, mk, mv):
    B_, S_, Hm, dh = mq.shape
    s = jnp.einsum('bqhd,bmhd->bhqm', mq, mk).astype(jnp.float32) * (dh ** -0.5)
    p = jax.nn.softmax(s, axis=-1)
    o = jnp.einsum('bhqm,bmhd->bqhd', p.astype(mv.dtype), mv)
    return o.reshape(B_, S_, Hm * dh)


def shared_kv(x, g_kv, w_kv, cos, sin):
    B_, S_, _ = x.shape
    kv = rmsnorm(x, g_kv) @ w_kv
    k, v = jnp.split(kv, 2, axis=-1)
    k = apply_rope(k.reshape(B_, S_, SWA_KV_HEADS, HEAD_DIM), cos, sin)
    return k, v.reshape(B_, S_, SWA_KV_HEADS, HEAD_DIM)


def swiglu(h, w_in, w_out):
    gate, up = jnp.split(h @ w_in, 2, axis=-1)
    return (jax.nn.silu(gate) * up) @ w_out


def setup_inputs(seed: int = 0) -> dict:
    key = jax.random.key(seed)
    ks = jax.random.split(key, 24)
    f32 = jnp.float32

    def w(k, shape, fan_in):
        return jax.random.normal(k, shape, f32) * (fan_in ** -0.5)

    def gain(k, shape):
        return 1.0 + 0.05 * jax.random.normal(k, shape, f32)

    x = jax.random.normal(ks[0], (BATCH, SEQ, D_MODEL), f32)
    mem = jax.random.normal(ks[1], (BATCH, MEM_TOKENS, D_MODEL), f32)
    offset = jax.random.randint(ks[2], (BATCH,), 0, 1024, dtype=jnp.int32)
    positions = (offset[:, None] + jnp.arange(SEQ, dtype=jnp.int32)[None, :]).astype(jnp.int32)

    i_bias = 0.1 * jax.random.normal(ks[3], (N_A_LAYERS, MLSTM_HEADS), f32)
    f_bias = 3.0 + 3.0 * jax.random.uniform(ks[4], (N_A_LAYERS, MLSTM_HEADS), f32)
    return {
        "x": x,
        "mem": mem,
        "positions": positions,
        "g_mix_pre": gain(ks[5], (DEPTH, D_MODEL)),
        "g_mix_post": gain(ks[6], (DEPTH, D_MODEL)),
        "g_ffn_pre": gain(ks[7], (DEPTH, D_MODEL)),
        "g_ffn_post": gain(ks[8], (DEPTH, D_MODEL)),
        "g_mem": gain(ks[9], (DEPTH, D_MODEL)),
        "w_mem_kv": w(ks[10], (DEPTH, D_MODEL, 2 * MEM_W), D_MODEL),
        "w_out": w(ks[11], (DEPTH, MIX_W, D_MODEL), MIX_W),
        "w_ffn_in": w(ks[12], (DEPTH, D_MODEL, 2 * D_FF), D_MODEL),
        "w_ffn_out": w(ks[13], (DEPTH, D_FF, D_MODEL), D_FF),
        "w_in_a": w(ks[14], (N_A_LAYERS, D_MODEL, A_IN_W), D_MODEL),
        "b_gates_a": jnp.concatenate([i_bias, f_bias], axis=-1),
        "g_mlstm_out": gain(ks[15], (N_A_LAYERS, MLSTM_V_W)),
        "g_kv": gain(ks[16], (D_MODEL,)),
        "w_kv": w(ks[17], (D_MODEL, 2 * SWA_KV_W), D_MODEL),
        "w_in_b": w(ks[18], (N_B_LAYERS, D_MODEL, B_IN_W), D_MODEL),
        "sinks_b": 0.5 * jax.random.normal(ks[19], (N_B_LAYERS, SWA_Q_HEADS), f32),
    }


def reference(x, mem, positions, g_mix_pre, g_mix_post, g_ffn_pre, g_ffn_post, g_mem, w_mem_kv, w_out,
              w_ffn_in, w_ffn_out, w_in_a, b_gates_a, g_mlstm_out, g_kv, w_kv, w_in_b, sinks_b):
    B_ = x.shape[0]
    cos, sin = rope_tables(positions)
    k_sh = None
    v_sh = None
    for l in range(DEPTH):
        h = rmsnorm(x, g_mix_pre[l])
        mk, mv = jnp.split(rmsnorm(mem, g_mem[l]) @ w_mem_kv[l], 2, axis=-1)
        mk = mk.reshape(B_, MEM_TOKENS, MEM_HEADS, MEM_HEAD_DIM)
        mv = mv.reshape(B_, MEM_TOKENS, MEM_HEADS, MEM_HEAD_DIM)
        if l < N_A_LAYERS:
            main, mq = mlstm_mixer(h, w_in_a[l], b_gates_a[l], g_mlstm_out[l])
        else:
            j = l - N_A_LAYERS
            main, mq = swa_mixer(h, w_in_b[j], sinks_b[j], k_sh, v_sh, cos, sin)
        mo = memory_attention(mq, mk, mv)
        mix = jnp.concatenate([main, mo], axis=-1) @ w_out[l]
        x = x + rmsnorm(mix, g_mix_post[l])
        x = x + rmsnorm(swiglu(rmsnorm(x, g_ffn_pre[l]), w_ffn_in[l], w_ffn_out[l]), g_ffn_post[l])
        if l == N_A_LAYERS - 1:
            k_sh, v_sh = shared_kv(x, g_kv, w_kv, cos, sin)
    return x
```

```python
import math
import os
from contextlib import ExitStack

import numpy as np
import concourse.bass as bass
import concourse.mybir as mybir
from concourse.bass_utils import run_bass_kernel_spmd

F32 = mybir.dt.float32
BF16 = mybir.dt.bfloat16
I32 = mybir.dt.int32
AF = mybir.ActivationFunctionType
ALU = mybir.AluOpType

D = 1024
S = 4096
TB = 512
NBLK = S // TB
DFF = 2816
NFC = DFF // 128
EPS = 1e-6
DQK = 96
DV = 192
PAIRS = [(0, 3), (1, 4), (2, 5), (6, 9), (7, 10), (8, 11)]
TWO_PI = 2.0 * math.pi

WSHAPES = {
    "w_mem_kv0": [D, 512], "w_mem_kv1": [D, 512], "w_in_a": [D, 2568], "w_out0": [D, D],
    "w_ffn_in0": [D, 2 * DFF], "w_ffn_out0": [DFF, D], "w_kv": [D, 512], "w_in_b": [D, D],
    "w_out1": [D, D], "w_ffn_in1": [D, 2 * DFF], "w_ffn_out1": [DFF, D],
}
G_MIXPRE, G_MIXPOST, G_FFNPRE, G_FFNPOST, G_MEM = 0, 8, 16, 24, 32
G_KV = 80
G_MOUT = 88
G_BG = 96
G_SINK = 104
G_INVF = 116
NG = 117

SAME_ENGINE_SYNC = True
NDQ = 8


class Sched:
    def __init__(self, nc, st):
        self.nc = nc
        self.E = {"pe": nc.tensor, "act": nc.scalar, "dve": nc.vector, "pool": nc.gpsimd, "sp": nc.sync}
        self.semh = {}
        for k in self.E:
            self.semh[k] = st.enter_context(nc.semaphore("c_" + k))
        self.cnt = {k: 0 for k in self.E}
        self.seen = {}
        self.lastw = {}
        self.rd = {}
        self.dq = {}
        self.dqi = {}
        for e in ("sp", "pool"):
            self.dq[e] = []
            for i in range(NDQ):
                nm = "d_%s%d" % (e, i)
                self.semh[nm] = st.enter_context(nc.semaphore(nm))
                self.dq[e].append([nm, 0])
            self.dqi[e] = 0
        self.out_tokens = []
        self.n_instr = 0
        self.n_uniq = 0
        self.st = st

    def _need(self, r, w):
        need = {}

        def add(t):
            if t is not None and need.get(t[0], 0) < t[1]:
                need[t[0]] = t[1]

        for k in r:
            add(self.lastw.get(k))
        for k in w:
            add(self.lastw.get(k))
            for s, v in self.rd.get(k, {}).items():
                add((s, v))
        return need

    def _wait(self, e, s, v):
        if s == e and (e == "pe" or not SAME_ENGINE_SYNC):
            return
        if self.seen.get((e, s), 0) >= v:
            return
        self.E[e].wait_ge(self.semh[s], v)
        self.seen[(e, s)] = v

    def _record(self, tok, r, w):
        for k in r:
            d = self.rd.setdefault(k, {})
            if d.get(tok[0], 0) < tok[1]:
                d[tok[0]] = tok[1]
        for k in w:
            self.lastw[k] = tok
            self.rd[k] = {}

    def op(self, e, fn, r=(), w=()):
        for s, v in self._need(r, w).items():
            self._wait(e, s, v)
        ins = fn(self.E[e])
        self.cnt[e] += 1
        ins.then_inc(self.semh[e], 1)
        self._record((e, self.cnt[e]), r, w)
        self.n_instr += 1

    def dma(self, e, out, in_, r=(), w=(), is_out=False):
        for s, v in self._need(r, w).items():
            self._wait(e, s, v)
        slot = self.dq[e][self.dqi[e]]
        self.dqi[e] = (self.dqi[e] + 1) % NDQ
        nm, tot = slot
        if tot > 0:
            self._wait(e, nm, tot)
        self.E[e].dma_start(out=out, in_=in_).then_inc(self.semh[nm], 16)
        slot[1] = tot + 16
        tok = (nm, tot + 16)
        self._record(tok, r, w)
        if is_out:
            self.out_tokens.append(tok)
        self.n_instr += 1

    def dma_unique(self, e, out, in_, r=(), w=()):
        for s, v in self._need(r, w).items():
            self._wait(e, s, v)
        nm = "u_%d" % self.n_uniq
        self.n_uniq += 1
        self.semh[nm] = self.st.enter_context(self.nc.semaphore(nm))
        self.E[e].dma_start(out=out, in_=in_).then_inc(self.semh[nm], 16)
        self._record((nm, 16), r, w)
        self.n_instr += 1

    def finish(self):
        for s, v in self.out_tokens:
            self._wait("sp", s, v)
        self.E["sp"].nop()


def build(nblk=NBLK, stage=4):
    nc = bass.Bass("TRN2", target_bir_lowering=False)

    def din(name, shape, dt=F32):
        return nc.dram_tensor(name, shape, dt, kind="ExternalInput").ap()

    x_d = din("x", [S, D])
    mem_d = din("mem", [256, D])
    pos_d = din("posr", [128, S], I32)
    W = {n: din(n, sh) for n, sh in WSHAPES.items()}
    gcol_d = din("gcol", [128, NG])
    cst_d = din("cst", [128, 512])
    out_d = nc.dram_tensor("out", [S, D], F32, kind="ExternalOutput").ap()
    SCR = {n: nc.dram_tensor("s_" + n, sh, BF16, kind="Internal").ap() for n, sh in WSHAPES.items()}

    with ExitStack() as st:
        def sb(name, shape, dt):
            return st.enter_context(nc.sbuf_tensor("sb_" + name, shape, dt))

        S_ = Sched(nc, st)
        xT = sb("xT", [128, 8, 512], F32)
        big32 = sb("big32", [128, 4096], F32)
        hT = sb("hT", [128, 8, 512], BF16)
        sq = sb("sq", [128, 4, 512], BF16)
        rs_tmp = sb("rs_tmp", [128, 512], F32)
        rstd = sb("rstd", [128, 512], F32)
        rstd_bf = sb("rstd_bf", [128, 512], BF16)
        rstd_pb = sb("rstd_pb", [128, 512], BF16)
        ws = [sb("ws%d" % i, [128, 6144], BF16) for i in range(3)]
        xstage = sb("xstage", [128, 4096], F32)
        A22 = sb("A22", [128, NFC, 512], BF16)
        mixT = sb("mixT", [128, 10, 512], BF16)
        gcol = sb("gcol", [128, NG], F32)
        cst = sb("cst", [128, 512], F32)
        ones_bf = sb("ones_bf", [128, 128], BF16)
        rm2_bf = sb("rm2_bf", [128, 128], BF16)
        tri4 = sb("tri4", [128, 512], BF16)
        ust4 = sb("ust4", [128, 512], BF16)
        ust4f = sb("ust4f", [128, 512], BF16)
        epsc = sb("epsc", [128, 1], F32)
        esink = sb("esink", [128, 6], F32)
        onesAB = sb("onesAB", [128, 2, 128], BF16)
        eb = sb("eb", [128, 2, 4, 128], F32)
        ea = sb("ea", [128, 2, 4, 128], F32)
        gsb = sb("gsb", [128, 2, 8], F32)
        gth = sb("gth", [128, 2, 8], F32)
        gee = sb("gee", [128, 2, 4], F32)
        glp = sb("glp", [128, 2, 4], F32)
        gig = sb("gig", [128, 2, 4], F32)
        ga2 = sb("ga2", [128, 2, 4], F32)
        gek = sb("gek", [128, 2, 4], F32)
        ktok = sb("ktok", [128, 2, 4, 96], BF16)
        kraw = sb("kraw", [128, 2, 384], BF16)
        vaug = sb("vaug", [128, 2, 4, 196], BF16)
        PTm = sb("PTm", [128, 2, 4, 128], BF16)
        CT = sb("CT", [128, 4, 193], F32)
        CTb = sb("CTb", [128, 4, 196], BF16)
        mqT = sb("mqT", [128, 2, 512], BF16)
        rdt = sb("rdt", [128, 2, 512], F32)
        sgt = sb("sgt", [128, 2, 512], BF16)
        cc2 = sb("cc2", [128, 512], F32)
        ss2 = sb("ss2", [128, 512], F32)
        tmpA = sb("tmpA", [128, 512], F32)
        tmpB = sb("tmpB", [128, 512], F32)
        posi = sb("posi", [128, 512], I32)
        kint = posi
        kpad = sb("kpad", [128, 4, 640], BF16)
        vsh = sb("vsh", [128, 5, 4, 128], BF16)
        PT = sb("PT", [128, 8, 512], BF16)
        qraw = sgt
        mkT = sb("mkT", [128, 2, 4, 256], BF16)
        mv = sb("mv", [128, 2, 2, 4, 128], BF16)
        psb = [st.enter_context(nc.psum_tensor("ps%d" % i, [128, 512], F32)) for i in range(8)]

        ident = cst[:, 0:128]
        tri_f = cst[:, 128:256]
        ust_f = cst[:, 256:384]

        state = {"bank": 0, "slot": 0, "sq": 0, "rdt": 0, "sgt": 0, "pt": 0, "qraw": 0}

        def bank():
            b = state["bank"]
            state["bank"] = (b + 1) % 8
            return psb[b], ("ps", b)

        def rot(name, n):
            v = state[name]
            state[name] = (v + 1) % n
            return v

        b32keys = [("b32", k) for k in range(8)]

        def a22w(ci):
            return [("A22", ci)] + ([("qk", ci, t_) for t_ in range(4)] if 8 <= ci < 16 else [])
        xTk = lambda k: ("xT", k)
        hTk = lambda k: ("hT", k)
        hTkeys = [hTk(k) for k in range(8)]

        S_.dma("sp", gcol[:], gcol_d, w=["gcol"])
        S_.dma("sp", cst[:], cst_d, w=["cst"])
        S_.op("dve", lambda e: e.memset(ones_bf[:], 1.0), w=["ones"])
        S_.op("dve", lambda e: e.memset(epsc[:], EPS), w=["epsc"])
        S_.op("dve", lambda e: e.tensor_copy(out=rm2_bf[:], in_=cst[:, 384:512]), r=["cst"], w=["rm2"])
        for i in range(4):
            S_.op("dve", lambda e: e.tensor_copy(out=tri4[:, i * 128:(i + 1) * 128], in_=tri_f), r=["cst"], w=["tri4"])
            S_.op("dve", lambda e: e.tensor_copy(out=ust4[:, i * 128:(i + 1) * 128], in_=ust_f), r=["cst"], w=["ust4"])
            if i > 0:
                S_.op("dve", lambda e: e.tensor_copy(out=ust4f[:, i * 128:(i + 1) * 128], in_=ust_f), r=["cst"], w=["ust4f"])
        S_.op("dve", lambda e: e.memset(ust4f[:, 0:128], 0.0), w=["ust4f"])
        S_.op("act", lambda e: e.activation(out=esink[:], in_=gcol[:, G_SINK:G_SINK + 6], func=AF.Exp), r=["gcol"], w=["esink"])
        S_.op("dve", lambda e: e.memset(onesAB[:], 0.0), w=["onesAB"])
        S_.op("dve", lambda e: e.memset(onesAB[:, 0, 0:64], 1.0), w=["onesAB"])
        S_.op("dve", lambda e: e.memset(onesAB[:, 1, 64:128], 1.0), w=["onesAB"])
        S_.op("pool", lambda e: e.memset(mkT[:], 0.0), w=[("mkT", 0), ("mkT", 1)])
        S_.op("pool", lambda e: e.memset(mv[:], 0.0), w=[("mv", 0), ("mv", 1)])
        S_.op("pool", lambda e: e.memset(kpad[:], 0.0), w=["kpad"])
        S_.op("pool", lambda e: e.memset(vsh[:], 0.0), w=["vsh"])
        S_.op("pool", lambda e: e.memset(CT[:], 0.0), w=["CT"])
        S_.op("pool", lambda e: e.memset(CTb[:], 0.0), w=["CTb"])
        S_.op("pool", lambda e: e.memset(vaug[:], 1.0), w=["vaug0", "vaug1"])

        scr_keys = {}

        def cast_rows(name, rows_per, gate=()):
            src, dst = W[name], SCR[name]
            n = src.shape[0]
            keys = []
            for i, r0 in enumerate(range(0, n, rows_per)):
                r1 = min(n, r0 + rows_per)
                k = ("scr", name, i)
                S_.dma_unique("pool", dst[r0:r1, :], src[r0:r1, :], r=gate, w=[k])
                keys.append(k)
            scr_keys[name] = keys

        def cast_in_b(gate=()):
            src, dst = W["w_in_b"], SCR["w_in_b"]
            keys = []
            i = 0
            for c, (a, b) in enumerate(PAIRS):
                for e_, h in enumerate((a, b)):
                    k = ("scr", "w_in_b", i)
                    i += 1
                    S_.dma_unique("pool", dst[:, c * 128 + e_ * 64:c * 128 + e_ * 64 + 64], src[:, h * 64:(h + 1) * 64], r=gate, w=[k])
                    keys.append(k)
            k = ("scr", "w_in_b", i)
            S_.dma_unique("pool", dst[:, 768:1024], src[:, 768:1024], r=gate, w=[k])
            keys.append(k)
            scr_keys["w_in_b"] = keys

        def cast_out1(gate=()):
            src, dst = W["w_out1"], SCR["w_out1"]
            keys = []
            i = 0
            for c, (a, b) in enumerate(PAIRS):
                for e_, h in enumerate((a, b)):
                    k = ("scr", "w_out1", i)
                    i += 1
                    S_.dma_unique("pool", dst[c * 128 + e_ * 64:c * 128 + e_ * 64 + 64, :], src[h * 64:(h + 1) * 64, :], r=gate, w=[k])
                    keys.append(k)
            k = ("scr", "w_out1", i)
            S_.dma_unique("pool", dst[768:1024, :], src[768:1024, :], r=gate, w=[k])
            keys.append(k)
            scr_keys["w_out1"] = keys

        jobs = []
        jstate = {"next": 0}
        LOOKAHEAD = 3

        def cast_cols(name, jobname, col_ranges, gate):
            keys = []
            for i, (c0, c1) in enumerate(col_ranges):
                k = ("scr", jobname, i)
                S_.dma_unique("pool", SCR[name][:, c0:c1], W[name][:, c0:c1], r=gate, w=[k])
                keys.append(k)
            return keys

        def cast_rowrange(name, jobname, r0, r1, gate):
            k = ("scr", jobname, 0)
            S_.dma_unique("pool", SCR[name][r0:r1, :], W[name][r0:r1, :], r=gate, w=[k])
            return [k]

        def whole(name):
            def f(gate):
                cast_rows(name, 512, gate)
                return scr_keys[name]
            return f

        jobs.append(("w_mem_kv0", whole("w_mem_kv0")))
        jobs.append(("w_mem_kv1", whole("w_mem_kv1")))
        jobs.append(("w_in_a", whole("w_in_a")))
        jobs.append(("w_out0", whole("w_out0")))

        def add_ffn_jobs(l):
            ni, no = "w_ffn_in%d" % l, "w_ffn_out%d" % l
            for u in range(8):
                ncol = 384 if u < 7 else 128
                jobs.append(("%s:u%d" % (ni, u), (lambda gate, u=u, ncol=ncol, ni=ni: cast_cols(
                    ni, "%s:u%d" % (ni, u), [(u * 384, u * 384 + ncol), (DFF + u * 384, DFF + u * 384 + ncol)], gate))))
            for g3 in range(3):
                r0, r1 = g3 * 1024, min(DFF, g3 * 1024 + 1024)
                jobs.append(("%s:g%d" % (no, g3), (lambda gate, g3=g3, r0=r0, r1=r1, no=no: cast_rowrange(no, "%s:g%d" % (no, g3), r0, r1, gate))))

        add_ffn_jobs(0)
        jobs.append(("w_kv", whole("w_kv")))

        def job_in_b(gate):
            cast_in_b(gate)
            return scr_keys["w_in_b"]

        def job_out1(gate):
            cast_out1(gate)
            return scr_keys["w_out1"]

        jobs.append(("w_in_b", job_in_b))
        jobs.append(("w_out1", job_out1))
        add_ffn_jobs(1)
        job_index = {n: i for i, (n, _) in enumerate(jobs)}

        def emit_jobs_upto(idx, gate):
            with nc.allow_non_contiguous_dma(reason="weight re-layout"):
                while jstate["next"] <= min(idx, len(jobs) - 1):
                    n, fn = jobs[jstate["next"]]
                    scr_keys[n] = fn(gate)
                    jstate["next"] += 1

        emit_jobs_upto(2, ())

        def wslot():
            i = state["slot"]
            state["slot"] = (i + 1) % 3
            return ws[i], ("ws", i)

        def wload(parts):
            slot, key = wslot()
            need = max(job_index[name] for _, _, name in parts)
            if jstate["next"] <= need:
                emit_jobs_upto(need, ())
            with nc.allow_non_contiguous_dma(reason="weight stream"):
                for fn, src, name in parts:
                    S_.dma("sp", fn(slot), src, r=scr_keys[name], w=[key])
            if jstate["next"] < len(jobs):
                emit_jobs_upto(max(need + LOOKAHEAD, jstate["next"]), [key])
            return slot, key

        def kview(name):
            return SCR[name].rearrange("(k p) n -> p k n", p=128)

        def stats_rstd(srcs, P, n, nfeat):
            sqs = []
            for ap, keys in srcs:
                i = rot("sq", 4)
                S_.op("act", lambda e: e.activation(out=sq[0:P, i, 0:n], in_=ap, func=AF.Square), r=keys, w=[("sq", i)])
                sqs.append(i)
            ps, pk = bank()
            for j, i in enumerate(sqs):
                S_.op("pe", lambda e: e.matmul(ps[0:P, 0:n], ones_bf[0:P, 0:P], sq[0:P, i, 0:n], start=(j == 0), stop=(j == len(sqs) - 1)),
                      r=[("sq", i), "ones"], w=[pk])
            S_.op("act", lambda e: e.activation(out=rs_tmp[0:P, 0:n], in_=ps[0:P, 0:n], func=AF.Ln, bias=epsc[0:P, :], scale=1.0 / nfeat),
                  r=[pk, "epsc"], w=["rs_tmp"])
            S_.op("act", lambda e: e.activation(out=rstd[0:P, 0:n], in_=rs_tmp[0:P, 0:n], func=AF.Exp, scale=-0.5), r=["rs_tmp"], w=["rstd"])

        def norm_to_hT(gbase, n=512, reuse_stats=False):
            if not reuse_stats:
                ps, pk = bank()
            for k in range(8):
                if not reuse_stats:
                    i = rot("sq", 4)
                    S_.op("act", lambda e: e.activation(out=sq[:, i, 0:n], in_=xT[:, k, 0:n], func=AF.Square), r=[xTk(k)], w=[("sq", i)])
                    S_.op("pe", lambda e: e.matmul(ps[:, 0:n], ones_bf[:], sq[:, i, 0:n], start=(k == 0), stop=(k == 7)),
                          r=[("sq", i), "ones"], w=[pk])
                S_.op("act", lambda e: e.activation(out=hT[:, k, 0:n], in_=xT[:, k, 0:n], func=AF.Copy, scale=gcol[:, gbase + k:gbase + k + 1]),
                      r=[xTk(k), "gcol"], w=[hTk(k)])
            if not reuse_stats:
                S_.op("act", lambda e: e.activation(out=rs_tmp[:, 0:n], in_=ps[:, 0:n], func=AF.Ln, bias=epsc[:], scale=1.0 / D),
                      r=[pk, "epsc"], w=["rs_tmp"])
                S_.op("act", lambda e: e.activation(out=rstd_bf[:, 0:n], in_=rs_tmp[:, 0:n], func=AF.Exp, scale=-0.5), r=["rs_tmp"], w=["rstd_bf"])
            for k in range(8):
                S_.op("dve", lambda e: e.tensor_tensor(out=hT[:, k, 0:n], in0=hT[:, k, 0:n], in1=rstd_bf[:, 0:n], op=ALU.mult),
                      r=[hTk(k), "rstd_bf"], w=[hTk(k)])

        def b32(k):
            return big32[:, k * 512:(k + 1) * 512]

        y16_all = big32[:].bitcast(BF16)

        def y16(k):
            return y16_all[:, k * 512:(k + 1) * 512]

        def y16k(k):
            return ("b32", k // 2)

        def post_norm_residual(gbase):
            ps, pk = bank()
            for k in range(8):
                i = rot("sq", 4)
                S_.op("act", lambda e: e.activation(out=sq[:, i, :], in_=y16(k), func=AF.Square), r=[y16k(k)], w=[("sq", i)])
                S_.op("pe", lambda e: e.matmul(ps[:, :], ones_bf[:], sq[:, i, :], start=(k == 0), stop=(k == 7)),
                      r=[("sq", i), "ones"], w=[pk])
            S_.op("act", lambda e: e.activation(out=rs_tmp[:], in_=ps[:], func=AF.Ln, bias=epsc[:], scale=1.0 / D),
                  r=[pk, "epsc"], w=["rs_tmp"])
            S_.op("act", lambda e: e.activation(out=rstd_pb[:], in_=rs_tmp[:], func=AF.Exp, scale=-0.5), r=["rs_tmp"], w=["rstd_pb"])
            for k in range(8):
                S_.op("dve", lambda e: e.scalar_tensor_tensor(out=y16(k), in0=y16(k), scalar=gcol[:, gbase + k:gbase + k + 1],
                                                             in1=rstd_pb[:], op0=ALU.mult, op1=ALU.mult),
                      r=[y16k(k), "rstd_pb", "gcol"], w=[y16k(k)])
                S_.op("pool" if k in (1, 4, 6) else "dve", lambda e: e.tensor_tensor(out=xT[:, k, :], in0=xT[:, k, :], in1=y16(k), op=ALU.add),
                      r=[y16k(k), xTk(k)], w=[xTk(k)])

        def load_transpose(src_rows, ntile):
            S_.dma("sp", big32[:, 0:ntile * 1024].rearrange("p (t d) -> p t d", t=ntile),
                   src_rows.rearrange("(t p) d -> p t d", p=128), w=b32keys[0:2 * ntile])
            for k in range(8):
                ps, pk = bank()
                for t in range(ntile):
                    S_.op("pe", lambda e: e.transpose(ps[:, t * 128:(t + 1) * 128], big32[:, t * 1024 + k * 128:t * 1024 + (k + 1) * 128], ident),
                          r=[("b32", 2 * t), ("b32", 2 * t + 1), "cst"], w=[pk])
                S_.op("act", lambda e: e.activation(out=xT[:, k, 0:ntile * 128], in_=ps[:, 0:ntile * 128], func=AF.Copy), r=[pk], w=[xTk(k)])

        xskeys = [("xs", t) for t in range(4)]

        def prefetch_x(blk):
            S_.dma("sp", xstage[:].rearrange("p (t d) -> p t d", t=4),
                   x_d[blk * TB:(blk + 1) * TB, :].rearrange("(t p) d -> p t d", p=128), w=xskeys)

        def transpose_x():
            for k in range(8):
                ps, pk = bank()
                for t in range(4):
                    S_.op("pe", lambda e: e.transpose(ps[:, t * 128:(t + 1) * 128], xstage[:, t * 1024 + k * 128:t * 1024 + (k + 1) * 128], ident),
                          r=[("xs", t), "cst"], w=[pk])
                S_.op("act", lambda e: e.activation(out=xT[:, k, :], in_=ps[:, :], func=AF.Copy), r=[pk], w=[xTk(k)])

        def store_out(blk):
            for t in range(4):
                for half in range(2):
                    ps, pk = bank()
                    for kk in range(4):
                        k = half * 4 + kk
                        S_.op("pe", lambda e: e.transpose(ps[:, kk * 128:(kk + 1) * 128], xT[:, k, t * 128:(t + 1) * 128], ident),
                              r=[xTk(k), "cst"], w=[pk])
                    S_.op("act", lambda e: e.activation(out=big32[:, t * 1024 + half * 512:t * 1024 + half * 512 + 512], in_=ps[:], func=AF.Copy),
                          r=[pk], w=[("b32", 2 * t + half)])
            S_.dma("sp", out_d[blk * TB:(blk + 1) * TB, :].rearrange("(t p) d -> p t d", p=128),
                   big32[:].rearrange("p (t d) -> p t d", t=4), r=b32keys, is_out=True)

        def mem_attention(l, mix_base):
            def scores(p):
                res = []
                for e_ in range(2):
                    h = 2 * p + e_
                    idx = []
                    for mc in range(2):
                        ps, pk = bank()
                        S_.op("pe", lambda e: e.matmul(ps[:, :], mkT[:, l, h, mc * 128:(mc + 1) * 128], mqT[:, p, :], start=True, stop=True),
                              r=[("mkT", l), ("mqT", p)], w=[pk])
                        i = rot("pt", 8)
                        S_.op("act", lambda e: e.activation(out=PT[:, i, :], in_=ps[:, :], func=AF.Exp), r=[pk], w=[("PT", i)])
                        idx.append(i)
                    res.append((h, idx))
                return res

            def outp(p, res):
                pso, pko = bank()
                psd, pkd = bank()
                n = 0
                for e_, (h, idx) in enumerate(res):
                    for mc in range(2):
                        S_.op("pe", lambda e: e.matmul(pso[:, :], mv[:, l, mc, h, :], PT[:, idx[mc], :], start=(n == 0), stop=(n == 3)),
                              r=[("mv", l), ("PT", idx[mc])], w=[pko])
                        n += 1
                n = 0
                for e_, (h, idx) in enumerate(res):
                    for mc in range(2):
                        S_.op("pe", lambda e: e.matmul(psd[:, :], onesAB[:, e_, :], PT[:, idx[mc], :], start=(n == 0), stop=(n == 3)),
                              r=["onesAB", ("PT", idx[mc])], w=[pkd])
                        n += 1
                ri = rot("rdt", 2)
                S_.op("act", lambda e: e.activation(out=rdt[:, ri, :], in_=psd[:, :], func=AF.Ln), r=[pkd], w=[("rdt", ri)])
                S_.op("act", lambda e: e.activation(out=rdt[:, ri, :], in_=rdt[:, ri, :], func=AF.Exp, scale=-1.0), r=[("rdt", ri)], w=[("rdt", ri)])
                S_.op("dve", lambda e: e.tensor_tensor(out=mixT[:, mix_base + p, :], in0=pso[:, :], in1=rdt[:, ri, :], op=ALU.mult),
                      r=[pko, ("rdt", ri)], w=[("mixT", mix_base + p)])

            r0 = scores(0)
            r1 = scores(1)
            outp(0, r0)
            outp(1, r1)

        def out_proj(l, chunks):
            name = "w_out%d" % l
            nch = len(chunks)
            for half in range(2):
                cols = slice(half * 512, (half + 1) * 512)
                if l == 0:
                    parts = [
                        (lambda s: s[0:96, 0:8 * 512].rearrange("p (c n) -> p c n", c=8),
                         SCR[name][0:768, cols].rearrange("(c p) n -> p c n", p=96), name),
                        (lambda s: s[:, 8 * 512:10 * 512].rearrange("p (c n) -> p c n", c=2),
                         SCR[name][768:1024, cols].rearrange("(c p) n -> p c n", p=128), name),
                    ]
                else:
                    parts = [
                        (lambda s: s[:, 0:8 * 512].rearrange("p (c n) -> p c n", c=8),
                         SCR[name][:, cols].rearrange("(c p) n -> p c n", p=128), name),
                    ]
                slot, sk = wload(parts)
                for dcl in range(4):
                    dc = half * 4 + dcl
                    ps, pk = bank()
                    for ci, P in enumerate(chunks):
                        S_.op("pe", lambda e: e.matmul(ps[:, :], slot[0:P, ci * 512 + dcl * 128:ci * 512 + (dcl + 1) * 128], mixT[0:P, ci, :],
                                                      start=(ci == 0), stop=(ci == nch - 1)),
                              r=[sk, ("mixT", ci)], w=[pk])
                    S_.op("dve", lambda e: e.tensor_copy(out=y16(dc), in_=ps[:, :]), r=[pk], w=[y16k(dc)])

        def ffn(l, hook=None):
            norm_to_hT(40 * l + G_FFNPRE)
            if hook is not None:
                hook()
            name = "w_ffn_in%d" % l
            kv = kview(name)
            for u in range(8):
                ncol = 384 if u < 7 else 128
                jn = "%s:u%d" % (name, u)
                parts = [
                    (lambda s: s[:, 0:8 * ncol].rearrange("p (k n) -> p k n", k=8), kv[:, :, u * 384:u * 384 + ncol], jn),
                    (lambda s: s[:, 8 * ncol:16 * ncol].rearrange("p (k n) -> p k n", k=8), kv[:, :, DFF + u * 384:DFF + u * 384 + ncol], jn),
                ]
                slot, sk = wload(parts)
                for j in range(ncol // 128):
                    f = u * 3 + j
                    psg, pkg = bank()
                    psu, pku = bank()
                    for kc in range(8):
                        S_.op("pe", lambda e: e.matmul(psg[:, :], slot[:, kc * ncol + j * 128:kc * ncol + (j + 1) * 128], hT[:, kc, :],
                                                      start=(kc == 0), stop=(kc == 7)), r=[sk, hTk(kc)], w=[pkg])
                    for kc in range(8):
                        S_.op("pe", lambda e: e.matmul(psu[:, :], slot[:, 8 * ncol + kc * ncol + j * 128:8 * ncol + kc * ncol + (j + 1) * 128], hT[:, kc, :],
                                                      start=(kc == 0), stop=(kc == 7)), r=[sk, hTk(kc)], w=[pku])
                    si = rot("sgt", 2)
                    S_.op("act", lambda e: e.activation(out=sgt[:, si, :], in_=psg[:, :], func=AF.Silu), r=[pkg], w=[("sgt", si)])
                    S_.op("dve", lambda e: e.tensor_tensor(out=A22[:, f, :], in0=psu[:, :], in1=sgt[:, si, :], op=ALU.mult),
                          r=[pku, ("sgt", si)], w=a22w(f))
            name = "w_ffn_out%d" % l
            kv = kview(name)
            for half in range(2):
                banks = [bank() for _ in range(4)]
                for g3 in range(3):
                    kc0, kc1 = g3 * 8, min(NFC, g3 * 8 + 8)
                    nk = kc1 - kc0
                    parts = [(lambda s: s[:, 0:nk * 512].rearrange("p (k n) -> p k n", k=nk), kv[:, kc0:kc1, half * 512:(half + 1) * 512], "%s:g%d" % (name, g3))]
                    slot, sk = wload(parts)
                    for dcl in range(4):
                        ps, pk = banks[dcl]
                        for kc in range(kc0, kc1):
                            S_.op("pe", lambda e: e.matmul(ps[:, :], slot[:, (kc - kc0) * 512 + dcl * 128:(kc - kc0) * 512 + (dcl + 1) * 128], A22[:, kc, :],
                                                          start=(kc == 0), stop=(kc == NFC - 1)), r=[sk, ("A22", kc)], w=[pk])
                for dcl in range(4):
                    ps, pk = banks[dcl]
                    dc = half * 4 + dcl
                    S_.op("dve", lambda e: e.tensor_copy(out=y16(dc), in_=ps[:, :]), r=[pk], w=[y16k(dc)])
            post_norm_residual(40 * l + G_FFNPOST)

        def rope_tables(blk):
            S_.dma("sp", posi[:], pos_d[:, blk * TB:(blk + 1) * TB], w=["posi"])
            S_.op("dve", lambda e: e.tensor_copy(out=tmpA[:], in_=posi[:]), r=["posi"], w=["tmpA"])
            S_.op("dve", lambda e: e.tensor_scalar_mul(out=tmpA[:], in0=tmpA[:], scalar1=gcol[:, G_INVF:G_INVF + 1]),
                  r=["tmpA", "gcol"], w=["tmpA"])
            for dst, shift, key in ((ss2, 0.0, "ss2"), (cc2, 0.5 * math.pi, "cc2")):
                S_.op("dve", lambda e: e.tensor_scalar(out=tmpB[:], in0=tmpA[:], scalar1=shift, scalar2=1.0 / TWO_PI, op0=ALU.add, op1=ALU.mult),
                      r=["tmpA"], w=["tmpB"])
                S_.op("dve", lambda e: e.tensor_copy(out=kint[:], in_=tmpB[:]), r=["tmpB"], w=["posi"])
                S_.op("dve", lambda e: e.tensor_copy(out=tmpB[:], in_=kint[:]), r=["posi"], w=["tmpB"])
                S_.op("dve", lambda e: e.scalar_tensor_tensor(out=tmpB[:], in0=tmpB[:], scalar=-TWO_PI, in1=tmpA[:], op0=ALU.mult, op1=ALU.add),
                      r=["tmpB", "tmpA"], w=["tmpB"])
                S_.op("dve", lambda e: e.tensor_scalar(out=tmpB[:], in0=tmpB[:], scalar1=shift, scalar2=-math.pi, op0=ALU.add, op1=ALU.max),
                      r=["tmpB"], w=["tmpB"])
                S_.op("dve", lambda e: e.tensor_scalar(out=tmpB[:], in0=tmpB[:], scalar1=math.pi, scalar2=0.0, op0=ALU.min, op1=ALU.add),
                      r=["tmpB"], w=["tmpB"])
                S_.op("act", lambda e: e.activation(out=dst[:], in_=tmpB[:], func=AF.Sin), r=["tmpB"], w=[key])

        def rope_evac(ps, pk, outs):
            qi = rot("sgt", 2)
            dbgm = os.environ.get("DBG_ROPE", "")
            S_.op("act", lambda e: e.activation(out=qraw[:, qi, :], in_=ps[:, :], func=AF.Copy), r=[pk], w=[("sgt", qi)])
            if dbgm == "1":
                return
            pr, pkr = bank()
            if dbgm != "4":
                S_.op("pe", lambda e: e.matmul(pr[:, :], rm2_bf[:], qraw[:, qi, :], start=True, stop=True), r=["rm2", ("sgt", qi)], w=[pkr])
            if dbgm == "3":
                return
            S_.op("dve", lambda e: e.tensor_tensor(out=tmpA[:], in0=ps[:, :], in1=cc2[:], op=ALU.mult), r=[pk, "cc2", ("sgt", qi)], w=["tmpA"])
            if dbgm == "4":
                return
            S_.op("dve", lambda e: e.tensor_tensor(out=tmpB[:], in0=pr[:, :], in1=ss2[:], op=ALU.mult), r=[pkr, "ss2"], w=["tmpB"])
            if dbgm == "2":
                return
            for (p0, p1), oap, wk in outs:
                S_.op("pool", lambda e: e.tensor_tensor(out=oap, in0=tmpA[p0:p1, :], in1=tmpB[p0:p1, :], op=ALU.add), r=["tmpA", "tmpB"], w=wk)

        prefetch_x(0)
        load_transpose(mem_d, 2)
        for l in range(2):
            norm_to_hT(40 * l + G_MEM, n=256)
            name = "w_mem_kv%d" % l
            slot, sk = wload([(lambda s: s[:, 0:8 * 512].rearrange("p (k n) -> p k n", k=8), kview(name), name)])
            for p in range(2):
                ps, pk = bank()
                for kc in range(8):
                    S_.op("pe", lambda e: e.matmul(ps[:, 0:256], slot[:, kc * 512 + p * 128:kc * 512 + (p + 1) * 128], hT[:, kc, 0:256],
                                                  start=(kc == 0), stop=(kc == 7)), r=[sk, hTk(kc)], w=[pk])
                S_.op("act", lambda e: e.activation(out=mkT[0:64, l, 2 * p, :], in_=ps[0:64, 0:256], func=AF.Copy, scale=0.125), r=[pk], w=[("mkT", l)])
                S_.op("act", lambda e: e.activation(out=mkT[64:128, l, 2 * p + 1, :], in_=ps[64:128, 0:256], func=AF.Copy, scale=0.125), r=[pk], w=[("mkT", l)])
            for mc in range(2):
                ps, pk = bank()
                for kc in range(8):
                    S_.op("pe", lambda e: e.matmul(ps[:, 0:256], hT[:, kc, mc * 128:(mc + 1) * 128], slot[:, kc * 512 + 256:kc * 512 + 512],
                                                  start=(kc == 0), stop=(kc == 7)), r=[sk, hTk(kc)], w=[pk])
                for h in range(4):
                    S_.op("act", lambda e: e.activation(out=mv[:, l, mc, h, (h % 2) * 64:(h % 2) * 64 + 64], in_=ps[:, h * 64:(h + 1) * 64], func=AF.Copy),
                          r=[pk], w=[("mv", l)])

        kva = kview("w_in_a")
        for blk in range(nblk):
            transpose_x()
            if blk + 1 < nblk:
                prefetch_x(blk + 1)

            norm_to_hT(G_MIXPRE)
            zk = []
            for ci_ in range(8, 16):
                zk += a22w(ci_)
            S_.op("pool", lambda e: e.memset(A22[96:128, 8:16, :], 0.0), w=zk)
            slot, sk = wload([(lambda s: s[:, 0:8 * 768].rearrange("p (k n) -> p k n", k=8), kva[:, :, 0:768], "w_in_a")])
            for qk in range(2):
                for h in range(4):
                    ps, pk = bank()
                    c0 = qk * 384 + h * 96
                    for kc in range(8):
                        S_.op("pe", lambda e: e.matmul(ps[0:96, :], slot[:, kc * 768 + c0:kc * 768 + c0 + 96], hT[:, kc, :], start=(kc == 0), stop=(kc == 7)),
                              r=[sk, hTk(kc)], w=[pk])
                    ci = 8 + qk * 4 + h
                    S_.op("act", lambda e: e.activation(out=A22[0:96, ci, :], in_=ps[0:96, :], func=AF.Copy), r=[pk], w=a22w(ci))
            slot, sk = wload([(lambda s: s[:, 0:8 * 768].rearrange("p (k n) -> p k n", k=8), kva[:, :, 1536:2304], "w_in_a")])
            for c in range(8):
                ps, pk = bank()
                for kc in range(8):
                    S_.op("pe", lambda e: e.matmul(ps[0:96, :], slot[:, kc * 768 + c * 96:kc * 768 + (c + 1) * 96], hT[:, kc, :], start=(kc == 0), stop=(kc == 7)),
                          r=[sk, hTk(kc)], w=[pk])
                S_.op("act", lambda e: e.activation(out=A22[0:96, c, :], in_=ps[0:96, :], func=AF.Sigmoid), r=[pk], w=[("A22", c)])
            slotA, skA = wload([(lambda s: s[:, 0:8 * 768].rearrange("p (k n) -> p k n", k=8), kva[:, :, 384:1152], "w_in_a")])
            slotB, skB = wload([
                (lambda s: s[:, 0:8 * 384].rearrange("p (k n) -> p k n", k=8), kva[:, :, 1152:1536], "w_in_a"),
                (lambda s: s[:, 8 * 384:8 * 384 + 64].rearrange("p (k n) -> p k n", k=8), kva[:, :, 2304:2312], "w_in_a"),
                (lambda s: s[:, 4096:4096 + 8 * 256].rearrange("p (k n) -> p k n", k=8), kva[:, :, 2312:2568], "w_in_a"),
            ])
            for p in range(2):
                ps, pk = bank()
                for kc in range(8):
                    S_.op("pe", lambda e: e.matmul(ps[:, :], slotB[:, 4096 + kc * 256 + p * 128:4096 + kc * 256 + (p + 1) * 128], hT[:, kc, :],
                                                  start=(kc == 0), stop=(kc == 7)), r=[skB, hTk(kc)], w=[pk])
                S_.op("act", lambda e: e.activation(out=mqT[:, p, :], in_=ps[:, :], func=AF.Copy), r=[pk], w=[("mqT", p)])

            qTv = lambda tsl: A22[0:96, 8:12, tsl]
            kTv = lambda tsl: A22[0:96, 12:16, tsl]
            qkeys_t = lambda t_: [("qk", 8 + h, t_) for h in range(4)]
            kkeys_t = lambda t_: [("qk", 12 + h, t_) for h in range(4)]
            def tnames(t):
                pb = t % 2
                return pb, slice(t * 128, (t + 1) * 128), "vaug%d" % pb, "g%d" % pb, "eb%d" % pb, "ea%d" % pb, "ktok%d" % pb, "PTm%d" % pb, "kraw%d" % pb

            def tileA1(t):
                pb, tsl, vk, gk, ebk, eak, ktk, ptk, krk = tnames(t)
                psk, pkk = bank()
                psv0, pkv0 = bank()
                psv1, pkv1 = bank()
                psg, pkg = bank()
                for kc in range(8):
                    S_.op("pe", lambda e: e.matmul(psg[:, 0:8], hT[:, kc, tsl], slotB[:, 8 * 384 + kc * 8:8 * 384 + (kc + 1) * 8], start=(kc == 0), stop=(kc == 7)),
                          r=[skB, hTk(kc)], w=[pkg])
                for kc in range(8):
                    S_.op("pe", lambda e: e.matmul(psk[:, 0:384], hT[:, kc, tsl], slotA[:, kc * 768:kc * 768 + 384], start=(kc == 0), stop=(kc == 7)),
                          r=[skA, hTk(kc)], w=[pkk])
                for kc in range(8):
                    S_.op("pe", lambda e: e.matmul(psv0[:, 0:384], hT[:, kc, tsl], slotA[:, kc * 768 + 384:kc * 768 + 768], start=(kc == 0), stop=(kc == 7)),
                          r=[skA, hTk(kc)], w=[pkv0])
                for kc in range(8):
                    S_.op("pe", lambda e: e.matmul(psv1[:, 0:384], hT[:, kc, tsl], slotB[:, kc * 384:(kc + 1) * 384], start=(kc == 0), stop=(kc == 7)),
                          r=[skB, hTk(kc)], w=[pkv1])
                S_.op("dve", lambda e: e.tensor_tensor(out=gsb[:, pb, :], in0=psg[:, 0:8], in1=gcol[:, G_BG:G_BG + 8], op=ALU.add), r=[pkg, "gcol"], w=[gk + "sb"])
                S_.op("act", lambda e: e.activation(out=gth[:, pb, :], in_=gsb[:, pb, :], func=AF.Exp, scale=-2.0 / 15.0), r=[gk + "sb"], w=[gk + "th"])
                S_.op("dve", lambda e: e.tensor_scalar(out=gsb[:, pb, :], in0=gth[:, pb, :], scalar1=1.0, scalar2=0.0, op0=ALU.add, op1=ALU.add), r=[gk + "th"], w=[gk + "sb"])
                S_.op("dve", lambda e: e.reciprocal(out=gsb[:, pb, :], in_=gsb[:, pb, :]), r=[gk + "sb"], w=[gk + "sb"])
                S_.op("dve", lambda e: e.tensor_scalar(out=gth[:, pb, :], in0=gth[:, pb, :], scalar1=-1.0, scalar2=1.0, op0=ALU.mult, op1=ALU.add), r=[gk + "th"], w=[gk + "th"])
                S_.op("dve", lambda e: e.tensor_tensor(out=gth[:, pb, :], in0=gth[:, pb, :], in1=gsb[:, pb, :], op=ALU.mult), r=[gk + "th", gk + "sb"], w=[gk + "th"])
                S_.op("act", lambda e: e.activation(out=gee[:, pb, :], in_=gth[:, pb, 4:8], func=AF.Exp, scale=-15.0), r=[gk + "th"], w=[gk + "ee"])
                S_.op("act", lambda e: e.activation(out=glp[:, pb, :], in_=gee[:, pb, :], func=AF.Ln, bias=1.0), r=[gk + "ee"], w=[gk + "lp"])
                S_.op("dve", lambda e: e.tensor_scalar(out=gig[:, pb, :], in0=gth[:, pb, 0:4], scalar1=15.0, scalar2=0.0, op0=ALU.mult, op1=ALU.add),
                      r=[gk + "th"], w=[gk + "ig"])
                S_.op("act", lambda e: e.activation(out=kraw[:, pb, :], in_=psk[:, 0:384], func=AF.Copy), r=[pkk], w=[krk])
                S_.op("act", lambda e: e.activation(out=vaug[:, pb, 0:2, 0:192], in_=psv0[:, 0:384].rearrange("p (h c) -> p h c", h=2), func=AF.Copy),
                      r=[pkv0], w=[vk])
                S_.op("act", lambda e: e.activation(out=vaug[:, pb, 2:4, 0:192], in_=psv1[:, 0:384].rearrange("p (h c) -> p h c", h=2), func=AF.Copy),
                      r=[pkv1], w=[vk])

            def tileA2a(t):
                pb, tsl, vk, gk, ebk, eak, ktk, ptk, krk = tnames(t)
                pcs, pkcs = bank()
                pa, pka = bank()
                pu, pku = bank()
                for h in range(4):
                    S_.op("pe", lambda e: e.matmul(pcs[0:96, h * 128:(h + 1) * 128], glp[:, pb, h:h + 1].broadcast_to([128, 96]), tri_f, start=True, stop=True),
                          r=[gk + "lp", "cst"], w=[pkcs])
                for h in range(4):
                    S_.op("pe", lambda e: e.matmul(pa[0:96, h * 128:(h + 1) * 128], glp[:, pb, h:h + 1].broadcast_to([128, 96]), tri_f, start=True, stop=False),
                          r=[gk + "lp", "cst"], w=[pka])
                    S_.op("pe", lambda e: e.matmul(pa[0:96, h * 128:(h + 1) * 128], gig[:, pb, h:h + 1].broadcast_to([128, 96]), ident, start=False, stop=True),
                          r=[gk + "ig", "cst"], w=[pka])
                S_.op("pe", lambda e: e.matmul(pu[:, 0:4], ust_f, glp[:, pb, :], start=True, stop=True), r=[gk + "lp", "cst"], w=[pku])
                S_.op("act", lambda e: e.activation(out=eb[0:96, pb, :, :], in_=pcs[0:96, :].rearrange("p (h j) -> p h j", h=4), func=AF.Exp, scale=-1.0),
                      r=[pkcs], w=[ebk])
                S_.op("act", lambda e: e.activation(out=ea[0:96, pb, :, :], in_=pa[0:96, :].rearrange("p (h j) -> p h j", h=4), func=AF.Exp),
                      r=[pka], w=[eak])
                S_.op("dve", lambda e: e.tensor_tensor(out=ga2[:, pb, :], in0=gig[:, pb, :], in1=pu[:, 0:4], op=ALU.subtract), r=[gk + "ig", pku], w=[gk + "a2"])
                S_.op("act", lambda e: e.activation(out=gek[:, pb, :], in_=ga2[:, pb, :], func=AF.Exp), r=[gk + "a2"], w=[gk + "ek"])
                S_.op("dve", lambda e: e.tensor_tensor(out=qTv(tsl), in0=qTv(tsl), in1=eb[0:96, pb, :, :], op=ALU.mult), r=qkeys_t(t) + [ebk], w=qkeys_t(t))
                S_.op("dve", lambda e: e.scalar_tensor_tensor(out=kTv(tsl), in0=kTv(tsl), scalar=DQK ** -0.5, in1=ea[0:96, pb, :, :], op0=ALU.mult, op1=ALU.mult),
                      r=kkeys_t(t) + [eak], w=kkeys_t(t))
                S_.op("dve", lambda e: e.scalar_tensor_tensor(out=ktok[:, pb, :, :], in0=kraw[:, pb, :].rearrange("p (h c) -> p h c", h=4), scalar=DQK ** -0.5,
                                                             in1=gek[:, pb, :].unsqueeze(2).broadcast_to([128, 4, 96]), op0=ALU.mult, op1=ALU.mult),
                      r=[krk, gk + "ek"], w=[ktk])

            def tileA2b(t):
                pb, tsl, vk, gk, ebk, eak, ktk, ptk, krk = tnames(t)
                pss, pks = bank()
                for h in range(4):
                    S_.op("pe", lambda e: e.matmul(pss[:, h * 128:(h + 1) * 128], A22[:, 12 + h, tsl], A22[:, 8 + h, tsl], start=True, stop=True),
                          r=[("qk", 12 + h, t), ("qk", 8 + h, t)], w=[pks])
                S_.op("dve", lambda e: e.tensor_tensor(out=PTm[:, pb, :, :], in0=pss[:, :].rearrange("p (h j) -> p h j", h=4),
                                                      in1=tri_f.unsqueeze(1).broadcast_to([128, 4, 128]), op=ALU.mult), r=[pks, "cst"], w=[ptk])

            def tileBn(t):
                pb, tsl, vk, gk, ebk, eak, ktk, ptk, krk = tnames(t)
                pn = [bank(), bank()]
                pd, pkd = bank()
                for c in range(2):
                    for h in range(4):
                        S_.op("pe", lambda e: e.matmul(pn[c][0][0:96, h * 128:(h + 1) * 128], vaug[:, pb, h, c * 96:(c + 1) * 96], PTm[:, pb, h, :], start=True, stop=False),
                              r=[vk, ptk], w=[pn[c][1]])
                        S_.op("pe", lambda e: e.matmul(pn[c][0][0:96, h * 128:(h + 1) * 128], CTb[:, h, c * 96:(c + 1) * 96], A22[:, 8 + h, tsl], start=False, stop=True),
                              r=["CTb", ("qk", 8 + h, t)], w=[pn[c][1]])
                for h in range(4):
                    S_.op("pe", lambda e: e.matmul(pd[0:96, h * 128:(h + 1) * 128], ones_bf[:, 0:96], PTm[:, pb, h, :], start=True, stop=False),
                          r=["ones", ptk], w=[pkd])
                    S_.op("pe", lambda e: e.matmul(pd[0:96, h * 128:(h + 1) * 128], CTb[:, h, 192:193].broadcast_to([128, 96]), A22[:, 8 + h, tsl], start=False, stop=True),
                          r=["CTb", ("qk", 8 + h, t)], w=[pkd])
                ri = rot("rdt", 2)
                S_.op("act", lambda e: e.activation(out=rdt[0:96, ri, :], in_=pd[0:96, :], func=AF.Abs), r=[pkd], w=[("rdt", ri)])
                S_.op("dve", lambda e: e.tensor_scalar(out=rdt[0:96, ri, :], in0=rdt[0:96, ri, :], scalar1=1.0, scalar2=0.0, op0=ALU.max, op1=ALU.add),
                      r=[("rdt", ri)], w=[("rdt", ri)])
                S_.op("act", lambda e: e.activation(out=rdt[0:96, ri, :], in_=rdt[0:96, ri, :], func=AF.Ln), r=[("rdt", ri)], w=[("rdt", ri)])
                S_.op("act", lambda e: e.activation(out=rdt[0:96, ri, :], in_=rdt[0:96, ri, :], func=AF.Exp, scale=-1.0), r=[("rdt", ri)], w=[("rdt", ri)])
                for c in range(2):
                    hk = [("b32", c * 4 + h) for h in range(4)]
                    S_.op("dve", lambda e: e.tensor_tensor(out=big32[0:96, c * 2048:(c + 1) * 2048].rearrange("p (h n) -> p h n", h=4)[:, :, tsl],
                                                          in0=pn[c][0][0:96, :].rearrange("p (h j) -> p h j", h=4),
                                                          in1=rdt[0:96, ri, :].rearrange("p (h j) -> p h j", h=4), op=ALU.mult),
                          r=[pn[c][1], ("rdt", ri)], w=hk)

            def tileBu(t):
                pb, tsl, vk, gk, ebk, eak, ktk, ptk, krk = tnames(t)
                pU = [bank(), bank()]
                for h in range(4):
                    S_.op("pe", lambda e: e.matmul(pU[h // 2][0][0:96, (h % 2) * 193:(h % 2) * 193 + 193], ktok[:, pb, h, :], vaug[:, pb, h, 0:193], start=True, stop=True),
                          r=[ktk, vk], w=[pU[h // 2][1]])
                for h in range(4):
                    S_.op("dve", lambda e: e.scalar_tensor_tensor(out=CT[0:96, h, :], in0=CT[0:96, h, :], scalar=eb[0:96, pb, h, 127:128],
                                                                 in1=pU[h // 2][0][0:96, (h % 2) * 193:(h % 2) * 193 + 193], op0=ALU.mult, op1=ALU.add),
                          r=["CT", ebk, pU[h // 2][1]], w=["CT"])
                S_.op("pool", lambda e: e.tensor_copy(out=CTb[0:96, :, 0:193], in_=CT[0:96, :, :]), r=["CT"], w=["CTb"])

            tileA1(0)
            tileA2a(0)
            tileA2b(0)
            for t in range(4):
                if t + 1 < 4:
                    tileA1(t + 1)
                tileBn(t)
                if t + 1 < 4:
                    tileA2a(t + 1)
                tileBu(t)
                if t + 1 < 4:
                    tileA2b(t + 1)

            def hn_sq(h):
                idx = []
                for c in range(2):
                    k = c * 4 + h
                    i = rot("sq", 4)
                    S_.op("act", lambda e: e.activation(out=sq[0:96, i, :], in_=big32[0:96, k * 512:(k + 1) * 512], func=AF.Square), r=[("b32", k)], w=[("sq", i)])
                    idx.append(i)
                return idx

            def hn_fin(h, idx):
                ps, pk = bank()
                for j, i in enumerate(idx):
                    S_.op("pe", lambda e: e.matmul(ps[0:96, :], ones_bf[0:96, 0:96], sq[0:96, i, :], start=(j == 0), stop=(j == 1)),
                          r=[("sq", i), "ones"], w=[pk])
                ri = rot("rdt", 2)
                S_.op("act", lambda e: e.activation(out=rdt[0:96, ri, :], in_=ps[0:96, :], func=AF.Ln, bias=epsc[0:96, :], scale=1.0 / DV),
                      r=[pk, "epsc"], w=[("rdt", ri)])
                S_.op("act", lambda e: e.activation(out=rdt[0:96, ri, :], in_=rdt[0:96, ri, :], func=AF.Exp, scale=-0.5), r=[("rdt", ri)], w=[("rdt", ri)])
                for c in range(2):
                    k = c * 4 + h
                    ch = 2 * h + c
                    S_.op("dve", lambda e: e.scalar_tensor_tensor(out=big32[0:96, k * 512:(k + 1) * 512], in0=big32[0:96, k * 512:(k + 1) * 512],
                                                                 scalar=gcol[0:96, G_MOUT + ch:G_MOUT + ch + 1], in1=rdt[0:96, ri, :], op0=ALU.mult, op1=ALU.mult),
                          r=[("b32", k), ("rdt", ri), "gcol"], w=[("b32", k)])
                    S_.op("pool", lambda e: e.tensor_tensor(out=mixT[0:96, ch, :], in0=big32[0:96, k * 512:(k + 1) * 512], in1=A22[0:96, ch, :], op=ALU.mult),
                          r=[("b32", k), ("A22", ch)], w=[("mixT", ch)])

            hidx = {0: hn_sq(0), 1: hn_sq(1)}
            for h in range(4):
                hn_fin(h, hidx[h])
                if h + 2 < 4:
                    hidx[h + 2] = hn_sq(h + 2)
            mem_attention(0, 8)
            out_proj(0, [96] * 8 + [128] * 2)
            post_norm_residual(G_MIXPOST)
            if stage >= 2:
                ffn(0, hook=lambda: rope_tables(blk))
            if stage >= 2.5:
                norm_to_hT(G_KV)
                slot, sk = wload([(lambda s: s[:, 0:8 * 512].rearrange("p (k n) -> p k n", k=8), kview("w_kv"), "w_kv")])
                pend = None
                for c2 in range(2):
                    ps, pk = bank()
                    for kc in range(8):
                        S_.op("pe", lambda e: e.matmul(ps[:, :], slot[:, kc * 512 + c2 * 128:kc * 512 + (c2 + 1) * 128], hT[:, kc, :], start=(kc == 0), stop=(kc == 7)),
                              r=[sk, hTk(kc)], w=[pk])
                    if pend is not None:
                        rope_evac(*pend)
                    pend = (ps, pk, [((0, 64), kpad[0:64, 2 * c2, 128:640], [("kpad", 2 * c2)]),
                                     ((64, 128), kpad[64:128, 2 * c2 + 1, 128:640], [("kpad", 2 * c2 + 1)])])
                pend_kv = pend
                for t in range(4):
                    ps, pk = bank()
                    for kc in range(8):
                        S_.op("pe", lambda e: e.matmul(ps[:, 0:256], hT[:, kc, t * 128:(t + 1) * 128], slot[:, kc * 512 + 256:kc * 512 + 512], start=(kc == 0), stop=(kc == 7)),
                              r=[sk, hTk(kc)], w=[pk])
                    if t == 0:
                        rope_evac(*pend_kv)
                    for kvh_ in range(4):
                        S_.op("act", lambda e: e.activation(out=vsh[:, 1 + t, kvh_, (kvh_ % 2) * 64:(kvh_ % 2) * 64 + 64], in_=ps[:, kvh_ * 64:(kvh_ + 1) * 64], func=AF.Copy),
                              r=[pk], w=[("vsh", 1 + t)])

            if stage >= 2.75:
                norm_to_hT(40 + G_MIXPRE, reuse_stats=(stage >= 2.5))
                slot, sk = wload([(lambda s: s[:, 0:6144].rearrange("p (k n) -> p k n", k=8), kview("w_in_b")[:, :, 0:768], "w_in_b")])
                slotM, skM = wload([(lambda s: s[:, 0:2048].rearrange("p (k n) -> p k n", k=8), kview("w_in_b")[:, :, 768:1024], "w_in_b")])
                pend = None
                for c in range(6):
                    ps, pk = bank()
                    for kc in range(8):
                        S_.op("pe", lambda e: e.matmul(ps[:, :], slot[:, kc * 768 + c * 128:kc * 768 + (c + 1) * 128], hT[:, kc, :], start=(kc == 0), stop=(kc == 7)),
                              r=[sk, hTk(kc)], w=[pk])
                    if pend is not None:
                        rope_evac(*pend)
                    pend = (ps, pk, [((0, 128), A22[:, c, :], [("A22", c)])])
                pend_q = pend
                for p in range(2):
                    ps, pk = bank()
                    for kc in range(8):
                        S_.op("pe", lambda e: e.matmul(ps[:, :], slotM[:, kc * 256 + p * 128:kc * 256 + (p + 1) * 128], hT[:, kc, :], start=(kc == 0), stop=(kc == 7)),
                              r=[skM, hTk(kc)], w=[pk])
                    S_.op("act", lambda e: e.activation(out=mqT[:, p, :], in_=ps[:, :], func=AF.Copy), r=[pk], w=[("mqT", p)])
                    if p == 0:
                        rope_evac(*pend_q)
                def swa_scores(c):
                    res = []
                    for e_, hq in enumerate(PAIRS[c]):
                        kvh = hq // 3
                        pc, pkc = bank()
                        pp, pkp = bank()
                        for qt in range(4):
                            qs = slice(qt * 128, (qt + 1) * 128)
                            S_.op("pe", lambda e: e.matmul(pc[:, qs], kpad[:, kvh, 128 + qt * 128:256 + qt * 128], A22[:, c, qs], start=True, stop=True),
                                  r=[("kpad", kvh), ("A22", c)], w=[pkc])
                            S_.op("pe", lambda e: e.matmul(pp[:, qs], kpad[:, kvh, qt * 128:128 + qt * 128], A22[:, c, qs], start=True, stop=True),
                                  r=[("kpad", kvh), ("A22", c)], w=[pkp])
                        ic = rot("pt", 8)
                        ip = rot("pt", 8)
                        S_.op("act", lambda e: e.activation(out=PT[:, ic, :], in_=pc[:, :], func=AF.Exp, scale=0.125), r=[pkc], w=[("PT", ic)])
                        S_.op("act", lambda e: e.activation(out=PT[:, ip, :], in_=pp[:, :], func=AF.Exp, scale=0.125), r=[pkp], w=[("PT", ip)])
                        S_.op("dve", lambda e: e.tensor_tensor(out=PT[:, ic, :], in0=PT[:, ic, :], in1=tri4[:], op=ALU.mult), r=[("PT", ic), "tri4"], w=[("PT", ic)])
                        mk_ = ust4f if blk == 0 else ust4
                        S_.op("pool", lambda e: e.tensor_tensor(out=PT[:, ip, :], in0=PT[:, ip, :], in1=mk_[:], op=ALU.mult),
                              r=[("PT", ip), "ust4", "ust4f"], w=[("PT", ip)])
                        res.append((kvh, ic, ip))
                    return res

                def swa_out(c, res):
                    po, pko = bank()
                    pdn, pkdn = bank()
                    for qt in range(4):
                        qs = slice(qt * 128, (qt + 1) * 128)
                        n = 0
                        for (kvh, ic, ip) in res:
                            S_.op("pe", lambda e: e.matmul(po[:, qs], vsh[:, 1 + qt, kvh, :], PT[:, ic, qs], start=(n == 0), stop=False),
                                  r=[("vsh", 1 + qt), ("PT", ic)], w=[pko])
                            n += 1
                            S_.op("pe", lambda e: e.matmul(po[:, qs], vsh[:, qt, kvh, :], PT[:, ip, qs], start=False, stop=(n == 3)),
                                  r=[("vsh", qt), ("PT", ip)], w=[pko])
                            n += 1
                    n = 0
                    for e_, (kvh, ic, ip) in enumerate(res):
                        for i_ in (ic, ip):
                            S_.op("pe", lambda e: e.matmul(pdn[:, :], onesAB[:, e_, :], PT[:, i_, :], start=(n == 0), stop=(n == 3)), r=["onesAB", ("PT", i_)], w=[pkdn])
                            n += 1
                    ri = rot("rdt", 2)
                    S_.op("act", lambda e: e.activation(out=rdt[:, ri, :], in_=pdn[:, :], func=AF.Ln, bias=esink[:, c:c + 1]),
                          r=[pkdn, "esink"], w=[("rdt", ri)])
                    S_.op("act", lambda e: e.activation(out=rdt[:, ri, :], in_=rdt[:, ri, :], func=AF.Exp, scale=-1.0), r=[("rdt", ri)], w=[("rdt", ri)])
                    S_.op("dve", lambda e: e.tensor_tensor(out=mixT[:, c, :], in0=po[:, :], in1=rdt[:, ri, :], op=ALU.mult),
                          r=[pko, ("rdt", ri)], w=[("mixT", c)])

                prev = None
                for c in range(6 if stage >= 2.9 else 0):
                    cur = swa_scores(c)
                    if prev is not None:
                        swa_out(*prev)
                    prev = (c, cur)
                if prev is not None:
                    swa_out(*prev)
            if stage >= 3:
                S_.op("pool", lambda e: e.tensor_copy(out=kpad[:, :, 0:128], in_=kpad[:, :, 512:640]), r=[("kpad", i) for i in range(4)], w=[("kpad", i) for i in range(4)])
                S_.op("pool", lambda e: e.tensor_copy(out=vsh[:, 0, :, :], in_=vsh[:, 4, :, :]), r=[("vsh", 4)], w=[("vsh", 0)])
                mem_attention(1, 6)
                out_proj(1, [128] * 8)
                post_norm_residual(40 + G_MIXPOST)
            if stage >= 4:
                ffn(1)
            store_out(blk)
        S_.finish()
        print("instructions emitted:", S_.n_instr, {k: v for k, v in S_.cnt.items()})
    return nc


def make_consts():
    cst = np.zeros((128, 512), np.float32)
    cst[:, 0:128] = np.eye(128, dtype=np.float32)
    s = np.arange(128)[:, None]
    j = np.arange(128)[None, :]
    cst[:, 128:256] = (s <= j).astype(np.float32)
    cst[:, 256:384] = (s > j).astype(np.float32)
    rm = np.zeros((64, 64), np.float32)
    for i in range(32):
        rm[32 + i, i] = -1.0
        rm[i, 32 + i] = 1.0
    cst[0:64, 384:448] = rm
    cst[64:128, 448:512] = rm
    return cst


def make_gcol(inp):
    g = np.zeros((128, NG), np.float32)

    def colz(v):
        return np.ascontiguousarray(np.asarray(v, np.float32).reshape(8, 128).T)

    for l in range(2):
        g[:, 40 * l + G_MIXPRE:40 * l + G_MIXPRE + 8] = colz(inp["g_mix_pre"][l])
        g[:, 40 * l + G_MIXPOST:40 * l + G_MIXPOST + 8] = colz(inp["g_mix_post"][l])
        g[:, 40 * l + G_FFNPRE:40 * l + G_FFNPRE + 8] = colz(inp["g_ffn_pre"][l])
        g[:, 40 * l + G_FFNPOST:40 * l + G_FFNPOST + 8] = colz(inp["g_ffn_post"][l])
        g[:, 40 * l + G_MEM:40 * l + G_MEM + 8] = colz(inp["g_mem"][l])
    g[:, G_KV:G_KV + 8] = colz(inp["g_kv"])
    g[0:96, G_MOUT:G_MOUT + 8] = np.asarray(inp["g_mlstm_out"][0], np.float32).reshape(8, 96).T
    g[:, G_BG:G_BG + 8] = np.asarray(inp["b_gates_a"][0], np.float32)[None, :]
    sk_ = np.asarray(inp["sinks_b"][0], np.float32)
    for c, (a, b) in enumerate(PAIRS):
        g[0:64, G_SINK + c] = sk_[a]
        g[64:128, G_SINK + c] = sk_[b]
    inv = (1.0 / (np.float32(10000.0) ** (np.arange(0, 64, 2, dtype=np.float32) / np.float32(64)))).astype(np.float32)
    g[:, G_INVF] = np.tile(inv, 4)
    return g


def make_in_maps(inp, ncores=8):
    shared = {
        "w_mem_kv0": inp["w_mem_kv"][0], "w_mem_kv1": inp["w_mem_kv"][1], "w_in_a": inp["w_in_a"][0],
        "w_out0": inp["w_out"][0], "w_out1": inp["w_out"][1], "w_ffn_in0": inp["w_ffn_in"][0], "w_ffn_in1": inp["w_ffn_in"][1],
        "w_ffn_out0": inp["w_ffn_out"][0], "w_ffn_out1": inp["w_ffn_out"][1], "w_kv": inp["w_kv"], "w_in_b": inp["w_in_b"][0],
    }
    shared = {k: np.ascontiguousarray(np.asarray(v, np.float32)) for k, v in shared.items()}
    shared["gcol"] = make_gcol(inp)
    shared["cst"] = make_consts()
    maps = []
    for b in range(ncores):
        m = dict(shared)
        m["x"] = np.ascontiguousarray(np.asarray(inp["x"][b], np.float32))
        m["mem"] = np.ascontiguousarray(np.asarray(inp["mem"][b], np.float32))
        m["posr"] = np.ascontiguousarray(np.broadcast_to(np.asarray(inp["positions"][b], np.int32)[None, :], (128, S)))
        maps.append(m)
    return maps


_NC_CACHE = {}


def kernel(**inputs):
    inp = {k: np.asarray(v) for k, v in inputs.items()}
    if "full" not in _NC_CACHE:
        _NC_CACHE["full"] = build()
    nc = _NC_CACHE["full"]
    maps = make_in_maps(inp, 8)
    res = run_bass_kernel_spmd(nc, maps, core_ids=list(range(8)))
    out = np.stack([np.asarray(r["out"], np.float32) for r in res.results], axis=0)
    return out
```

```python
import math
import os
from contextlib import ExitStack

import numpy as np
import concourse.bass as bass
import concourse.mybir as mybir
from concourse.bass_utils import run_bass_kernel_spmd

F32 = mybir.dt.float32
BF16 = mybir.dt.bfloat16
I32 = mybir.dt.int32
AF = mybir.ActivationFunctionType
ALU = mybir.AluOpType

D = 1024
S = 4096
TB = 512
NBLK = S // TB
DFF = 2816
NFC = DFF // 128
EPS = 1e-6
DQK = 96
DV = 192
PAIRS = [(0, 3), (1, 4), (2, 5), (6, 9), (7, 10), (8, 11)]
TWO_PI = 2.0 * math.pi

WSHAPES = {
    "w_mem_kv0": [D, 512], "w_mem_kv1": [D, 512], "w_in_a": [D, 2568], "w_out0": [D, D],
    "w_ffn_in0": [D, 2 * DFF], "w_ffn_out0": [DFF, D], "w_kv": [D, 512], "w_in_b": [D, D],
    "w_out1": [D, D], "w_ffn_in1": [D, 2 * DFF], "w_ffn_out1": [DFF, D],
}
G_MIXPRE, G_MIXPOST, G_FFNPRE, G_FFNPOST, G_MEM = 0, 8, 16, 24, 32
G_KV = 80
G_MOUT = 88
G_BG = 96
G_SINK = 104
G_INVF = 116
NG = 117

SAME_ENGINE_SYNC = True
NDQ = 8


class Sched:
    def __init__(self, nc, st):
        self.nc = nc
        self.E = {"pe": nc.tensor, "act": nc.scalar, "dve": nc.vector, "pool": nc.gpsimd, "sp": nc.sync}
        self.semh = {}
        for k in self.E:
            self.semh[k] = st.enter_context(nc.semaphore("c_" + k))
        self.cnt = {k: 0 for k in self.E}
        self.seen = {}
        self.lastw = {}
        self.rd = {}
        self.dq = {}
        self.dqi = {}
        for e in ("sp", "pool"):
            self.dq[e] = []
            for i in range(NDQ):
                nm = "d_%s%d" % (e, i)
                self.semh[nm] = st.enter_context(nc.semaphore(nm))
                self.dq[e].append([nm, 0])
            self.dqi[e] = 0
        self.out_tokens = []
        self.n_instr = 0
        self.n_uniq = 0
        self.st = st

    def _need(self, r, w):
        need = {}

        def add(t):
            if t is not None and need.get(t[0], 0) < t[1]:
                need[t[0]] = t[1]

        for k in r:
            add(self.lastw.get(k))
        for k in w:
            add(self.lastw.get(k))
            for s, v in self.rd.get(k, {}).items():
                add((s, v))
        return need

    def _wait(self, e, s, v):
        if s == e and (e == "pe" or not SAME_ENGINE_SYNC):
            return
        if self.seen.get((e, s), 0) >= v:
            return
        self.E[e].wait_ge(self.semh[s], v)
        self.seen[(e, s)] = v

    def _record(self, tok, r, w):
        for k in r:
            d = self.rd.setdefault(k, {})
            if d.get(tok[0], 0) < tok[1]:
                d[tok[0]] = tok[1]
        for k in w:
            self.lastw[k] = tok
            self.rd[k] = {}

    def op(self, e, fn, r=(), w=()):
        for s, v in self._need(r, w).items():
            self._wait(e, s, v)
        ins = fn(self.E[e])
        self.cnt[e] += 1
        ins.then_inc(self.semh[e], 1)
        self._record((e, self.cnt[e]), r, w)
        self.n_instr += 1

    def dma(self, e, out, in_, r=(), w=(), is_out=False):
        for s, v in self._need(r, w).items():
            self._wait(e, s, v)
        slot = self.dq[e][self.dqi[e]]
        self.dqi[e] = (self.dqi[e] + 1) % NDQ
        nm, tot = slot
        if tot > 0:
            self._wait(e, nm, tot)
        self.E[e].dma_start(out=out, in_=in_).then_inc(self.semh[nm], 16)
        slot[1] = tot + 16
        tok = (nm, tot + 16)
        self._record(tok, r, w)
        if is_out:
            self.out_tokens.append(tok)
        self.n_instr += 1

    def dma_unique(self, e, out, in_, r=(), w=()):
        for s, v in self._need(r, w).items():
            self._wait(e, s, v)
        nm = "u_%d" % self.n_uniq
        self.n_uniq += 1
        self.semh[nm] = self.st.enter_context(self.nc.semaphore(nm))
        self.E[e].dma_start(out=out, in_=in_).then_inc(self.semh[nm], 16)
        self._record((nm, 16), r, w)
        self.n_instr += 1

    def finish(self):
        for s, v in self.out_tokens:
            self._wait("sp", s, v)
        self.E["sp"].nop()


def build(nblk=NBLK, stage=4):
    nc = bass.Bass("TRN2", target_bir_lowering=False)

    def din(name, shape, dt=F32):
        return nc.dram_tensor(name, shape, dt, kind="ExternalInput").ap()

    x_d = din("x", [S, D])
    mem_d = din("mem", [256, D])
    pos_d = din("posr", [128, S], I32)
    W = {n: din(n, sh) for n, sh in WSHAPES.items()}
    gcol_d = din("gcol", [128, NG])
    cst_d = din("cst", [128, 512])
    out_d = nc.dram_tensor("out", [S, D], F32, kind="ExternalOutput").ap()
    SCR = {n: nc.dram_tensor("s_" + n, sh, BF16, kind="Internal").ap() for n, sh in WSHAPES.items()}

    with ExitStack() as st:
        def sb(name, shape, dt):
            return st.enter_context(nc.sbuf_tensor("sb_" + name, shape, dt))

        S_ = Sched(nc, st)
        xT = sb("xT", [128, 8, 512], F32)
        big32 = sb("big32", [128, 4096], F32)
        hT = sb("hT", [128, 8, 512], BF16)
        sq = sb("sq", [128, 4, 512], BF16)
        rs_tmp = sb("rs_tmp", [128, 512], F32)
        rstd = sb("rstd", [128, 512], F32)
        rstd_bf = sb("rstd_bf", [128, 512], BF16)
        rstd_pb = sb("rstd_pb", [128, 512], BF16)
        ws = [sb("ws%d" % i, [128, 6144], BF16) for i in range(3)]
        xstage = sb("xstage", [128, 4096], F32)
        A22 = sb("A22", [128, NFC, 512], BF16)
        mixT = sb("mixT", [128, 10, 512], BF16)
        gcol = sb("gcol", [128, NG], F32)
        cst = sb("cst", [128, 512], F32)
        ones_bf = sb("ones_bf", [128, 128], BF16)
        rm2_bf = sb("rm2_bf", [128, 128], BF16)
        tri4 = sb("tri4", [128, 512], BF16)
        ust4 = sb("ust4", [128, 512], BF16)
        ust4f = sb("ust4f", [128, 512], BF16)
        epsc = sb("epsc", [128, 1], F32)
        esink = sb("esink", [128, 6], F32)
        onesAB = sb("onesAB", [128, 2, 128], BF16)
        eb = sb("eb", [128, 2, 4, 128], F32)
        ea = sb("ea", [128, 2, 4, 128], F32)
        gsb = sb("gsb", [128, 2, 8], F32)
        gth = sb("gth", [128, 2, 8], F32)
        gee = sb("gee", [128, 2, 4], F32)
        glp = sb("glp", [128, 2, 4], F32)
        gig = sb("gig", [128, 2, 4], F32)
        ga2 = sb("ga2", [128, 2, 4], F32)
        gek = sb("gek", [128, 2, 4], F32)
        ktok = sb("ktok", [128, 2, 4, 96], BF16)
        kraw = sb("kraw", [128, 2, 384], BF16)
        vaug = sb("vaug", [128, 2, 4, 196], BF16)
        PTm = sb("PTm", [128, 2, 4, 128], BF16)
        CT = sb("CT", [128, 4, 193], F32)
        CTb = sb("CTb", [128, 4, 196], BF16)
        mqT = sb("mqT", [128, 2, 512], BF16)
        rdt = sb("rdt", [128, 2, 512], F32)
        sgt = sb("sgt", [128, 2, 512], BF16)
        cc2 = sb("cc2", [128, 512], F32)
        ss2 = sb("ss2", [128, 512], F32)
        tmpA = sb("tmpA", [128, 512], F32)
        tmpB = sb("tmpB", [128, 512], F32)
        posi = sb("posi", [128, 512], I32)
        kint = posi
        kpad = sb("kpad", [128, 4, 640], BF16)
        vsh = sb("vsh", [128, 5, 4, 128], BF16)
        PT = sb("PT", [128, 8, 512], BF16)
        qraw = sgt
        mkT = sb("mkT", [128, 2, 4, 256], BF16)
        mv = sb("mv", [128, 2, 2, 4, 128], BF16)
        psb = [st.enter_context(nc.psum_tensor("ps%d" % i, [128, 512], F32)) for i in range(8)]

        ident = cst[:, 0:128]
        tri_f = cst[:, 128:256]
        ust_f = cst[:, 256:384]

        state = {"bank": 0, "slot": 0, "sq": 0, "rdt": 0, "sgt": 0, "pt": 0, "qraw": 0}

        def bank():
            b = state["bank"]
            state["bank"] = (b + 1) % 8
            return psb[b], ("ps", b)

        def rot(name, n):
            v = state[name]
            state[name] = (v + 1) % n
            return v

        b32keys = [("b32", k) for k in range(8)]

        def a22w(ci):
            return [("A22", ci)] + ([("qk", ci, t_) for t_ in range(4)] if 8 <= ci < 16 else [])
        xTk = lambda k: ("xT", k)
        hTk = lambda k: ("hT", k)
        hTkeys = [hTk(k) for k in range(8)]

        S_.dma("sp", gcol[:], gcol_d, w=["gcol"])
        S_.dma("sp", cst[:], cst_d, w=["cst"])
        S_.op("dve", lambda e: e.memset(ones_bf[:], 1.0), w=["ones"])
        S_.op("dve", lambda e: e.memset(epsc[:], EPS), w=["epsc"])
        S_.op("dve", lambda e: e.tensor_copy(out=rm2_bf[:], in_=cst[:, 384:512]), r=["cst"], w=["rm2"])
        for i in range(4):
            S_.op("dve", lambda e: e.tensor_copy(out=tri4[:, i * 128:(i + 1) * 128], in_=tri_f), r=["cst"], w=["tri4"])
            S_.op("dve", lambda e: e.tensor_copy(out=ust4[:, i * 128:(i + 1) * 128], in_=ust_f), r=["cst"], w=["ust4"])
            if i > 0:
                S_.op("dve", lambda e: e.tensor_copy(out=ust4f[:, i * 128:(i + 1) * 128], in_=ust_f), r=["cst"], w=["ust4f"])
        S_.op("dve", lambda e: e.memset(ust4f[:, 0:128], 0.0), w=["ust4f"])
        S_.op("act", lambda e: e.activation(out=esink[:], in_=gcol[:, G_SINK:G_SINK + 6], func=AF.Exp), r=["gcol"], w=["esink"])
        S_.op("dve", lambda e: e.memset(onesAB[:], 0.0), w=["onesAB"])
        S_.op("dve", lambda e: e.memset(onesAB[:, 0, 0:64], 1.0), w=["onesAB"])
        S_.op("dve", lambda e: e.memset(onesAB[:, 1, 64:128], 1.0), w=["onesAB"])
        S_.op("pool", lambda e: e.memset(mkT[:], 0.0), w=[("mkT", 0), ("mkT", 1)])
        S_.op("pool", lambda e: e.memset(mv[:], 0.0), w=[("mv", 0), ("mv", 1)])
        S_.op("pool", lambda e: e.memset(kpad[:], 0.0), w=["kpad"])
        S_.op("pool", lambda e: e.memset(vsh[:], 0.0), w=["vsh"])
        S_.op("pool", lambda e: e.memset(CT[:], 0.0), w=["CT"])
        S_.op("pool", lambda e: e.memset(CTb[:], 0.0), w=["CTb"])
        S_.op("pool", lambda e: e.memset(vaug[:], 1.0), w=["vaug0", "vaug1"])

        scr_keys = {}

        def cast_rows(name, rows_per, gate=()):
            src, dst = W[name], SCR[name]
            n = src.shape[0]
            keys = []
            for i, r0 in enumerate(range(0, n, rows_per)):
                r1 = min(n, r0 + rows_per)
                k = ("scr", name, i)
                S_.dma_unique("pool", dst[r0:r1, :], src[r0:r1, :], r=gate, w=[k])
                keys.append(k)
            scr_keys[name] = keys

        def cast_in_b(gate=()):
            src, dst = W["w_in_b"], SCR["w_in_b"]
            keys = []
            i = 0
            for c, (a, b) in enumerate(PAIRS):
                for e_, h in enumerate((a, b)):
                    k = ("scr", "w_in_b", i)
                    i += 1
                    S_.dma_unique("pool", dst[:, c * 128 + e_ * 64:c * 128 + e_ * 64 + 64], src[:, h * 64:(h + 1) * 64], r=gate, w=[k])
                    keys.append(k)
            k = ("scr", "w_in_b", i)
            S_.dma_unique("pool", dst[:, 768:1024], src[:, 768:1024], r=gate, w=[k])
            keys.append(k)
            scr_keys["w_in_b"] = keys

        def cast_out1(gate=()):
            src, dst = W["w_out1"], SCR["w_out1"]
            keys = []
            i = 0
            for c, (a, b) in enumerate(PAIRS):
                for e_, h in enumerate((a, b)):
                    k = ("scr", "w_out1", i)
                    i += 1
                    S_.dma_unique("pool", dst[c * 128 + e_ * 64:c * 128 + e_ * 64 + 64, :], src[h * 64:(h + 1) * 64, :], r=gate, w=[k])
                    keys.append(k)
            k = ("scr", "w_out1", i)
            S_.dma_unique("pool", dst[768:1024, :], src[768:1024, :], r=gate, w=[k])
            keys.append(k)
            scr_keys["w_out1"] = keys

        jobs = []
        jstate = {"next": 0}
        LOOKAHEAD = 3

        def cast_cols(name, jobname, col_ranges, gate):
            keys = []
            for i, (c0, c1) in enumerate(col_ranges):
                k = ("scr", jobname, i)
                S_.dma_unique("pool", SCR[name][:, c0:c1], W[name][:, c0:c1], r=gate, w=[k])
                keys.append(k)
            return keys

        def cast_rowrange(name, jobname, r0, r1, gate):
            k = ("scr", jobname, 0)
            S_.dma_unique("pool", SCR[name][r0:r1, :], W[name][r0:r1, :], r=gate, w=[k])
            return [k]

        def whole(name):
            def f(gate):
                cast_rows(name, 512, gate)
                return scr_keys[name]
            return f

        jobs.append(("w_mem_kv0", whole("w_mem_kv0")))
        jobs.append(("w_mem_kv1", whole("w_mem_kv1")))
        jobs.append(("w_in_a", whole("w_in_a")))
        jobs.append(("w_out0", whole("w_out0")))

        def add_ffn_jobs(l):
            ni, no = "w_ffn_in%d" % l, "w_ffn_out%d" % l
            for u in range(8):
                ncol = 384 if u < 7 else 128
                jobs.append(("%s:u%d" % (ni, u), (lambda gate, u=u, ncol=ncol, ni=ni: cast_cols(
                    ni, "%s:u%d" % (ni, u), [(u * 384, u * 384 + ncol), (DFF + u * 384, DFF + u * 384 + ncol)], gate))))
            for g3 in range(3):
                r0, r1 = g3 * 1024, min(DFF, g3 * 1024 + 1024)
                jobs.append(("%s:g%d" % (no, g3), (lambda gate, g3=g3, r0=r0, r1=r1, no=no: cast_rowrange(no, "%s:g%d" % (no, g3), r0, r1, gate))))

        add_ffn_jobs(0)
        jobs.append(("w_kv", whole("w_kv")))

        def job_in_b(gate):
            cast_in_b(gate)
            return scr_keys["w_in_b"]

        def job_out1(gate):
            cast_out1(gate)
            return scr_keys["w_out1"]

        jobs.append(("w_in_b", job_in_b))
        jobs.append(("w_out1", job_out1))
        add_ffn_jobs(1)
        job_index = {n: i for i, (n, _) in enumerate(jobs)}

        def emit_jobs_upto(idx, gate):
            with nc.allow_non_contiguous_dma(reason="weight re-layout"):
                while jstate["next"] <= min(idx, len(jobs) - 1):
                    n, fn = jobs[jstate["next"]]
                    scr_keys[n] = fn(gate)
                    jstate["next"] += 1

        emit_jobs_upto(2, ())

        def wslot():
            i = state["slot"]
            state["slot"] = (i + 1) % 3
            return ws[i], ("ws", i)

        def wload(parts):
            slot, key = wslot()
            need = max(job_index[name] for _, _, name in parts)
            if jstate["next"] <= need:
                emit_jobs_upto(need, ())
            with nc.allow_non_contiguous_dma(reason="weight stream"):
                for fn, src, name in parts:
                    S_.dma("sp", fn(slot), src, r=scr_keys[name], w=[key])
            if jstate["next"] < len(jobs):
                emit_jobs_upto(max(need + LOOKAHEAD, jstate["next"]), [key])
            return slot, key

        def kview(name):
            return SCR[name].rearrange("(k p) n -> p k n", p=128)

        def stats_rstd(srcs, P, n, nfeat):
            sqs = []
            for ap, keys in srcs:
                i = rot("sq", 4)
                S_.op("act", lambda e: e.activation(out=sq[0:P, i, 0:n], in_=ap, func=AF.Square), r=keys, w=[("sq", i)])
                sqs.append(i)
            ps, pk = bank()
            for j, i in enumerate(sqs):
                S_.op("pe", lambda e: e.matmul(ps[0:P, 0:n], ones_bf[0:P, 0:P], sq[0:P, i, 0:n], start=(j == 0), stop=(j == len(sqs) - 1)),
                      r=[("sq", i), "ones"], w=[pk])
            S_.op("act", lambda e: e.activation(out=rs_tmp[0:P, 0:n], in_=ps[0:P, 0:n], func=AF.Ln, bias=epsc[0:P, :], scale=1.0 / nfeat),
                  r=[pk, "epsc"], w=["rs_tmp"])
            S_.op("act", lambda e: e.activation(out=rstd[0:P, 0:n], in_=rs_tmp[0:P, 0:n], func=AF.Exp, scale=-0.5), r=["rs_tmp"], w=["rstd"])

        def norm_to_hT(gbase, n=512, reuse_stats=False):
            if not reuse_stats:
                ps, pk = bank()
            for k in range(8):
                if not reuse_stats:
                    i = rot("sq", 4)
                    S_.op("act", lambda e: e.activation(out=sq[:, i, 0:n], in_=xT[:, k, 0:n], func=AF.Square), r=[xTk(k)], w=[("sq", i)])
                    S_.op("pe", lambda e: e.matmul(ps[:, 0:n], ones_bf[:], sq[:, i, 0:n], start=(k == 0), stop=(k == 7)),
                          r=[("sq", i), "ones"], w=[pk])
                if k % 2 == 0:
                    S_.op("act", lambda e: e.activation(out=hT[:, k, 0:n], in_=xT[:, k, 0:n], func=AF.Copy, scale=gcol[:, gbase + k:gbase + k + 1]),
                          r=[xTk(k), "gcol"], w=[hTk(k)])
                else:
                    S_.op("dve", lambda e: e.tensor_scalar_mul(out=hT[:, k, 0:n], in0=xT[:, k, 0:n], scalar1=gcol[:, gbase + k:gbase + k + 1]),
                          r=[xTk(k), "gcol"], w=[hTk(k)])
            if not reuse_stats:
                S_.op("act", lambda e: e.activation(out=rs_tmp[:, 0:n], in_=ps[:, 0:n], func=AF.Ln, bias=epsc[:], scale=1.0 / D),
                      r=[pk, "epsc"], w=["rs_tmp"])
                S_.op("act", lambda e: e.activation(out=rstd_bf[:, 0:n], in_=rs_tmp[:, 0:n], func=AF.Exp, scale=-0.5), r=["rs_tmp"], w=["rstd_bf"])
            for k in range(8):
                S_.op("dve", lambda e: e.tensor_tensor(out=hT[:, k, 0:n], in0=hT[:, k, 0:n], in1=rstd_bf[:, 0:n], op=ALU.mult),
                      r=[hTk(k), "rstd_bf"], w=[hTk(k)])

        def b32(k):
            return big32[:, k * 512:(k + 1) * 512]

        y16_all = big32[:].bitcast(BF16)

        def y16(k):
            return y16_all[:, k * 512:(k + 1) * 512]

        def y16k(k):
            return ("b32", k // 2)

        def post_norm_residual(gbase):
            ps, pk = bank()
            for k in range(8):
                i = rot("sq", 4)
                S_.op("act", lambda e: e.activation(out=sq[:, i, :], in_=y16(k), func=AF.Square), r=[y16k(k)], w=[("sq", i)])
                S_.op("pe", lambda e: e.matmul(ps[:, :], ones_bf[:], sq[:, i, :], start=(k == 0), stop=(k == 7)),
                      r=[("sq", i), "ones"], w=[pk])
            S_.op("act", lambda e: e.activation(out=rs_tmp[:], in_=ps[:], func=AF.Ln, bias=epsc[:], scale=1.0 / D),
                  r=[pk, "epsc"], w=["rs_tmp"])
            S_.op("act", lambda e: e.activation(out=rstd_pb[:], in_=rs_tmp[:], func=AF.Exp, scale=-0.5), r=["rs_tmp"], w=["rstd_pb"])
            for k in range(8):
                S_.op("dve", lambda e: e.scalar_tensor_tensor(out=y16(k), in0=y16(k), scalar=gcol[:, gbase + k:gbase + k + 1],
                                                             in1=rstd_pb[:], op0=ALU.mult, op1=ALU.mult),
                      r=[y16k(k), "rstd_pb", "gcol"], w=[y16k(k)])
                S_.op("pool" if k in (1, 4, 6) else "dve", lambda e: e.tensor_tensor(out=xT[:, k, :], in0=xT[:, k, :], in1=y16(k), op=ALU.add),
                      r=[y16k(k), xTk(k)], w=[xTk(k)])

        def load_transpose(src_rows, ntile):
            S_.dma("sp", big32[:, 0:ntile * 1024].rearrange("p (t d) -> p t d", t=ntile),
                   src_rows.rearrange("(t p) d -> p t d", p=128), w=b32keys[0:2 * ntile])
            for k in range(8):
                ps, pk = bank()
                for t in range(ntile):
                    S_.op("pe", lambda e: e.transpose(ps[:, t * 128:(t + 1) * 128], big32[:, t * 1024 + k * 128:t * 1024 + (k + 1) * 128], ident),
                          r=[("b32", 2 * t), ("b32", 2 * t + 1), "cst"], w=[pk])
                S_.op("act", lambda e: e.activation(out=xT[:, k, 0:ntile * 128], in_=ps[:, 0:ntile * 128], func=AF.Copy), r=[pk], w=[xTk(k)])

        xskeys = [("xs", t) for t in range(4)]

        def prefetch_x(blk):
            S_.dma("sp", xstage[:].rearrange("p (t d) -> p t d", t=4),
                   x_d[blk * TB:(blk + 1) * TB, :].rearrange("(t p) d -> p t d", p=128), w=xskeys)

        def transpose_x():
            for k in range(8):
                ps, pk = bank()
                for t in range(4):
                    S_.op("pe", lambda e: e.transpose(ps[:, t * 128:(t + 1) * 128], xstage[:, t * 1024 + k * 128:t * 1024 + (k + 1) * 128], ident),
                          r=[("xs", t), "cst"], w=[pk])
                S_.op("act", lambda e: e.activation(out=xT[:, k, :], in_=ps[:, :], func=AF.Copy), r=[pk], w=[xTk(k)])

        def store_out(blk):
            for half in range(2):
                for t in range(4):
                    ps, pk = bank()
                    for kk in range(4):
                        k = half * 4 + kk
                        S_.op("pe", lambda e: e.transpose(ps[:, kk * 128:(kk + 1) * 128], xT[:, k, t * 128:(t + 1) * 128], ident),
                              r=[xTk(k), "cst"], w=[pk])
                    S_.op("act", lambda e: e.activation(out=big32[:, t * 1024 + half * 512:t * 1024 + half * 512 + 512], in_=ps[:], func=AF.Copy),
                          r=[pk], w=[("b32", 2 * t + half)])
            S_.dma("sp", out_d[blk * TB:(blk + 1) * TB, :].rearrange("(t p) d -> p t d", p=128),
                   big32[:].rearrange("p (t d) -> p t d", t=4), r=b32keys, is_out=True)

        def mem_attention(l, mix_base):
            def scores(p):
                res = []
                for e_ in range(2):
                    h = 2 * p + e_
                    idx = []
                    for mc in range(2):
                        ps, pk = bank()
                        S_.op("pe", lambda e: e.matmul(ps[:, :], mkT[:, l, h, mc * 128:(mc + 1) * 128], mqT[:, p, :], start=True, stop=True),
                              r=[("mkT", l), ("mqT", p)], w=[pk])
                        i = rot("pt", 8)
                        S_.op("act", lambda e: e.activation(out=PT[:, i, :], in_=ps[:, :], func=AF.Exp), r=[pk], w=[("PT", i)])
                        idx.append(i)
                    res.append((h, idx))
                return res

            def outp(p, res):
                pso, pko = bank()
                psd, pkd = bank()
                n = 0
                for e_, (h, idx) in enumerate(res):
                    for mc in range(2):
                        S_.op("pe", lambda e: e.matmul(pso[:, :], mv[:, l, mc, h, :], PT[:, idx[mc], :], start=(n == 0), stop=(n == 3)),
                              r=[("mv", l), ("PT", idx[mc])], w=[pko])
                        n += 1
                n = 0
                for e_, (h, idx) in enumerate(res):
                    for mc in range(2):
                        S_.op("pe", lambda e: e.matmul(psd[:, :], onesAB[:, e_, :], PT[:, idx[mc], :], start=(n == 0), stop=(n == 3)),
                              r=["onesAB", ("PT", idx[mc])], w=[pkd])
                        n += 1
                ri = rot("rdt", 2)
                S_.op("act", lambda e: e.activation(out=rdt[:, ri, :], in_=psd[:, :], func=AF.Ln), r=[pkd], w=[("rdt", ri)])
                S_.op("act", lambda e: e.activation(out=rdt[:, ri, :], in_=rdt[:, ri, :], func=AF.Exp, scale=-1.0), r=[("rdt", ri)], w=[("rdt", ri)])
                S_.op("dve", lambda e: e.tensor_tensor(out=mixT[:, mix_base + p, :], in0=pso[:, :], in1=rdt[:, ri, :], op=ALU.mult),
                      r=[pko, ("rdt", ri)], w=[("mixT", mix_base + p)])

            r0 = scores(0)
            r1 = scores(1)
            outp(0, r0)
            outp(1, r1)

        def out_proj(l, chunks):
            name = "w_out%d" % l
            nch = len(chunks)
            for half in range(2):
                cols = slice(half * 512, (half + 1) * 512)
                if l == 0:
                    parts = [
                        (lambda s: s[0:96, 0:8 * 512].rearrange("p (c n) -> p c n", c=8),
                         SCR[name][0:768, cols].rearrange("(c p) n -> p c n", p=96), name),
                        (lambda s: s[:, 8 * 512:10 * 512].rearrange("p (c n) -> p c n", c=2),
                         SCR[name][768:1024, cols].rearrange("(c p) n -> p c n", p=128), name),
                    ]
                else:
                    parts = [
                        (lambda s: s[:, 0:8 * 512].rearrange("p (c n) -> p c n", c=8),
                         SCR[name][:, cols].rearrange("(c p) n -> p c n", p=128), name),
                    ]
                slot, sk = wload(parts)
                for dcl in range(4):
                    dc = half * 4 + dcl
                    ps, pk = bank()
                    for ci, P in enumerate(chunks):
                        S_.op("pe", lambda e: e.matmul(ps[:, :], slot[0:P, ci * 512 + dcl * 128:ci * 512 + (dcl + 1) * 128], mixT[0:P, ci, :],
                                                      start=(ci == 0), stop=(ci == nch - 1)),
                              r=[sk, ("mixT", ci)], w=[pk])
                    S_.op("dve", lambda e: e.tensor_copy(out=y16(dc), in_=ps[:, :]), r=[pk], w=[y16k(dc)])

        def ffn(l, hook=None):
            norm_to_hT(40 * l + G_FFNPRE)
            if hook is not None:
                hook()
            name = "w_ffn_in%d" % l
            kv = kview(name)
            for u in range(8):
                ncol = 384 if u < 7 else 128
                jn = "%s:u%d" % (name, u)
                parts = [
                    (lambda s: s[:, 0:8 * ncol].rearrange("p (k n) -> p k n", k=8), kv[:, :, u * 384:u * 384 + ncol], jn),
                    (lambda s: s[:, 8 * ncol:16 * ncol].rearrange("p (k n) -> p k n", k=8), kv[:, :, DFF + u * 384:DFF + u * 384 + ncol], jn),
                ]
                slot, sk = wload(parts)
                for j in range(ncol // 128):
                    f = u * 3 + j
                    psg, pkg = bank()
                    psu, pku = bank()
                    for kc in range(8):
                        S_.op("pe", lambda e: e.matmul(psg[:, :], slot[:, kc * ncol + j * 128:kc * ncol + (j + 1) * 128], hT[:, kc, :],
                                                      start=(kc == 0), stop=(kc == 7)), r=[sk, hTk(kc)], w=[pkg])
                    for kc in range(8):
                        S_.op("pe", lambda e: e.matmul(psu[:, :], slot[:, 8 * ncol + kc * ncol + j * 128:8 * ncol + kc * ncol + (j + 1) * 128], hT[:, kc, :],
                                                      start=(kc == 0), stop=(kc == 7)), r=[sk, hTk(kc)], w=[pku])
                    si = rot("sgt", 2)
                    S_.op("act", lambda e: e.activation(out=sgt[:, si, :], in_=psg[:, :], func=AF.Silu), r=[pkg], w=[("sgt", si)])
                    S_.op("dve", lambda e: e.tensor_tensor(out=A22[:, f, :], in0=psu[:, :], in1=sgt[:, si, :], op=ALU.mult),
                          r=[pku, ("sgt", si)], w=a22w(f))
            name = "w_ffn_out%d" % l
            kv = kview(name)
            for half in range(2):
                banks = [bank() for _ in range(4)]
                for g3 in range(3):
                    kc0, kc1 = g3 * 8, min(NFC, g3 * 8 + 8)
                    nk = kc1 - kc0
                    parts = [(lambda s: s[:, 0:nk * 512].rearrange("p (k n) -> p k n", k=nk), kv[:, kc0:kc1, half * 512:(half + 1) * 512], "%s:g%d" % (name, g3))]
                    slot, sk = wload(parts)
                    for dcl in range(4):
                        ps, pk = banks[dcl]
                        for kc in range(kc0, kc1):
                            S_.op("pe", lambda e: e.matmul(ps[:, :], slot[:, (kc - kc0) * 512 + dcl * 128:(kc - kc0) * 512 + (dcl + 1) * 128], A22[:, kc, :],
                                                          start=(kc == 0), stop=(kc == NFC - 1)), r=[sk, ("A22", kc)], w=[pk])
                for dcl in range(4):
                    ps, pk = banks[dcl]
                    dc = half * 4 + dcl
                    S_.op("dve", lambda e: e.tensor_copy(out=y16(dc), in_=ps[:, :]), r=[pk], w=[y16k(dc)])
            post_norm_residual(40 * l + G_FFNPOST)

        def rope_tables(blk):
            S_.dma("sp", posi[:], pos_d[:, blk * TB:(blk + 1) * TB], w=["posi"])
            S_.op("dve", lambda e: e.tensor_copy(out=tmpA[:], in_=posi[:]), r=["posi"], w=["tmpA"])
            S_.op("dve", lambda e: e.tensor_scalar_mul(out=tmpA[:], in0=tmpA[:], scalar1=gcol[:, G_INVF:G_INVF + 1]),
                  r=["tmpA", "gcol"], w=["tmpA"])
            for dst, shift, key in ((ss2, 0.0, "ss2"), (cc2, 0.5 * math.pi, "cc2")):
                S_.op("dve", lambda e: e.tensor_scalar(out=tmpB[:], in0=tmpA[:], scalar1=shift, scalar2=1.0 / TWO_PI, op0=ALU.add, op1=ALU.mult),
                      r=["tmpA"], w=["tmpB"])
                S_.op("dve", lambda e: e.tensor_copy(out=kint[:], in_=tmpB[:]), r=["tmpB"], w=["posi"])
                S_.op("dve", lambda e: e.tensor_copy(out=tmpB[:], in_=kint[:]), r=["posi"], w=["tmpB"])
                S_.op("dve", lambda e: e.scalar_tensor_tensor(out=tmpB[:], in0=tmpB[:], scalar=-TWO_PI, in1=tmpA[:], op0=ALU.mult, op1=ALU.add),
                      r=["tmpB", "tmpA"], w=["tmpB"])
                S_.op("dve", lambda e: e.tensor_scalar(out=tmpB[:], in0=tmpB[:], scalar1=shift, scalar2=-math.pi, op0=ALU.add, op1=ALU.max),
                      r=["tmpB"], w=["tmpB"])
                S_.op("dve", lambda e: e.tensor_scalar(out=tmpB[:], in0=tmpB[:], scalar1=math.pi, scalar2=0.0, op0=ALU.min, op1=ALU.add),
                      r=["tmpB"], w=["tmpB"])
                S_.op("act", lambda e: e.activation(out=dst[:], in_=tmpB[:], func=AF.Sin), r=["tmpB"], w=[key])

        def rope_evac(ps, pk, outs):
            qi = rot("sgt", 2)
            dbgm = os.environ.get("DBG_ROPE", "")
            S_.op("act", lambda e: e.activation(out=qraw[:, qi, :], in_=ps[:, :], func=AF.Copy), r=[pk], w=[("sgt", qi)])
            if dbgm == "1":
                return
            pr, pkr = bank()
            if dbgm != "4":
                S_.op("pe", lambda e: e.matmul(pr[:, :], rm2_bf[:], qraw[:, qi, :], start=True, stop=True), r=["rm2", ("sgt", qi)], w=[pkr])
            if dbgm == "3":
                return
            S_.op("dve", lambda e: e.tensor_tensor(out=tmpA[:], in0=ps[:, :], in1=cc2[:], op=ALU.mult), r=[pk, "cc2", ("sgt", qi)], w=["tmpA"])
            if dbgm == "4":
                return
            S_.op("dve", lambda e: e.tensor_tensor(out=tmpB[:], in0=pr[:, :], in1=ss2[:], op=ALU.mult), r=[pkr, "ss2"], w=["tmpB"])
            if dbgm == "2":
                return
            for (p0, p1), oap, wk in outs:
                S_.op("pool", lambda e: e.tensor_tensor(out=oap, in0=tmpA[p0:p1, :], in1=tmpB[p0:p1, :], op=ALU.add), r=["tmpA", "tmpB"], w=wk)

        prefetch_x(0)
        load_transpose(mem_d, 2)
        for l in range(2):
            norm_to_hT(40 * l + G_MEM, n=256)
            name = "w_mem_kv%d" % l
            slot, sk = wload([(lambda s: s[:, 0:8 * 512].rearrange("p (k n) -> p k n", k=8), kview(name), name)])
            for p in range(2):
                ps, pk = bank()
                for kc in range(8):
                    S_.op("pe", lambda e: e.matmul(ps[:, 0:256], slot[:, kc * 512 + p * 128:kc * 512 + (p + 1) * 128], hT[:, kc, 0:256],
                                                  start=(kc == 0), stop=(kc == 7)), r=[sk, hTk(kc)], w=[pk])
                S_.op("act", lambda e: e.activation(out=mkT[0:64, l, 2 * p, :], in_=ps[0:64, 0:256], func=AF.Copy, scale=0.125), r=[pk], w=[("mkT", l)])
                S_.op("act", lambda e: e.activation(out=mkT[64:128, l, 2 * p + 1, :], in_=ps[64:128, 0:256], func=AF.Copy, scale=0.125), r=[pk], w=[("mkT", l)])
            for mc in range(2):
                ps, pk = bank()
                for kc in range(8):
                    S_.op("pe", lambda e: e.matmul(ps[:, 0:256], hT[:, kc, mc * 128:(mc + 1) * 128], slot[:, kc * 512 + 256:kc * 512 + 512],
                                                  start=(kc == 0), stop=(kc == 7)), r=[sk, hTk(kc)], w=[pk])
                for h in range(4):
                    S_.op("act", lambda e: e.activation(out=mv[:, l, mc, h, (h % 2) * 64:(h % 2) * 64 + 64], in_=ps[:, h * 64:(h + 1) * 64], func=AF.Copy),
                          r=[pk], w=[("mv", l)])

        kva = kview("w_in_a")
        for blk in range(nblk):
            transpose_x()
            if blk + 1 < nblk:
                prefetch_x(blk + 1)

            norm_to_hT(G_MIXPRE)
            zk = []
            for ci_ in range(8, 16):
                zk += a22w(ci_)
            S_.op("pool", lambda e: e.memset(A22[96:128, 8:16, :], 0.0), w=zk)
            slot, sk = wload([(lambda s: s[:, 0:8 * 768].rearrange("p (k n) -> p k n", k=8), kva[:, :, 0:768], "w_in_a")])
            for qk in range(2):
                for h in range(4):
                    ps, pk = bank()
                    c0 = qk * 384 + h * 96
                    for kc in range(8):
                        S_.op("pe", lambda e: e.matmul(ps[0:96, :], slot[:, kc * 768 + c0:kc * 768 + c0 + 96], hT[:, kc, :], start=(kc == 0), stop=(kc == 7)),
                              r=[sk, hTk(kc)], w=[pk])
                    ci = 8 + qk * 4 + h
                    S_.op("act", lambda e: e.activation(out=A22[0:96, ci, :], in_=ps[0:96, :], func=AF.Copy), r=[pk], w=a22w(ci))
            slot, sk = wload([(lambda s: s[:, 0:8 * 768].rearrange("p (k n) -> p k n", k=8), kva[:, :, 1536:2304], "w_in_a")])
            for c in range(8):
                ps, pk = bank()
                for kc in range(8):
                    S_.op("pe", lambda e: e.matmul(ps[0:96, :], slot[:, kc * 768 + c * 96:kc * 768 + (c + 1) * 96], hT[:, kc, :], start=(kc == 0), stop=(kc == 7)),
                          r=[sk, hTk(kc)], w=[pk])
                S_.op("act", lambda e: e.activation(out=A22[0:96, c, :], in_=ps[0:96, :], func=AF.Sigmoid), r=[pk], w=[("A22", c)])
            slotA, skA = wload([(lambda s: s[:, 0:8 * 768].rearrange("p (k n) -> p k n", k=8), kva[:, :, 384:1152], "w_in_a")])
            slotB, skB = wload([
                (lambda s: s[:, 0:8 * 384].rearrange("p (k n) -> p k n", k=8), kva[:, :, 1152:1536], "w_in_a"),
                (lambda s: s[:, 8 * 384:8 * 384 + 64].rearrange("p (k n) -> p k n", k=8), kva[:, :, 2304:2312], "w_in_a"),
                (lambda s: s[:, 4096:4096 + 8 * 256].rearrange("p (k n) -> p k n", k=8), kva[:, :, 2312:2568], "w_in_a"),
            ])
            for p in range(2):
                ps, pk = bank()
                for kc in range(8):
                    S_.op("pe", lambda e: e.matmul(ps[:, :], slotB[:, 4096 + kc * 256 + p * 128:4096 + kc * 256 + (p + 1) * 128], hT[:, kc, :],
                                                  start=(kc == 0), stop=(kc == 7)), r=[skB, hTk(kc)], w=[pk])
                S_.op("act", lambda e: e.activation(out=mqT[:, p, :], in_=ps[:, :], func=AF.Copy), r=[pk], w=[("mqT", p)])

            qTv = lambda tsl: A22[0:96, 8:12, tsl]
            kTv = lambda tsl: A22[0:96, 12:16, tsl]
            qkeys_t = lambda t_: [("qk", 8 + h, t_) for h in range(4)]
            kkeys_t = lambda t_: [("qk", 12 + h, t_) for h in range(4)]
            def tnames(t):
                pb = t % 2
                return pb, slice(t * 128, (t + 1) * 128), "vaug%d" % pb, "g%d" % pb, "eb%d" % pb, "ea%d" % pb, "ktok%d" % pb, "PTm%d" % pb, "kraw%d" % pb

            def tileA1(t):
                pb, tsl, vk, gk, ebk, eak, ktk, ptk, krk = tnames(t)
                psk, pkk = bank()
                psv0, pkv0 = bank()
                psv1, pkv1 = bank()
                psg, pkg = bank()
                for kc in range(8):
                    S_.op("pe", lambda e: e.matmul(psg[:, 0:8], hT[:, kc, tsl], slotB[:, 8 * 384 + kc * 8:8 * 384 + (kc + 1) * 8], start=(kc == 0), stop=(kc == 7)),
                          r=[skB, hTk(kc)], w=[pkg])
                for kc in range(8):
                    S_.op("pe", lambda e: e.matmul(psk[:, 0:384], hT[:, kc, tsl], slotA[:, kc * 768:kc * 768 + 384], start=(kc == 0), stop=(kc == 7)),
                          r=[skA, hTk(kc)], w=[pkk])
                for kc in range(8):
                    S_.op("pe", lambda e: e.matmul(psv0[:, 0:384], hT[:, kc, tsl], slotA[:, kc * 768 + 384:kc * 768 + 768], start=(kc == 0), stop=(kc == 7)),
                          r=[skA, hTk(kc)], w=[pkv0])
                for kc in range(8):
                    S_.op("pe", lambda e: e.matmul(psv1[:, 0:384], hT[:, kc, tsl], slotB[:, kc * 384:(kc + 1) * 384], start=(kc == 0), stop=(kc == 7)),
                          r=[skB, hTk(kc)], w=[pkv1])
                S_.op("dve", lambda e: e.tensor_tensor(out=gsb[:, pb, :], in0=psg[:, 0:8], in1=gcol[:, G_BG:G_BG + 8], op=ALU.add), r=[pkg, "gcol"], w=[gk + "sb"])
                S_.op("act", lambda e: e.activation(out=gth[:, pb, :], in_=gsb[:, pb, :], func=AF.Exp, scale=-2.0 / 15.0), r=[gk + "sb"], w=[gk + "th"])
                S_.op("dve", lambda e: e.tensor_scalar(out=gsb[:, pb, :], in0=gth[:, pb, :], scalar1=1.0, scalar2=0.0, op0=ALU.add, op1=ALU.add), r=[gk + "th"], w=[gk + "sb"])
                S_.op("dve", lambda e: e.reciprocal(out=gsb[:, pb, :], in_=gsb[:, pb, :]), r=[gk + "sb"], w=[gk + "sb"])
                S_.op("dve", lambda e: e.tensor_scalar(out=gth[:, pb, :], in0=gth[:, pb, :], scalar1=-1.0, scalar2=1.0, op0=ALU.mult, op1=ALU.add), r=[gk + "th"], w=[gk + "th"])
                S_.op("dve", lambda e: e.tensor_tensor(out=gth[:, pb, :], in0=gth[:, pb, :], in1=gsb[:, pb, :], op=ALU.mult), r=[gk + "th", gk + "sb"], w=[gk + "th"])
                S_.op("act", lambda e: e.activation(out=gee[:, pb, :], in_=gth[:, pb, 4:8], func=AF.Exp, scale=-15.0), r=[gk + "th"], w=[gk + "ee"])
                S_.op("act", lambda e: e.activation(out=glp[:, pb, :], in_=gee[:, pb, :], func=AF.Ln, bias=1.0), r=[gk + "ee"], w=[gk + "lp"])
                S_.op("dve", lambda e: e.tensor_scalar(out=gig[:, pb, :], in0=gth[:, pb, 0:4], scalar1=15.0, scalar2=0.0, op0=ALU.mult, op1=ALU.add),
                      r=[gk + "th"], w=[gk + "ig"])
                S_.op("act", lambda e: e.activation(out=kraw[:, pb, :], in_=psk[:, 0:384], func=AF.Copy), r=[pkk], w=[krk])
                S_.op("act", lambda e: e.activation(out=vaug[:, pb, 0:2, 0:192], in_=psv0[:, 0:384].rearrange("p (h c) -> p h c", h=2), func=AF.Copy),
                      r=[pkv0], w=[vk])
                S_.op("act", lambda e: e.activation(out=vaug[:, pb, 2:4, 0:192], in_=psv1[:, 0:384].rearrange("p (h c) -> p h c", h=2), func=AF.Copy),
                      r=[pkv1], w=[vk])

            def tileA2a(t):
                pb, tsl, vk, gk, ebk, eak, ktk, ptk, krk = tnames(t)
                pcs, pkcs = bank()
                pa, pka = bank()
                pu, pku = bank()
                for h in range(4):
                    S_.op("pe", lambda e: e.matmul(pcs[0:96, h * 128:(h + 1) * 128], glp[:, pb, h:h + 1].broadcast_to([128, 96]), tri_f, start=True, stop=True),
                          r=[gk + "lp", "cst"], w=[pkcs])
                for h in range(4):
                    S_.op("pe", lambda e: e.matmul(pa[0:96, h * 128:(h + 1) * 128], glp[:, pb, h:h + 1].broadcast_to([128, 96]), tri_f, start=True, stop=False),
                          r=[gk + "lp", "cst"], w=[pka])
                    S_.op("pe", lambda e: e.matmul(pa[0:96, h * 128:(h + 1) * 128], gig[:, pb, h:h + 1].broadcast_to([128, 96]), ident, start=False, stop=True),
                          r=[gk + "ig", "cst"], w=[pka])
                S_.op("pe", lambda e: e.matmul(pu[:, 0:4], ust_f, glp[:, pb, :], start=True, stop=True), r=[gk + "lp", "cst"], w=[pku])
                S_.op("act", lambda e: e.activation(out=eb[0:96, pb, :, :], in_=pcs[0:96, :].rearrange("p (h j) -> p h j", h=4), func=AF.Exp, scale=-1.0),
                      r=[pkcs], w=[ebk])
                S_.op("act", lambda e: e.activation(out=ea[0:96, pb, :, :], in_=pa[0:96, :].rearrange("p (h j) -> p h j", h=4), func=AF.Exp),
                      r=[pka], w=[eak])
                S_.op("dve", lambda e: e.tensor_tensor(out=ga2[:, pb, :], in0=gig[:, pb, :], in1=pu[:, 0:4], op=ALU.subtract), r=[gk + "ig", pku], w=[gk + "a2"])
                S_.op("act", lambda e: e.activation(out=gek[:, pb, :], in_=ga2[:, pb, :], func=AF.Exp), r=[gk + "a2"], w=[gk + "ek"])
                S_.op("dve", lambda e: e.tensor_tensor(out=qTv(tsl), in0=qTv(tsl), in1=eb[0:96, pb, :, :], op=ALU.mult), r=qkeys_t(t) + [ebk], w=qkeys_t(t))
                S_.op("dve", lambda e: e.scalar_tensor_tensor(out=kTv(tsl), in0=kTv(tsl), scalar=DQK ** -0.5, in1=ea[0:96, pb, :, :], op0=ALU.mult, op1=ALU.mult),
                      r=kkeys_t(t) + [eak], w=kkeys_t(t))
                S_.op("dve", lambda e: e.scalar_tensor_tensor(out=ktok[:, pb, :, :], in0=kraw[:, pb, :].rearrange("p (h c) -> p h c", h=4), scalar=DQK ** -0.5,
                                                             in1=gek[:, pb, :].unsqueeze(2).broadcast_to([128, 4, 96]), op0=ALU.mult, op1=ALU.mult),
                      r=[krk, gk + "ek"], w=[ktk])

            def tileA2b(t):
                pb, tsl, vk, gk, ebk, eak, ktk, ptk, krk = tnames(t)
                pss, pks = bank()
                for h in range(4):
                    S_.op("pe", lambda e: e.matmul(pss[:, h * 128:(h + 1) * 128], A22[:, 12 + h, tsl], A22[:, 8 + h, tsl], start=True, stop=True),
                          r=[("qk", 12 + h, t), ("qk", 8 + h, t)], w=[pks])
                S_.op("dve", lambda e: e.tensor_tensor(out=PTm[:, pb, :, :], in0=pss[:, :].rearrange("p (h j) -> p h j", h=4),
                                                      in1=tri_f.unsqueeze(1).broadcast_to([128, 4, 128]), op=ALU.mult), r=[pks, "cst"], w=[ptk])

            def tileBn(t):
                pb, tsl, vk, gk, ebk, eak, ktk, ptk, krk = tnames(t)
                pn = [bank(), bank()]
                pd, pkd = bank()
                for c in range(2):
                    for h in range(4):
                        S_.op("pe", lambda e: e.matmul(pn[c][0][0:96, h * 128:(h + 1) * 128], vaug[:, pb, h, c * 96:(c + 1) * 96], PTm[:, pb, h, :], start=True, stop=False),
                              r=[vk, ptk], w=[pn[c][1]])
                        S_.op("pe", lambda e: e.matmul(pn[c][0][0:96, h * 128:(h + 1) * 128], CTb[:, h, c * 96:(c + 1) * 96], A22[:, 8 + h, tsl], start=False, stop=True),
                              r=["CTb", ("qk", 8 + h, t)], w=[pn[c][1]])
                for h in range(4):
                    S_.op("pe", lambda e: e.matmul(pd[0:96, h * 128:(h + 1) * 128], ones_bf[:, 0:96], PTm[:, pb, h, :], start=True, stop=False),
                          r=["ones", ptk], w=[pkd])
                    S_.op("pe", lambda e: e.matmul(pd[0:96, h * 128:(h + 1) * 128], CTb[:, h, 192:193].broadcast_to([128, 96]), A22[:, 8 + h, tsl], start=False, stop=True),
                          r=["CTb", ("qk", 8 + h, t)], w=[pkd])
                ri = rot("rdt", 2)
                S_.op("act", lambda e: e.activation(out=rdt[0:96, ri, :], in_=pd[0:96, :], func=AF.Abs), r=[pkd], w=[("rdt", ri)])
                S_.op("dve", lambda e: e.tensor_scalar(out=rdt[0:96, ri, :], in0=rdt[0:96, ri, :], scalar1=1.0, scalar2=0.0, op0=ALU.max, op1=ALU.add),
                      r=[("rdt", ri)], w=[("rdt", ri)])
                S_.op("act", lambda e: e.activation(out=rdt[0:96, ri, :], in_=rdt[0:96, ri, :], func=AF.Ln), r=[("rdt", ri)], w=[("rdt", ri)])
                S_.op("act", lambda e: e.activation(out=rdt[0:96, ri, :], in_=rdt[0:96, ri, :], func=AF.Exp, scale=-1.0), r=[("rdt", ri)], w=[("rdt", ri)])
                for c in range(2):
                    hk = [("b32", c * 4 + h) for h in range(4)]
                    S_.op("dve", lambda e: e.tensor_tensor(out=big32[0:96, c * 2048:(c + 1) * 2048].rearrange("p (h n) -> p h n", h=4)[:, :, tsl],
                                                          in0=pn[c][0][0:96, :].rearrange("p (h j) -> p h j", h=4),
                                                          in1=rdt[0:96, ri, :].rearrange("p (h j) -> p h j", h=4), op=ALU.mult),
                          r=[pn[c][1], ("rdt", ri)], w=hk)

            def tileBu(t):
                pb, tsl, vk, gk, ebk, eak, ktk, ptk, krk = tnames(t)
                pU = [bank(), bank()]
                for h in range(4):
                    S_.op("pe", lambda e: e.matmul(pU[h // 2][0][0:96, (h % 2) * 193:(h % 2) * 193 + 193], ktok[:, pb, h, :], vaug[:, pb, h, 0:193], start=True, stop=True),
                          r=[ktk, vk], w=[pU[h // 2][1]])
                for h in range(4):
                    S_.op("dve", lambda e: e.scalar_tensor_tensor(out=CT[0:96, h, :], in0=CT[0:96, h, :], scalar=eb[0:96, pb, h, 127:128],
                                                                 in1=pU[h // 2][0][0:96, (h % 2) * 193:(h % 2) * 193 + 193], op0=ALU.mult, op1=ALU.add),
                          r=["CT", ebk, pU[h // 2][1]], w=["CT"])
                S_.op("pool", lambda e: e.tensor_copy(out=CTb[0:96, :, 0:193], in_=CT[0:96, :, :]), r=["CT"], w=["CTb"])

            tileA1(0)
            tileA2a(0)
            tileA2b(0)
            for t in range(4):
                if t + 1 < 4:
                    tileA1(t + 1)
                tileBn(t)
                if t + 1 < 4:
                    tileA2a(t + 1)
                tileBu(t)
                if t + 1 < 4:
                    tileA2b(t + 1)

            def hn_sq(h):
                idx = []
                for c in range(2):
                    k = c * 4 + h
                    i = rot("sq", 4)
                    S_.op("act", lambda e: e.activation(out=sq[0:96, i, :], in_=big32[0:96, k * 512:(k + 1) * 512], func=AF.Square), r=[("b32", k)], w=[("sq", i)])
                    idx.append(i)
                return idx

            def hn_fin(h, idx):
                ps, pk = bank()
                for j, i in enumerate(idx):
                    S_.op("pe", lambda e: e.matmul(ps[0:96, :], ones_bf[0:96, 0:96], sq[0:96, i, :], start=(j == 0), stop=(j == 1)),
                          r=[("sq", i), "ones"], w=[pk])
                ri = rot("rdt", 2)
                S_.op("act", lambda e: e.activation(out=rdt[0:96, ri, :], in_=ps[0:96, :], func=AF.Ln, bias=epsc[0:96, :], scale=1.0 / DV),
                      r=[pk, "epsc"], w=[("rdt", ri)])
                S_.op("act", lambda e: e.activation(out=rdt[0:96, ri, :], in_=rdt[0:96, ri, :], func=AF.Exp, scale=-0.5), r=[("rdt", ri)], w=[("rdt", ri)])
                for c in range(2):
                    k = c * 4 + h
                    ch = 2 * h + c
                    S_.op("dve", lambda e: e.scalar_tensor_tensor(out=big32[0:96, k * 512:(k + 1) * 512], in0=big32[0:96, k * 512:(k + 1) * 512],
                                                                 scalar=gcol[0:96, G_MOUT + ch:G_MOUT + ch + 1], in1=rdt[0:96, ri, :], op0=ALU.mult, op1=ALU.mult),
                          r=[("b32", k), ("rdt", ri), "gcol"], w=[("b32", k)])
                    S_.op("pool", lambda e: e.tensor_tensor(out=mixT[0:96, ch, :], in0=big32[0:96, k * 512:(k + 1) * 512], in1=A22[0:96, ch, :], op=ALU.mult),
                          r=[("b32", k), ("A22", ch)], w=[("mixT", ch)])

            hidx = {0: hn_sq(0), 1: hn_sq(1)}
            for h in range(4):
                hn_fin(h, hidx[h])
                if h + 2 < 4:
                    hidx[h + 2] = hn_sq(h + 2)
            mem_attention(0, 8)
            out_proj(0, [96] * 8 + [128] * 2)
            post_norm_residual(G_MIXPOST)
            if stage >= 2:
                ffn(0, hook=lambda: rope_tables(blk))
            if stage >= 2.5:
                norm_to_hT(G_KV)
                slot, sk = wload([(lambda s: s[:, 0:8 * 512].rearrange("p (k n) -> p k n", k=8), kview("w_kv"), "w_kv")])
                pend = None
                for c2 in range(2):
                    ps, pk = bank()
                    for kc in range(8):
                        S_.op("pe", lambda e: e.matmul(ps[:, :], slot[:, kc * 512 + c2 * 128:kc * 512 + (c2 + 1) * 128], hT[:, kc, :], start=(kc == 0), stop=(kc == 7)),
                              r=[sk, hTk(kc)], w=[pk])
                    if pend is not None:
                        rope_evac(*pend)
                    pend = (ps, pk, [((0, 64), kpad[0:64, 2 * c2, 128:640], [("kpad", 2 * c2)]),
                                     ((64, 128), kpad[64:128, 2 * c2 + 1, 128:640], [("kpad", 2 * c2 + 1)])])
                pend_kv = pend
                for t in range(4):
                    ps, pk = bank()
                    for kc in range(8):
                        S_.op("pe", lambda e: e.matmul(ps[:, 0:256], hT[:, kc, t * 128:(t + 1) * 128], slot[:, kc * 512 + 256:kc * 512 + 512], start=(kc == 0), stop=(kc == 7)),
                              r=[sk, hTk(kc)], w=[pk])
                    if t == 0:
                        rope_evac(*pend_kv)
                    for kvh_ in range(4):
                        S_.op("act", lambda e: e.activation(out=vsh[:, 1 + t, kvh_, (kvh_ % 2) * 64:(kvh_ % 2) * 64 + 64], in_=ps[:, kvh_ * 64:(kvh_ + 1) * 64], func=AF.Copy),
                              r=[pk], w=[("vsh", 1 + t)])

            if stage >= 2.75:
                norm_to_hT(40 + G_MIXPRE, reuse_stats=(stage >= 2.5))
                slot, sk = wload([(lambda s: s[:, 0:6144].rearrange("p (k n) -> p k n", k=8), kview("w_in_b")[:, :, 0:768], "w_in_b")])
                slotM, skM = wload([(lambda s: s[:, 0:2048].rearrange("p (k n) -> p k n", k=8), kview("w_in_b")[:, :, 768:1024], "w_in_b")])
                pend = None
                for c in range(6):
                    ps, pk = bank()
                    for kc in range(8):
                        S_.op("pe", lambda e: e.matmul(ps[:, :], slot[:, kc * 768 + c * 128:kc * 768 + (c + 1) * 128], hT[:, kc, :], start=(kc == 0), stop=(kc == 7)),
                              r=[sk, hTk(kc)], w=[pk])
                    if pend is not None:
                        rope_evac(*pend)
                    pend = (ps, pk, [((0, 128), A22[:, c, :], [("A22", c)])])
                pend_q = pend
                for p in range(2):
                    ps, pk = bank()
                    for kc in range(8):
                        S_.op("pe", lambda e: e.matmul(ps[:, :], slotM[:, kc * 256 + p * 128:kc * 256 + (p + 1) * 128], hT[:, kc, :], start=(kc == 0), stop=(kc == 7)),
                              r=[skM, hTk(kc)], w=[pk])
                    S_.op("act", lambda e: e.activation(out=mqT[:, p, :], in_=ps[:, :], func=AF.Copy), r=[pk], w=[("mqT", p)])
                    if p == 0:
                        rope_evac(*pend_q)
                def swa_scores(c):
                    res = []
                    for e_, hq in enumerate(PAIRS[c]):
                        kvh = hq // 3
                        pc, pkc = bank()
                        pp, pkp = bank()
                        for qt in range(4):
                            qs = slice(qt * 128, (qt + 1) * 128)
                            S_.op("pe", lambda e: e.matmul(pc[:, qs], kpad[:, kvh, 128 + qt * 128:256 + qt * 128], A22[:, c, qs], start=True, stop=True),
                                  r=[("kpad", kvh), ("A22", c)], w=[pkc])
                            S_.op("pe", lambda e: e.matmul(pp[:, qs], kpad[:, kvh, qt * 128:128 + qt * 128], A22[:, c, qs], start=True, stop=True),
                                  r=[("kpad", kvh), ("A22", c)], w=[pkp])
                        ic = rot("pt", 8)
                        ip = rot("pt", 8)
                        S_.op("act", lambda e: e.activation(out=PT[:, ic, :], in_=pc[:, :], func=AF.Exp, scale=0.125), r=[pkc], w=[("PT", ic)])
                        S_.op("act", lambda e: e.activation(out=PT[:, ip, :], in_=pp[:, :], func=AF.Exp, scale=0.125), r=[pkp], w=[("PT", ip)])
                        S_.op("dve", lambda e: e.tensor_tensor(out=PT[:, ic, :], in0=PT[:, ic, :], in1=tri4[:], op=ALU.mult), r=[("PT", ic), "tri4"], w=[("PT", ic)])
                        mk_ = ust4f if blk == 0 else ust4
                        S_.op("pool", lambda e: e.tensor_tensor(out=PT[:, ip, :], in0=PT[:, ip, :], in1=mk_[:], op=ALU.mult),
                              r=[("PT", ip), "ust4", "ust4f"], w=[("PT", ip)])
                        res.append((kvh, ic, ip))
                    return res

                def swa_out(c, res):
                    po, pko = bank()
                    pdn, pkdn = bank()
                    for qt in range(4):
                        qs = slice(qt * 128, (qt + 1) * 128)
                        n = 0
                        for (kvh, ic, ip) in res:
                            S_.op("pe", lambda e: e.matmul(po[:, qs], vsh[:, 1 + qt, kvh, :], PT[:, ic, qs], start=(n == 0), stop=False),
                                  r=[("vsh", 1 + qt), ("PT", ic)], w=[pko])
                            n += 1
                            S_.op("pe", lambda e: e.matmul(po[:, qs], vsh[:, qt, kvh, :], PT[:, ip, qs], start=False, stop=(n == 3)),
                                  r=[("vsh", qt), ("PT", ip)], w=[pko])
                            n += 1
                    n = 0
                    for e_, (kvh, ic, ip) in enumerate(res):
                        for i_ in (ic, ip):
                            S_.op("pe", lambda e: e.matmul(pdn[:, :], onesAB[:, e_, :], PT[:, i_, :], start=(n == 0), stop=(n == 3)), r=["onesAB", ("PT", i_)], w=[pkdn])
                            n += 1
                    ri = rot("rdt", 2)
                    S_.op("act", lambda e: e.activation(out=rdt[:, ri, :], in_=pdn[:, :], func=AF.Ln, bias=esink[:, c:c + 1]),
                          r=[pkdn, "esink"], w=[("rdt", ri)])
                    S_.op("act", lambda e: e.activation(out=rdt[:, ri, :], in_=rdt[:, ri, :], func=AF.Exp, scale=-1.0), r=[("rdt", ri)], w=[("rdt", ri)])
                    S_.op("dve", lambda e: e.tensor_tensor(out=mixT[:, c, :], in0=po[:, :], in1=rdt[:, ri, :], op=ALU.mult),
                          r=[pko, ("rdt", ri)], w=[("mixT", c)])

                prev = None
                for c in range(6 if stage >= 2.9 else 0):
                    cur = swa_scores(c)
                    if prev is not None:
                        swa_out(*prev)
                    prev = (c, cur)
                if prev is not None:
                    swa_out(*prev)
            if stage >= 3:
                S_.op("pool", lambda e: e.tensor_copy(out=kpad[:, :, 0:128], in_=kpad[:, :, 512:640]), r=[("kpad", i) for i in range(4)], w=[("kpad", i) for i in range(4)])
                S_.op("pool", lambda e: e.tensor_copy(out=vsh[:, 0, :, :], in_=vsh[:, 4, :, :]), r=[("vsh", 4)], w=[("vsh", 0)])
                mem_attention(1, 6)
                out_proj(1, [128] * 8)
                post_norm_residual(40 + G_MIXPOST)
            if stage >= 4:
                ffn(1)
            store_out(blk)
        S_.finish()
        print("instructions emitted:", S_.n_instr, {k: v for k, v in S_.cnt.items()})
    return nc


def make_consts():
    cst = np.zeros((128, 512), np.float32)
    cst[:, 0:128] = np.eye(128, dtype=np.float32)
    s = np.arange(128)[:, None]
    j = np.arange(128)[None, :]
    cst[:, 128:256] = (s <= j).astype(np.float32)
    cst[:, 256:384] = (s > j).astype(np.float32)
    rm = np.zeros((64, 64), np.float32)
    for i in range(32):
        rm[32 + i, i] = -1.0
        rm[i, 32 + i] = 1.0
    cst[0:64, 384:448] = rm
    cst[64:128, 448:512] = rm
    return cst


def make_gcol(inp):
    g = np.zeros((128, NG), np.float32)

    def colz(v):
        return np.ascontiguousarray(np.asarray(v, np.float32).reshape(8, 128).T)

    for l in range(2):
        g[:, 40 * l + G_MIXPRE:40 * l + G_MIXPRE + 8] = colz(inp["g_mix_pre"][l])
        g[:, 40 * l + G_MIXPOST:40 * l + G_MIXPOST + 8] = colz(inp["g_mix_post"][l])
        g[:, 40 * l + G_FFNPRE:40 * l + G_FFNPRE + 8] = colz(inp["g_ffn_pre"][l])
        g[:, 40 * l + G_FFNPOST:40 * l + G_FFNPOST + 8] = colz(inp["g_ffn_post"][l])
        g[:, 40 * l + G_MEM:40 * l + G_MEM + 8] = colz(inp["g_mem"][l])
    g[:, G_KV:G_KV + 8] = colz(inp["g_kv"])
    g[0:96, G_MOUT:G_MOUT + 8] = np.asarray(inp["g_mlstm_out"][0], np.float32).reshape(8, 96).T
    g[:, G_BG:G_BG + 8] = np.asarray(inp["b_gates_a"][0], np.float32)[None, :]
    sk_ = np.asarray(inp["sinks_b"][0], np.float32)
    for c, (a, b) in enumerate(PAIRS):
        g[0:64, G_SINK + c] = sk_[a]
        g[64:128, G_SINK + c] = sk_[b]
    inv = (1.0 / (np.float32(10000.0) ** (np.arange(0, 64, 2, dtype=np.float32) / np.float32(64)))).astype(np.float32)
    g[:, G_INVF] = np.tile(inv, 4)
    return g


def make_in_maps(inp, ncores=8):
    shared = {
        "w_mem_kv0": inp["w_mem_kv"][0], "w_mem_kv1": inp["w_mem_kv"][1], "w_in_a": inp["w_in_a"][0],
        "w_out0": inp["w_out"][0], "w_out1": inp["w_out"][1], "w_ffn_in0": inp["w_ffn_in"][0], "w_ffn_in1": inp["w_ffn_in"][1],
        "w_ffn_out0": inp["w_ffn_out"][0], "w_ffn_out1": inp["w_ffn_out"][1], "w_kv": inp["w_kv"], "w_in_b": inp["w_in_b"][0],
    }
    shared = {k: np.ascontiguousarray(np.asarray(v, np.float32)) for k, v in shared.items()}
    shared["gcol"] = make_gcol(inp)
    shared["cst"] = make_consts()
    maps = []
    for b in range(ncores):
        m = dict(shared)
        m["x"] = np.ascontiguousarray(np.asarray(inp["x"][b], np.float32))
        m["mem"] = np.ascontiguousarray(np.asarray(inp["mem"][b], np.float32))
        m["posr"] = np.ascontiguousarray(np.broadcast_to(np.asarray(inp["positions"][b], np.int32)[None, :], (128, S)))
        maps.append(m)
    return maps


_NC_CACHE = {}


def kernel(**inputs):
    inp = {k: np.asarray(v) for k, v in inputs.items()}
    if "full" not in _NC_CACHE:
        _NC_CACHE["full"] = build()
    nc = _NC_CACHE["full"]
    maps = make_in_maps(inp, 8)
    res = run_bass_kernel_spmd(nc, maps, core_ids=list(range(8)))
    out = np.stack([np.asarray(r["out"], np.float32) for r in res.results], axis=0)
    return out
```

```python
import math
import os
from contextlib import ExitStack

import numpy as np
import concourse.bass as bass
import concourse.mybir as mybir
from concourse.bass_utils import run_bass_kernel_spmd

F32 = mybir.dt.float32
BF16 = mybir.dt.bfloat16
I32 = mybir.dt.int32
AF = mybir.ActivationFunctionType
ALU = mybir.AluOpType

D = 1024
S = 4096
TB = 512
NBLK = S // TB
DFF = 2816
NFC = DFF // 128
EPS = 1e-6
DQK = 96
DV = 192
PAIRS = [(0, 3), (1, 4), (2, 5), (6, 9), (7, 10), (8, 11)]
TWO_PI = 2.0 * math.pi

WSHAPES = {
    "w_mem_kv0": [D, 512], "w_mem_kv1": [D, 512], "w_in_a": [D, 2568], "w_out0": [D, D],
    "w_ffn_in0": [D, 2 * DFF], "w_ffn_out0": [DFF, D], "w_kv": [D, 512], "w_in_b": [D, D],
    "w_out1": [D, D], "w_ffn_in1": [D, 2 * DFF], "w_ffn_out1": [DFF, D],
}
G_MIXPRE, G_MIXPOST, G_FFNPRE, G_FFNPOST, G_MEM = 0, 8, 16, 24, 32
G_KV = 80
G_MOUT = 88
G_BG = 96
G_SINK = 104
G_INVF = 116
NG = 117

SAME_ENGINE_SYNC = True
NDQ = 8


class Sched:
    def __init__(self, nc, st):
        self.nc = nc
        self.E = {"pe": nc.tensor, "act": nc.scalar, "dve": nc.vector, "pool": nc.gpsimd, "sp": nc.sync}
        self.semh = {}
        for k in self.E:
            self.semh[k] = st.enter_context(nc.semaphore("c_" + k))
        self.cnt = {k: 0 for k in self.E}
        self.seen = {}
        self.lastw = {}
        self.rd = {}
        self.dq = {}
        self.dqi = {}
        for e in ("sp", "pool"):
            self.dq[e] = []
            for i in range(NDQ):
                nm = "d_%s%d" % (e, i)
                self.semh[nm] = st.enter_context(nc.semaphore(nm))
                self.dq[e].append([nm, 0])
            self.dqi[e] = 0
        self.out_tokens = []
        self.n_instr = 0
        self.n_uniq = 0
        self.st = st

    def _need(self, r, w):
        need = {}

        def add(t):
            if t is not None and need.get(t[0], 0) < t[1]:
                need[t[0]] = t[1]

        for k in r:
            add(self.lastw.get(k))
        for k in w:
            add(self.lastw.get(k))
            for s, v in self.rd.get(k, {}).items():
                add((s, v))
        return need

    def _wait(self, e, s, v):
        if s == e and (e == "pe" or not SAME_ENGINE_SYNC):
            return
        if self.seen.get((e, s), 0) >= v:
            return
        self.E[e].wait_ge(self.semh[s], v)
        self.seen[(e, s)] = v

    def _record(self, tok, r, w):
        for k in r:
            d = self.rd.setdefault(k, {})
            if d.get(tok[0], 0) < tok[1]:
                d[tok[0]] = tok[1]
        for k in w:
            self.lastw[k] = tok
            self.rd[k] = {}

    def op(self, e, fn, r=(), w=()):
        for s, v in self._need(r, w).items():
            self._wait(e, s, v)
        ins = fn(self.E[e])
        self.cnt[e] += 1
        ins.then_inc(self.semh[e], 1)
        self._record((e, self.cnt[e]), r, w)
        self.n_instr += 1

    def dma(self, e, out, in_, r=(), w=(), is_out=False):
        for s, v in self._need(r, w).items():
            self._wait(e, s, v)
        slot = self.dq[e][self.dqi[e]]
        self.dqi[e] = (self.dqi[e] + 1) % NDQ
        nm, tot = slot
        if tot > 0:
            self._wait(e, nm, tot)
        self.E[e].dma_start(out=out, in_=in_).then_inc(self.semh[nm], 16)
        slot[1] = tot + 16
        tok = (nm, tot + 16)
        self._record(tok, r, w)
        if is_out:
            self.out_tokens.append(tok)
        self.n_instr += 1

    def dma_unique(self, e, out, in_, r=(), w=()):
        for s, v in self._need(r, w).items():
            self._wait(e, s, v)
        nm = "u_%d" % self.n_uniq
        self.n_uniq += 1
        self.semh[nm] = self.st.enter_context(self.nc.semaphore(nm))
        self.E[e].dma_start(out=out, in_=in_).then_inc(self.semh[nm], 16)
        self._record((nm, 16), r, w)
        self.n_instr += 1

    def finish(self):
        for s, v in self.out_tokens:
            self._wait("sp", s, v)
        self.E["sp"].nop()


def build(nblk=NBLK, stage=4):
    nc = bass.Bass("TRN2", target_bir_lowering=False)

    def din(name, shape, dt=F32):
        return nc.dram_tensor(name, shape, dt, kind="ExternalInput").ap()

    x_d = din("x", [S, D])
    mem_d = din("mem", [256, D])
    pos_d = din("posr", [128, S], I32)
    W = {n: din(n, sh) for n, sh in WSHAPES.items()}
    gcol_d = din("gcol", [128, NG])
    cst_d = din("cst", [128, 512])
    out_d = nc.dram_tensor("out", [S, D], F32, kind="ExternalOutput").ap()
    SCR = {n: nc.dram_tensor("s_" + n, sh, BF16, kind="Internal").ap() for n, sh in WSHAPES.items()}

    with ExitStack() as st:
        def sb(name, shape, dt):
            return st.enter_context(nc.sbuf_tensor("sb_" + name, shape, dt))

        S_ = Sched(nc, st)
        xT = sb("xT", [128, 8, 512], F32)
        big32 = sb("big32", [128, 4096], F32)
        hT = sb("hT", [128, 8, 512], BF16)
        sq = sb("sq", [128, 4, 512], BF16)
        rs_tmp = sb("rs_tmp", [128, 512], F32)
        rstd = sb("rstd", [128, 512], F32)
        rstd_bf = sb("rstd_bf", [128, 512], BF16)
        rstd_pb = sb("rstd_pb", [128, 512], BF16)
        ws = [sb("ws%d" % i, [128, 6144], BF16) for i in range(3)]
        xstage = sb("xstage", [128, 4096], F32)
        A22 = sb("A22", [128, NFC, 512], BF16)
        mixT = sb("mixT", [128, 10, 512], BF16)
        gcol = sb("gcol", [128, NG], F32)
        cst = sb("cst", [128, 512], F32)
        ones_bf = sb("ones_bf", [128, 128], BF16)
        rm2_bf = sb("rm2_bf", [128, 128], BF16)
        tri4 = sb("tri4", [128, 512], BF16)
        ust4 = sb("ust4", [128, 512], BF16)
        ust4f = sb("ust4f", [128, 512], BF16)
        epsc = sb("epsc", [128, 1], F32)
        esink = sb("esink", [128, 6], F32)
        onesAB = sb("onesAB", [128, 2, 128], BF16)
        eb = sb("eb", [128, 2, 4, 128], F32)
        ea = sb("ea", [128, 2, 4, 128], F32)
        gsb = sb("gsb", [128, 2, 8], F32)
        gth = sb("gth", [128, 2, 8], F32)
        gee = sb("gee", [128, 2, 4], F32)
        glp = sb("glp", [128, 2, 4], F32)
        gig = sb("gig", [128, 2, 4], F32)
        ga2 = sb("ga2", [128, 2, 4], F32)
        gek = sb("gek", [128, 2, 4], F32)
        ktok = sb("ktok", [128, 2, 4, 96], BF16)
        kraw = sb("kraw", [128, 2, 384], BF16)
        vaug = sb("vaug", [128, 2, 4, 196], BF16)
        PTm = sb("PTm", [128, 2, 4, 128], BF16)
        CT = sb("CT", [128, 4, 193], F32)
        CTb = sb("CTb", [128, 4, 196], BF16)
        mqT = sb("mqT", [128, 2, 512], BF16)
        rdt = sb("rdt", [128, 2, 512], F32)
        sgt = sb("sgt", [128, 2, 512], BF16)
        cc2 = sb("cc2", [128, 512], F32)
        ss2 = sb("ss2", [128, 512], F32)
        tmpA = sb("tmpA", [128, 512], F32)
        tmpB = sb("tmpB", [128, 512], F32)
        posi = sb("posi", [128, 512], I32)
        kint = posi
        kpad = sb("kpad", [128, 4, 640], BF16)
        vsh = sb("vsh", [128, 5, 4, 128], BF16)
        PT = sb("PT", [128, 8, 512], BF16)
        qraw = sgt
        mkT = sb("mkT", [128, 2, 4, 256], BF16)
        mv = sb("mv", [128, 2, 2, 4, 128], BF16)
        psb = [st.enter_context(nc.psum_tensor("ps%d" % i, [128, 512], F32)) for i in range(8)]

        ident = cst[:, 0:128]
        tri_f = cst[:, 128:256]
        ust_f = cst[:, 256:384]

        state = {"bank": 0, "slot": 0, "sq": 0, "rdt": 0, "sgt": 0, "pt": 0, "qraw": 0}

        def bank():
            b = state["bank"]
            state["bank"] = (b + 1) % 8
            return psb[b], ("ps", b)

        def rot(name, n):
            v = state[name]
            state[name] = (v + 1) % n
            return v

        b32keys = [("b32", k) for k in range(8)]

        def a22w(ci):
            return [("A22", ci)] + ([("qk", ci, t_) for t_ in range(4)] if 8 <= ci < 16 else [])
        xTk = lambda k: ("xT", k)
        hTk = lambda k: ("hT", k)
        hTkeys = [hTk(k) for k in range(8)]

        S_.dma("sp", gcol[:], gcol_d, w=["gcol"])
        S_.dma("sp", cst[:], cst_d, w=["cst"])
        S_.op("dve", lambda e: e.memset(ones_bf[:], 1.0), w=["ones"])
        S_.op("dve", lambda e: e.memset(epsc[:], EPS), w=["epsc"])
        S_.op("dve", lambda e: e.tensor_copy(out=rm2_bf[:], in_=cst[:, 384:512]), r=["cst"], w=["rm2"])
        for i in range(4):
            S_.op("dve", lambda e: e.tensor_copy(out=tri4[:, i * 128:(i + 1) * 128], in_=tri_f), r=["cst"], w=["tri4"])
            S_.op("dve", lambda e: e.tensor_copy(out=ust4[:, i * 128:(i + 1) * 128], in_=ust_f), r=["cst"], w=["ust4"])
            if i > 0:
                S_.op("dve", lambda e: e.tensor_copy(out=ust4f[:, i * 128:(i + 1) * 128], in_=ust_f), r=["cst"], w=["ust4f"])
        S_.op("dve", lambda e: e.memset(ust4f[:, 0:128], 0.0), w=["ust4f"])
        S_.op("act", lambda e: e.activation(out=esink[:], in_=gcol[:, G_SINK:G_SINK + 6], func=AF.Exp), r=["gcol"], w=["esink"])
        S_.op("dve", lambda e: e.memset(onesAB[:], 0.0), w=["onesAB"])
        S_.op("dve", lambda e: e.memset(onesAB[:, 0, 0:64], 1.0), w=["onesAB"])
        S_.op("dve", lambda e: e.memset(onesAB[:, 1, 64:128], 1.0), w=["onesAB"])
        S_.op("pool", lambda e: e.memset(mkT[:], 0.0), w=[("mkT", 0), ("mkT", 1)])
        S_.op("pool", lambda e: e.memset(mv[:], 0.0), w=[("mv", 0), ("mv", 1)])
        S_.op("pool", lambda e: e.memset(kpad[:], 0.0), w=["kpad"])
        S_.op("pool", lambda e: e.memset(vsh[:], 0.0), w=["vsh"])
        S_.op("pool", lambda e: e.memset(CT[:], 0.0), w=["CT"])
        S_.op("pool", lambda e: e.memset(CTb[:], 0.0), w=["CTb"])
        S_.op("pool", lambda e: e.memset(vaug[:], 1.0), w=["vaug0", "vaug1"])

        scr_keys = {}

        def cast_rows(name, rows_per, gate=()):
            src, dst = W[name], SCR[name]
            n = src.shape[0]
            keys = []
            for i, r0 in enumerate(range(0, n, rows_per)):
                r1 = min(n, r0 + rows_per)
                k = ("scr", name, i)
                S_.dma_unique("pool", dst[r0:r1, :], src[r0:r1, :], r=gate, w=[k])
                keys.append(k)
            scr_keys[name] = keys

        def cast_in_b(gate=()):
            src, dst = W["w_in_b"], SCR["w_in_b"]
            keys = []
            i = 0
            for c, (a, b) in enumerate(PAIRS):
                for e_, h in enumerate((a, b)):
                    k = ("scr", "w_in_b", i)
                    i += 1
                    S_.dma_unique("pool", dst[:, c * 128 + e_ * 64:c * 128 + e_ * 64 + 64], src[:, h * 64:(h + 1) * 64], r=gate, w=[k])
                    keys.append(k)
            k = ("scr", "w_in_b", i)
            S_.dma_unique("pool", dst[:, 768:1024], src[:, 768:1024], r=gate, w=[k])
            keys.append(k)
            scr_keys["w_in_b"] = keys

        def cast_out1(gate=()):
            src, dst = W["w_out1"], SCR["w_out1"]
            keys = []
            i = 0
            for c, (a, b) in enumerate(PAIRS):
                for e_, h in enumerate((a, b)):
                    k = ("scr", "w_out1", i)
                    i += 1
                    S_.dma_unique("pool", dst[c * 128 + e_ * 64:c * 128 + e_ * 64 + 64, :], src[h * 64:(h + 1) * 64, :], r=gate, w=[k])
                    keys.append(k)
            k = ("scr", "w_out1", i)
            S_.dma_unique("pool", dst[768:1024, :], src[768:1024, :], r=gate, w=[k])
            keys.append(k)
            scr_keys["w_out1"] = keys

        jobs = []
        jstate = {"next": 0}
        LOOKAHEAD = 3

        def cast_cols(name, jobname, col_ranges, gate):
            keys = []
            for i, (c0, c1) in enumerate(col_ranges):
                k = ("scr", jobname, i)
                S_.dma_unique("pool", SCR[name][:, c0:c1], W[name][:, c0:c1], r=gate, w=[k])
                keys.append(k)
            return keys

        def cast_rowrange(name, jobname, r0, r1, gate):
            k = ("scr", jobname, 0)
            S_.dma_unique("pool", SCR[name][r0:r1, :], W[name][r0:r1, :], r=gate, w=[k])
            return [k]

        def whole(name):
            def f(gate):
                cast_rows(name, 512, gate)
                return scr_keys[name]
            return f

        jobs.append(("w_mem_kv0", whole("w_mem_kv0")))
        jobs.append(("w_mem_kv1", whole("w_mem_kv1")))
        jobs.append(("w_in_a", whole("w_in_a")))
        jobs.append(("w_out0", whole("w_out0")))

        def add_ffn_jobs(l):
            ni, no = "w_ffn_in%d" % l, "w_ffn_out%d" % l
            for u in range(8):
                ncol = 384 if u < 7 else 128
                jobs.append(("%s:u%d" % (ni, u), (lambda gate, u=u, ncol=ncol, ni=ni: cast_cols(
                    ni, "%s:u%d" % (ni, u), [(u * 384, u * 384 + ncol), (DFF + u * 384, DFF + u * 384 + ncol)], gate))))
            for g3 in range(3):
                r0, r1 = g3 * 1024, min(DFF, g3 * 1024 + 1024)
                jobs.append(("%s:g%d" % (no, g3), (lambda gate, g3=g3, r0=r0, r1=r1, no=no: cast_rowrange(no, "%s:g%d" % (no, g3), r0, r1, gate))))

        add_ffn_jobs(0)
        jobs.append(("w_kv", whole("w_kv")))

        def job_in_b(gate):
            cast_in_b(gate)
            return scr_keys["w_in_b"]

        def job_out1(gate):
            cast_out1(gate)
            return scr_keys["w_out1"]

        jobs.append(("w_in_b", job_in_b))
        jobs.append(("w_out1", job_out1))
        add_ffn_jobs(1)
        job_index = {n: i for i, (n, _) in enumerate(jobs)}

        def emit_jobs_upto(idx, gate):
            with nc.allow_non_contiguous_dma(reason="weight re-layout"):
                while jstate["next"] <= min(idx, len(jobs) - 1):
                    n, fn = jobs[jstate["next"]]
                    scr_keys[n] = fn(gate)
                    jstate["next"] += 1

        emit_jobs_upto(2, ())

        def wslot():
            i = state["slot"]
            state["slot"] = (i + 1) % 3
            return ws[i], ("ws", i)

        def wload(parts):
            slot, key = wslot()
            need = max(job_index[name] for _, _, name in parts)
            if jstate["next"] <= need:
                emit_jobs_upto(need, ())
            with nc.allow_non_contiguous_dma(reason="weight stream"):
                for fn, src, name in parts:
                    S_.dma("sp", fn(slot), src, r=scr_keys[name], w=[key])
            if jstate["next"] < len(jobs):
                emit_jobs_upto(max(need + LOOKAHEAD, jstate["next"]), [key])
            return slot, key

        def kview(name):
            return SCR[name].rearrange("(k p) n -> p k n", p=128)

        def stats_rstd(srcs, P, n, nfeat):
            sqs = []
            for ap, keys in srcs:
                i = rot("sq", 4)
                S_.op("act", lambda e: e.activation(out=sq[0:P, i, 0:n], in_=ap, func=AF.Square), r=keys, w=[("sq", i)])
                sqs.append(i)
            ps, pk = bank()
            for j, i in enumerate(sqs):
                S_.op("pe", lambda e: e.matmul(ps[0:P, 0:n], ones_bf[0:P, 0:P], sq[0:P, i, 0:n], start=(j == 0), stop=(j == len(sqs) - 1)),
                      r=[("sq", i), "ones"], w=[pk])
            S_.op("act", lambda e: e.activation(out=rs_tmp[0:P, 0:n], in_=ps[0:P, 0:n], func=AF.Ln, bias=epsc[0:P, :], scale=1.0 / nfeat),
                  r=[pk, "epsc"], w=["rs_tmp"])
            S_.op("act", lambda e: e.activation(out=rstd[0:P, 0:n], in_=rs_tmp[0:P, 0:n], func=AF.Exp, scale=-0.5), r=["rs_tmp"], w=["rstd"])

        def norm_to_hT(gbase, n=512, reuse_stats=False):
            if not reuse_stats:
                ps, pk = bank()
            for k in range(8):
                if not reuse_stats:
                    i = rot("sq", 4)
                    S_.op("act", lambda e: e.activation(out=sq[:, i, 0:n], in_=xT[:, k, 0:n], func=AF.Square), r=[xTk(k)], w=[("sq", i)])
                    S_.op("pe", lambda e: e.matmul(ps[:, 0:n], ones_bf[:], sq[:, i, 0:n], start=(k == 0), stop=(k == 7)),
                          r=[("sq", i), "ones"], w=[pk])
                if k % 2 == 0:
                    S_.op("act", lambda e: e.activation(out=hT[:, k, 0:n], in_=xT[:, k, 0:n], func=AF.Copy, scale=gcol[:, gbase + k:gbase + k + 1]),
                          r=[xTk(k), "gcol"], w=[hTk(k)])
                else:
                    S_.op("dve", lambda e: e.tensor_scalar_mul(out=hT[:, k, 0:n], in0=xT[:, k, 0:n], scalar1=gcol[:, gbase + k:gbase + k + 1]),
                          r=[xTk(k), "gcol"], w=[hTk(k)])
            if not reuse_stats:
                S_.op("act", lambda e: e.activation(out=rs_tmp[:, 0:n], in_=ps[:, 0:n], func=AF.Ln, bias=epsc[:], scale=1.0 / D),
                      r=[pk, "epsc"], w=["rs_tmp"])
                S_.op("act", lambda e: e.activation(out=rstd_bf[:, 0:n], in_=rs_tmp[:, 0:n], func=AF.Exp, scale=-0.5), r=["rs_tmp"], w=["rstd_bf"])
            for k in range(8):
                S_.op("dve", lambda e: e.tensor_tensor(out=hT[:, k, 0:n], in0=hT[:, k, 0:n], in1=rstd_bf[:, 0:n], op=ALU.mult),
                      r=[hTk(k), "rstd_bf"], w=[hTk(k)])

        def b32(k):
            return big32[:, k * 512:(k + 1) * 512]

        y16_all = big32[:].bitcast(BF16)

        def y16(k):
            return y16_all[:, k * 512:(k + 1) * 512]

        def y16k(k):
            return ("b32", k // 2)

        def post_norm_residual(gbase):
            ps, pk = bank()
            for k in range(8):
                i = rot("sq", 4)
                S_.op("act", lambda e: e.activation(out=sq[:, i, :], in_=y16(k), func=AF.Square), r=[y16k(k)], w=[("sq", i)])
                S_.op("pe", lambda e: e.matmul(ps[:, :], ones_bf[:], sq[:, i, :], start=(k == 0), stop=(k == 7)),
                      r=[("sq", i), "ones"], w=[pk])
            S_.op("act", lambda e: e.activation(out=rs_tmp[:], in_=ps[:], func=AF.Ln, bias=epsc[:], scale=1.0 / D),
                  r=[pk, "epsc"], w=["rs_tmp"])
            S_.op("act", lambda e: e.activation(out=rstd_pb[:], in_=rs_tmp[:], func=AF.Exp, scale=-0.5), r=["rs_tmp"], w=["rstd_pb"])
            for k in range(8):
                S_.op("dve", lambda e: e.scalar_tensor_tensor(out=y16(k), in0=y16(k), scalar=gcol[:, gbase + k:gbase + k + 1],
                                                             in1=rstd_pb[:], op0=ALU.mult, op1=ALU.mult),
                      r=[y16k(k), "rstd_pb", "gcol"], w=[y16k(k)])
                S_.op("pool" if k in (1, 4, 6) else "dve", lambda e: e.tensor_tensor(out=xT[:, k, :], in0=xT[:, k, :], in1=y16(k), op=ALU.add),
                      r=[y16k(k), xTk(k)], w=[xTk(k)])

        def load_transpose(src_rows, ntile):
            S_.dma("sp", big32[:, 0:ntile * 1024].rearrange("p (t d) -> p t d", t=ntile),
                   src_rows.rearrange("(t p) d -> p t d", p=128), w=b32keys[0:2 * ntile])
            for k in range(8):
                ps, pk = bank()
                for t in range(ntile):
                    S_.op("pe", lambda e: e.transpose(ps[:, t * 128:(t + 1) * 128], big32[:, t * 1024 + k * 128:t * 1024 + (k + 1) * 128], ident),
                          r=[("b32", 2 * t), ("b32", 2 * t + 1), "cst"], w=[pk])
                S_.op("act", lambda e: e.activation(out=xT[:, k, 0:ntile * 128], in_=ps[:, 0:ntile * 128], func=AF.Copy), r=[pk], w=[xTk(k)])

        xskeys = [("xs", t) for t in range(4)]

        def prefetch_x(blk):
            S_.dma("sp", xstage[:].rearrange("p (t d) -> p t d", t=4),
                   x_d[blk * TB:(blk + 1) * TB, :].rearrange("(t p) d -> p t d", p=128), w=xskeys)

        def transpose_x():
            for k in range(8):
                ps, pk = bank()
                for t in range(4):
                    S_.op("pe", lambda e: e.transpose(ps[:, t * 128:(t + 1) * 128], xstage[:, t * 1024 + k * 128:t * 1024 + (k + 1) * 128], ident),
                          r=[("xs", t), "cst"], w=[pk])
                S_.op("act", lambda e: e.activation(out=xT[:, k, :], in_=ps[:, :], func=AF.Copy), r=[pk], w=[xTk(k)])

        def store_out(blk):
            for t in range(4):
                for half in range(2):
                    ps, pk = bank()
                    for kk in range(4):
                        k = half * 4 + kk
                        S_.op("pe", lambda e: e.transpose(ps[:, kk * 128:(kk + 1) * 128], xT[:, k, t * 128:(t + 1) * 128], ident),
                              r=[xTk(k), "cst"], w=[pk])
                    S_.op("act", lambda e: e.activation(out=big32[:, t * 1024 + half * 512:t * 1024 + half * 512 + 512], in_=ps[:], func=AF.Copy),
                          r=[pk], w=[("b32", 2 * t + half)])
            S_.dma("sp", out_d[blk * TB:(blk + 1) * TB, :].rearrange("(t p) d -> p t d", p=128),
                   big32[:].rearrange("p (t d) -> p t d", t=4), r=b32keys, is_out=True)

        def mem_attention(l, mix_base):
            def scores(p):
                res = []
                for e_ in range(2):
                    h = 2 * p + e_
                    idx = []
                    for mc in range(2):
                        ps, pk = bank()
                        S_.op("pe", lambda e: e.matmul(ps[:, :], mkT[:, l, h, mc * 128:(mc + 1) * 128], mqT[:, p, :], start=True, stop=True),
                              r=[("mkT", l), ("mqT", p)], w=[pk])
                        i = rot("pt", 8)
                        S_.op("act", lambda e: e.activation(out=PT[:, i, :], in_=ps[:, :], func=AF.Exp), r=[pk], w=[("PT", i)])
                        idx.append(i)
                    res.append((h, idx))
                return res

            def outp(p, res):
                pso, pko = bank()
                psd, pkd = bank()
                n = 0
                for e_, (h, idx) in enumerate(res):
                    for mc in range(2):
                        S_.op("pe", lambda e: e.matmul(pso[:, :], mv[:, l, mc, h, :], PT[:, idx[mc], :], start=(n == 0), stop=(n == 3)),
                              r=[("mv", l), ("PT", idx[mc])], w=[pko])
                        n += 1
                n = 0
                for e_, (h, idx) in enumerate(res):
                    for mc in range(2):
                        S_.op("pe", lambda e: e.matmul(psd[:, :], onesAB[:, e_, :], PT[:, idx[mc], :], start=(n == 0), stop=(n == 3)),
                              r=["onesAB", ("PT", idx[mc])], w=[pkd])
                        n += 1
                ri = rot("rdt", 2)
                S_.op("act", lambda e: e.activation(out=rdt[:, ri, :], in_=psd[:, :], func=AF.Ln), r=[pkd], w=[("rdt", ri)])
                S_.op("act", lambda e: e.activation(out=rdt[:, ri, :], in_=rdt[:, ri, :], func=AF.Exp, scale=-1.0), r=[("rdt", ri)], w=[("rdt", ri)])
                S_.op("dve", lambda e: e.tensor_tensor(out=mixT[:, mix_base + p, :], in0=pso[:, :], in1=rdt[:, ri, :], op=ALU.mult),
                      r=[pko, ("rdt", ri)], w=[("mixT", mix_base + p)])

            r0 = scores(0)
            r1 = scores(1)
            outp(0, r0)
            outp(1, r1)

        def out_proj(l, chunks):
            name = "w_out%d" % l
            nch = len(chunks)
            for half in range(2):
                cols = slice(half * 512, (half + 1) * 512)
                if l == 0:
                    parts = [
                        (lambda s: s[0:96, 0:8 * 512].rearrange("p (c n) -> p c n", c=8),
                         SCR[name][0:768, cols].rearrange("(c p) n -> p c n", p=96), name),
                        (lambda s: s[:, 8 * 512:10 * 512].rearrange("p (c n) -> p c n", c=2),
                         SCR[name][768:1024, cols].rearrange("(c p) n -> p c n", p=128), name),
                    ]
                else:
                    parts = [
                        (lambda s: s[:, 0:8 * 512].rearrange("p (c n) -> p c n", c=8),
                         SCR[name][:, cols].rearrange("(c p) n -> p c n", p=128), name),
                    ]
                slot, sk = wload(parts)
                for dcl in range(4):
                    dc = half * 4 + dcl
                    ps, pk = bank()
                    for ci, P in enumerate(chunks):
                        S_.op("pe", lambda e: e.matmul(ps[:, :], slot[0:P, ci * 512 + dcl * 128:ci * 512 + (dcl + 1) * 128], mixT[0:P, ci, :],
                                                      start=(ci == 0), stop=(ci == nch - 1)),
                              r=[sk, ("mixT", ci)], w=[pk])
                    S_.op("dve", lambda e: e.tensor_copy(out=y16(dc), in_=ps[:, :]), r=[pk], w=[y16k(dc)])

        def ffn(l, hook=None):
            norm_to_hT(40 * l + G_FFNPRE)
            if hook is not None:
                hook()
            name = "w_ffn_in%d" % l
            kv = kview(name)
            for u in range(8):
                ncol = 384 if u < 7 else 128
                jn = "%s:u%d" % (name, u)
                parts = [
                    (lambda s: s[:, 0:8 * ncol].rearrange("p (k n) -> p k n", k=8), kv[:, :, u * 384:u * 384 + ncol], jn),
                    (lambda s: s[:, 8 * ncol:16 * ncol].rearrange("p (k n) -> p k n", k=8), kv[:, :, DFF + u * 384:DFF + u * 384 + ncol], jn),
                ]
                slot, sk = wload(parts)
                for j in range(ncol // 128):
                    f = u * 3 + j
                    psg, pkg = bank()
                    psu, pku = bank()
                    for kc in range(8):
                        S_.op("pe", lambda e: e.matmul(psg[:, :], slot[:, kc * ncol + j * 128:kc * ncol + (j + 1) * 128], hT[:, kc, :],
                                                      start=(kc == 0), stop=(kc == 7)), r=[sk, hTk(kc)], w=[pkg])
                    for kc in range(8):
                        S_.op("pe", lambda e: e.matmul(psu[:, :], slot[:, 8 * ncol + kc * ncol + j * 128:8 * ncol + kc * ncol + (j + 1) * 128], hT[:, kc, :],
                                                      start=(kc == 0), stop=(kc == 7)), r=[sk, hTk(kc)], w=[pku])
                    si = rot("sgt", 2)
                    S_.op("act", lambda e: e.activation(out=sgt[:, si, :], in_=psg[:, :], func=AF.Silu), r=[pkg], w=[("sgt", si)])
                    S_.op("dve", lambda e: e.tensor_tensor(out=A22[:, f, :], in0=psu[:, :], in1=sgt[:, si, :], op=ALU.mult),
                          r=[pku, ("sgt", si)], w=a22w(f))
            name = "w_ffn_out%d" % l
            kv = kview(name)
            for half in range(2):
                banks = [bank() for _ in range(4)]
                for g3 in range(3):
                    kc0, kc1 = g3 * 8, min(NFC, g3 * 8 + 8)
                    nk = kc1 - kc0
                    parts = [(lambda s: s[:, 0:nk * 512].rearrange("p (k n) -> p k n", k=nk), kv[:, kc0:kc1, half * 512:(half + 1) * 512], "%s:g%d" % (name, g3))]
                    slot, sk = wload(parts)
                    for dcl in range(4):
                        ps, pk = banks[dcl]
                        for kc in range(kc0, kc1):
                            S_.op("pe", lambda e: e.matmul(ps[:, :], slot[:, (kc - kc0) * 512 + dcl * 128:(kc - kc0) * 512 + (dcl + 1) * 128], A22[:, kc, :],
                                                          start=(kc == 0), stop=(kc == NFC - 1)), r=[sk, ("A22", kc)], w=[pk])
                for dcl in range(4):
                    ps, pk = banks[dcl]
                    dc = half * 4 + dcl
                    S_.op("dve", lambda e: e.tensor_copy(out=y16(dc), in_=ps[:, :]), r=[pk], w=[y16k(dc)])
            post_norm_residual(40 * l + G_FFNPOST)

        def rope_tables(blk):
            S_.dma("sp", posi[:], pos_d[:, blk * TB:(blk + 1) * TB], w=["posi"])
            S_.op("dve", lambda e: e.tensor_copy(out=tmpA[:], in_=posi[:]), r=["posi"], w=["tmpA"])
            S_.op("dve", lambda e: e.tensor_scalar_mul(out=tmpA[:], in0=tmpA[:], scalar1=gcol[:, G_INVF:G_INVF + 1]),
                  r=["tmpA", "gcol"], w=["tmpA"])
            for dst, shift, key in ((ss2, 0.0, "ss2"), (cc2, 0.5 * math.pi, "cc2")):
                S_.op("dve", lambda e: e.tensor_scalar(out=tmpB[:], in0=tmpA[:], scalar1=shift, scalar2=1.0 / TWO_PI, op0=ALU.add, op1=ALU.mult),
                      r=["tmpA"], w=["tmpB"])
                S_.op("dve", lambda e: e.tensor_copy(out=kint[:], in_=tmpB[:]), r=["tmpB"], w=["posi"])
                S_.op("dve", lambda e: e.tensor_copy(out=tmpB[:], in_=kint[:]), r=["posi"], w=["tmpB"])
                S_.op("dve", lambda e: e.scalar_tensor_tensor(out=tmpB[:], in0=tmpB[:], scalar=-TWO_PI, in1=tmpA[:], op0=ALU.mult, op1=ALU.add),
                      r=["tmpB", "tmpA"], w=["tmpB"])
                S_.op("dve", lambda e: e.tensor_scalar(out=tmpB[:], in0=tmpB[:], scalar1=shift, scalar2=-math.pi, op0=ALU.add, op1=ALU.max),
                      r=["tmpB"], w=["tmpB"])
                S_.op("dve", lambda e: e.tensor_scalar(out=tmpB[:], in0=tmpB[:], scalar1=math.pi, scalar2=0.0, op0=ALU.min, op1=ALU.add),
                      r=["tmpB"], w=["tmpB"])
                S_.op("act", lambda e: e.activation(out=dst[:], in_=tmpB[:], func=AF.Sin), r=["tmpB"], w=[key])

        def rope_evac(ps, pk, outs):
            qi = rot("sgt", 2)
            dbgm = os.environ.get("DBG_ROPE", "")
            S_.op("act", lambda e: e.activation(out=qraw[:, qi, :], in_=ps[:, :], func=AF.Copy), r=[pk], w=[("sgt", qi)])
            if dbgm == "1":
                return
            pr, pkr = bank()
            if dbgm != "4":
                S_.op("pe", lambda e: e.matmul(pr[:, :], rm2_bf[:], qraw[:, qi, :], start=True, stop=True), r=["rm2", ("sgt", qi)], w=[pkr])
            if dbgm == "3":
                return
            S_.op("dve", lambda e: e.tensor_tensor(out=tmpA[:], in0=ps[:, :], in1=cc2[:], op=ALU.mult), r=[pk, "cc2", ("sgt", qi)], w=["tmpA"])
            if dbgm == "4":
                return
            S_.op("dve", lambda e: e.tensor_tensor(out=tmpB[:], in0=pr[:, :], in1=ss2[:], op=ALU.mult), r=[pkr, "ss2"], w=["tmpB"])
            if dbgm == "2":
                return
            for (p0, p1), oap, wk in outs:
                S_.op("pool", lambda e: e.tensor_tensor(out=oap, in0=tmpA[p0:p1, :], in1=tmpB[p0:p1, :], op=ALU.add), r=["tmpA", "tmpB"], w=wk)

        prefetch_x(0)
        load_transpose(mem_d, 2)
        for l in range(2):
            norm_to_hT(40 * l + G_MEM, n=256)
            name = "w_mem_kv%d" % l
            slot, sk = wload([(lambda s: s[:, 0:8 * 512].rearrange("p (k n) -> p k n", k=8), kview(name), name)])
            for p in range(2):
                ps, pk = bank()
                for kc in range(8):
                    S_.op("pe", lambda e: e.matmul(ps[:, 0:256], slot[:, kc * 512 + p * 128:kc * 512 + (p + 1) * 128], hT[:, kc, 0:256],
                                                  start=(kc == 0), stop=(kc == 7)), r=[sk, hTk(kc)], w=[pk])
                S_.op("act", lambda e: e.activation(out=mkT[0:64, l, 2 * p, :], in_=ps[0:64, 0:256], func=AF.Copy, scale=0.125), r=[pk], w=[("mkT", l)])
                S_.op("act", lambda e: e.activation(out=mkT[64:128, l, 2 * p + 1, :], in_=ps[64:128, 0:256], func=AF.Copy, scale=0.125), r=[pk], w=[("mkT", l)])
            for mc in range(2):
                ps, pk = bank()
                for kc in range(8):
                    S_.op("pe", lambda e: e.matmul(ps[:, 0:256], hT[:, kc, mc * 128:(mc + 1) * 128], slot[:, kc * 512 + 256:kc * 512 + 512],
                                                  start=(kc == 0), stop=(kc == 7)), r=[sk, hTk(kc)], w=[pk])
                for h in range(4):
                    S_.op("act", lambda e: e.activation(out=mv[:, l, mc, h, (h % 2) * 64:(h % 2) * 64 + 64], in_=ps[:, h * 64:(h + 1) * 64], func=AF.Copy),
                          r=[pk], w=[("mv", l)])

        kva = kview("w_in_a")
        for blk in range(nblk):
            transpose_x()
            if blk + 1 < nblk:
                prefetch_x(blk + 1)

            norm_to_hT(G_MIXPRE)
            zk = []
            for ci_ in range(8, 16):
                zk += a22w(ci_)
            S_.op("pool", lambda e: e.memset(A22[96:128, 8:16, :], 0.0), w=zk)
            slot, sk = wload([(lambda s: s[:, 0:8 * 768].rearrange("p (k n) -> p k n", k=8), kva[:, :, 0:768], "w_in_a")])
            for qk in range(2):
                for h in range(4):
                    ps, pk = bank()
                    c0 = qk * 384 + h * 96
                    for kc in range(8):
                        S_.op("pe", lambda e: e.matmul(ps[0:96, :], slot[:, kc * 768 + c0:kc * 768 + c0 + 96], hT[:, kc, :], start=(kc == 0), stop=(kc == 7)),
                              r=[sk, hTk(kc)], w=[pk])
                    ci = 8 + qk * 4 + h
                    S_.op("act", lambda e: e.activation(out=A22[0:96, ci, :], in_=ps[0:96, :], func=AF.Copy), r=[pk], w=a22w(ci))
            slot, sk = wload([(lambda s: s[:, 0:8 * 768].rearrange("p (k n) -> p k n", k=8), kva[:, :, 1536:2304], "w_in_a")])
            for c in range(8):
                ps, pk = bank()
                for kc in range(8):
                    S_.op("pe", lambda e: e.matmul(ps[0:96, :], slot[:, kc * 768 + c * 96:kc * 768 + (c + 1) * 96], hT[:, kc, :], start=(kc == 0), stop=(kc == 7)),
                          r=[sk, hTk(kc)], w=[pk])
                S_.op("act", lambda e: e.activation(out=A22[0:96, c, :], in_=ps[0:96, :], func=AF.Sigmoid), r=[pk], w=[("A22", c)])
            slotA, skA = wload([(lambda s: s[:, 0:8 * 768].rearrange("p (k n) -> p k n", k=8), kva[:, :, 384:1152], "w_in_a")])
            slotB, skB = wload([
                (lambda s: s[:, 0:8 * 384].rearrange("p (k n) -> p k n", k=8), kva[:, :, 1152:1536], "w_in_a"),
                (lambda s: s[:, 8 * 384:8 * 384 + 64].rearrange("p (k n) -> p k n", k=8), kva[:, :, 2304:2312], "w_in_a"),
                (lambda s: s[:, 4096:4096 + 8 * 256].rearrange("p (k n) -> p k n", k=8), kva[:, :, 2312:2568], "w_in_a"),
            ])
            for p in range(2):
                ps, pk = bank()
                for kc in range(8):
                    S_.op("pe", lambda e: e.matmul(ps[:, :], slotB[:, 4096 + kc * 256 + p * 128:4096 + kc * 256 + (p + 1) * 128], hT[:, kc, :],
                                                  start=(kc == 0), stop=(kc == 7)), r=[skB, hTk(kc)], w=[pk])
                S_.op("act", lambda e: e.activation(out=mqT[:, p, :], in_=ps[:, :], func=AF.Copy), r=[pk], w=[("mqT", p)])

            qTv = lambda tsl: A22[0:96, 8:12, tsl]
            kTv = lambda tsl: A22[0:96, 12:16, tsl]
            qkeys_t = lambda t_: [("qk", 8 + h, t_) for h in range(4)]
            kkeys_t = lambda t_: [("qk", 12 + h, t_) for h in range(4)]
            def tnames(t):
                pb = t % 2
                return pb, slice(t * 128, (t + 1) * 128), "vaug%d" % pb, "g%d" % pb, "eb%d" % pb, "ea%d" % pb, "ktok%d" % pb, "PTm%d" % pb, "kraw%d" % pb

            def tileA1(t):
                pb, tsl, vk, gk, ebk, eak, ktk, ptk, krk = tnames(t)
                psk, pkk = bank()
                psv0, pkv0 = bank()
                psv1, pkv1 = bank()
                psg, pkg = bank()
                for kc in range(8):
                    S_.op("pe", lambda e: e.matmul(psg[:, 0:8], hT[:, kc, tsl], slotB[:, 8 * 384 + kc * 8:8 * 384 + (kc + 1) * 8], start=(kc == 0), stop=(kc == 7)),
                          r=[skB, hTk(kc)], w=[pkg])
                for kc in range(8):
                    S_.op("pe", lambda e: e.matmul(psk[:, 0:384], hT[:, kc, tsl], slotA[:, kc * 768:kc * 768 + 384], start=(kc == 0), stop=(kc == 7)),
                          r=[skA, hTk(kc)], w=[pkk])
                for kc in range(8):
                    S_.op("pe", lambda e: e.matmul(psv0[:, 0:384], hT[:, kc, tsl], slotA[:, kc * 768 + 384:kc * 768 + 768], start=(kc == 0), stop=(kc == 7)),
                          r=[skA, hTk(kc)], w=[pkv0])
                for kc in range(8):
                    S_.op("pe", lambda e: e.matmul(psv1[:, 0:384], hT[:, kc, tsl], slotB[:, kc * 384:(kc + 1) * 384], start=(kc == 0), stop=(kc == 7)),
                          r=[skB, hTk(kc)], w=[pkv1])
                S_.op("dve", lambda e: e.tensor_tensor(out=gsb[:, pb, :], in0=psg[:, 0:8], in1=gcol[:, G_BG:G_BG + 8], op=ALU.add), r=[pkg, "gcol"], w=[gk + "sb"])
                S_.op("act", lambda e: e.activation(out=gth[:, pb, :], in_=gsb[:, pb, :], func=AF.Exp, scale=-2.0 / 15.0), r=[gk + "sb"], w=[gk + "th"])
                S_.op("dve", lambda e: e.tensor_scalar(out=gsb[:, pb, :], in0=gth[:, pb, :], scalar1=1.0, scalar2=0.0, op0=ALU.add, op1=ALU.add), r=[gk + "th"], w=[gk + "sb"])
                S_.op("dve", lambda e: e.reciprocal(out=gsb[:, pb, :], in_=gsb[:, pb, :]), r=[gk + "sb"], w=[gk + "sb"])
                S_.op("dve", lambda e: e.tensor_scalar(out=gth[:, pb, :], in0=gth[:, pb, :], scalar1=-1.0, scalar2=1.0, op0=ALU.mult, op1=ALU.add), r=[gk + "th"], w=[gk + "th"])
                S_.op("dve", lambda e: e.tensor_tensor(out=gth[:, pb, :], in0=gth[:, pb, :], in1=gsb[:, pb, :], op=ALU.mult), r=[gk + "th", gk + "sb"], w=[gk + "th"])
                S_.op("act", lambda e: e.activation(out=gee[:, pb, :], in_=gth[:, pb, 4:8], func=AF.Exp, scale=-15.0), r=[gk + "th"], w=[gk + "ee"])
                S_.op("act", lambda e: e.activation(out=glp[:, pb, :], in_=gee[:, pb, :], func=AF.Ln, bias=1.0), r=[gk + "ee"], w=[gk + "lp"])
                S_.op("dve", lambda e: e.tensor_scalar(out=gig[:, pb, :], in0=gth[:, pb, 0:4], scalar1=15.0, scalar2=0.0, op0=ALU.mult, op1=ALU.add),
                      r=[gk + "th"], w=[gk + "ig"])
                S_.op("act", lambda e: e.activation(out=kraw[:, pb, :], in_=psk[:, 0:384], func=AF.Copy), r=[pkk], w=[krk])
                S_.op("act", lambda e: e.activation(out=vaug[:, pb, 0:2, 0:192], in_=psv0[:, 0:384].rearrange("p (h c) -> p h c", h=2), func=AF.Copy),
                      r=[pkv0], w=[vk])
                S_.op("act", lambda e: e.activation(out=vaug[:, pb, 2:4, 0:192], in_=psv1[:, 0:384].rearrange("p (h c) -> p h c", h=2), func=AF.Copy),
                      r=[pkv1], w=[vk])

            def tileA2a(t):
                pb, tsl, vk, gk, ebk, eak, ktk, ptk, krk = tnames(t)
                pcs, pkcs = bank()
                pa, pka = bank()
                pu, pku = bank()
                for h in range(4):
                    S_.op("pe", lambda e: e.matmul(pcs[0:96, h * 128:(h + 1) * 128], glp[:, pb, h:h + 1].broadcast_to([128, 96]), tri_f, start=True, stop=True),
                          r=[gk + "lp", "cst"], w=[pkcs])
                for h in range(4):
                    S_.op("pe", lambda e: e.matmul(pa[0:96, h * 128:(h + 1) * 128], glp[:, pb, h:h + 1].broadcast_to([128, 96]), tri_f, start=True, stop=False),
                          r=[gk + "lp", "cst"], w=[pka])
                    S_.op("pe", lambda e: e.matmul(pa[0:96, h * 128:(h + 1) * 128], gig[:, pb, h:h + 1].broadcast_to([128, 96]), ident, start=False, stop=True),
                          r=[gk + "ig", "cst"], w=[pka])
                S_.op("pe", lambda e: e.matmul(pu[:, 0:4], ust_f, glp[:, pb, :], start=True, stop=True), r=[gk + "lp", "cst"], w=[pku])
                S_.op("act", lambda e: e.activation(out=eb[0:96, pb, :, :], in_=pcs[0:96, :].rearrange("p (h j) -> p h j", h=4), func=AF.Exp, scale=-1.0),
                      r=[pkcs], w=[ebk])
                S_.op("act", lambda e: e.activation(out=ea[0:96, pb, :, :], in_=pa[0:96, :].rearrange("p (h j) -> p h j", h=4), func=AF.Exp),
                      r=[pka], w=[eak])
                S_.op("dve", lambda e: e.tensor_tensor(out=ga2[:, pb, :], in0=gig[:, pb, :], in1=pu[:, 0:4], op=ALU.subtract), r=[gk + "ig", pku], w=[gk + "a2"])
                S_.op("act", lambda e: e.activation(out=gek[:, pb, :], in_=ga2[:, pb, :], func=AF.Exp), r=[gk + "a2"], w=[gk + "ek"])
                S_.op("dve", lambda e: e.tensor_tensor(out=qTv(tsl), in0=qTv(tsl), in1=eb[0:96, pb, :, :], op=ALU.mult), r=qkeys_t(t) + [ebk], w=qkeys_t(t))
                S_.op("dve", lambda e: e.scalar_tensor_tensor(out=kTv(tsl), in0=kTv(tsl), scalar=DQK ** -0.5, in1=ea[0:96, pb, :, :], op0=ALU.mult, op1=ALU.mult),
                      r=kkeys_t(t) + [eak], w=kkeys_t(t))
                S_.op("dve", lambda e: e.scalar_tensor_tensor(out=ktok[:, pb, :, :], in0=kraw[:, pb, :].rearrange("p (h c) -> p h c", h=4), scalar=DQK ** -0.5,
                                                             in1=gek[:, pb, :].unsqueeze(2).broadcast_to([128, 4, 96]), op0=ALU.mult, op1=ALU.mult),
                      r=[krk, gk + "ek"], w=[ktk])

            def tileA2b(t):
                pb, tsl, vk, gk, ebk, eak, ktk, ptk, krk = tnames(t)
                pss, pks = bank()
                for h in range(4):
                    S_.op("pe", lambda e: e.matmul(pss[:, h * 128:(h + 1) * 128], A22[:, 12 + h, tsl], A22[:, 8 + h, tsl], start=True, stop=True),
                          r=[("qk", 12 + h, t), ("qk", 8 + h, t)], w=[pks])
                S_.op("dve", lambda e: e.tensor_tensor(out=PTm[:, pb, :, :], in0=pss[:, :].rearrange("p (h j) -> p h j", h=4),
                                                      in1=tri_f.unsqueeze(1).broadcast_to([128, 4, 128]), op=ALU.mult), r=[pks, "cst"], w=[ptk])

            def tileBn(t):
                pb, tsl, vk, gk, ebk, eak, ktk, ptk, krk = tnames(t)
                pn = [bank(), bank()]
                pd, pkd = bank()
                for c in range(2):
                    for h in range(4):
                        S_.op("pe", lambda e: e.matmul(pn[c][0][0:96, h * 128:(h + 1) * 128], vaug[:, pb, h, c * 96:(c + 1) * 96], PTm[:, pb, h, :], start=True, stop=False),
                              r=[vk, ptk], w=[pn[c][1]])
                        S_.op("pe", lambda e: e.matmul(pn[c][0][0:96, h * 128:(h + 1) * 128], CTb[:, h, c * 96:(c + 1) * 96], A22[:, 8 + h, tsl], start=False, stop=True),
                              r=["CTb", ("qk", 8 + h, t)], w=[pn[c][1]])
                for h in range(4):
                    S_.op("pe", lambda e: e.matmul(pd[0:96, h * 128:(h + 1) * 128], ones_bf[:, 0:96], PTm[:, pb, h, :], start=True, stop=False),
                          r=["ones", ptk], w=[pkd])
                    S_.op("pe", lambda e: e.matmul(pd[0:96, h * 128:(h + 1) * 128], CTb[:, h, 192:193].broadcast_to([128, 96]), A22[:, 8 + h, tsl], start=False, stop=True),
                          r=["CTb", ("qk", 8 + h, t)], w=[pkd])
                ri = rot("rdt", 2)
                S_.op("act", lambda e: e.activation(out=rdt[0:96, ri, :], in_=pd[0:96, :], func=AF.Abs), r=[pkd], w=[("rdt", ri)])
                S_.op("dve", lambda e: e.tensor_scalar(out=rdt[0:96, ri, :], in0=rdt[0:96, ri, :], scalar1=1.0, scalar2=0.0, op0=ALU.max, op1=ALU.add),
                      r=[("rdt", ri)], w=[("rdt", ri)])
                S_.op("act", lambda e: e.activation(out=rdt[0:96, ri, :], in_=rdt[0:96, ri, :], func=AF.Ln), r=[("rdt", ri)], w=[("rdt", ri)])
                S_.op("act", lambda e: e.activation(out=rdt[0:96, ri, :], in_=rdt[0:96, ri, :], func=AF.Exp, scale=-1.0), r=[("rdt", ri)], w=[("rdt", ri)])
                for c in range(2):
                    hk = [("b32", c * 4 + h) for h in range(4)]
                    S_.op("dve", lambda e: e.tensor_tensor(out=big32[0:96, c * 2048:(c + 1) * 2048].rearrange("p (h n) -> p h n", h=4)[:, :, tsl],
                                                          in0=pn[c][0][0:96, :].rearrange("p (h j) -> p h j", h=4),
                                                          in1=rdt[0:96, ri, :].rearrange("p (h j) -> p h j", h=4), op=ALU.mult),
                          r=[pn[c][1], ("rdt", ri)], w=hk)

            def tileBu(t):
                pb, tsl, vk, gk, ebk, eak, ktk, ptk, krk = tnames(t)
                pU = [bank(), bank()]
                for h in range(4):
                    S_.op("pe", lambda e: e.matmul(pU[h // 2][0][0:96, (h % 2) * 193:(h % 2) * 193 + 193], ktok[:, pb, h, :], vaug[:, pb, h, 0:193], start=True, stop=True),
                          r=[ktk, vk], w=[pU[h // 2][1]])
                for h in range(4):
                    S_.op("dve", lambda e: e.scalar_tensor_tensor(out=CT[0:96, h, :], in0=CT[0:96, h, :], scalar=eb[0:96, pb, h, 127:128],
                                                                 in1=pU[h // 2][0][0:96, (h % 2) * 193:(h % 2) * 193 + 193], op0=ALU.mult, op1=ALU.add),
                          r=["CT", ebk, pU[h // 2][1]], w=["CT"])
                S_.op("pool", lambda e: e.tensor_copy(out=CTb[0:96, :, 0:193], in_=CT[0:96, :, :]), r=["CT"], w=["CTb"])

            tileA1(0)
            tileA2a(0)
            tileA2b(0)
            for t in range(4):
                if t + 1 < 4:
                    tileA1(t + 1)
                tileBn(t)
                if t + 1 < 4:
                    tileA2a(t + 1)
                tileBu(t)
                if t + 1 < 4:
                    tileA2b(t + 1)

            def hn_sq(h):
                idx = []
                for c in range(2):
                    k = c * 4 + h
                    i = rot("sq", 4)
                    S_.op("act", lambda e: e.activation(out=sq[0:96, i, :], in_=big32[0:96, k * 512:(k + 1) * 512], func=AF.Square), r=[("b32", k)], w=[("sq", i)])
                    idx.append(i)
                return idx

            def hn_fin(h, idx):
                ps, pk = bank()
                for j, i in enumerate(idx):
                    S_.op("pe", lambda e: e.matmul(ps[0:96, :], ones_bf[0:96, 0:96], sq[0:96, i, :], start=(j == 0), stop=(j == 1)),
                          r=[("sq", i), "ones"], w=[pk])
                ri = rot("rdt", 2)
                S_.op("act", lambda e: e.activation(out=rdt[0:96, ri, :], in_=ps[0:96, :], func=AF.Ln, bias=epsc[0:96, :], scale=1.0 / DV),
                      r=[pk, "epsc"], w=[("rdt", ri)])
                S_.op("act", lambda e: e.activation(out=rdt[0:96, ri, :], in_=rdt[0:96, ri, :], func=AF.Exp, scale=-0.5), r=[("rdt", ri)], w=[("rdt", ri)])
                for c in range(2):
                    k = c * 4 + h
                    ch = 2 * h + c
                    S_.op("dve", lambda e: e.scalar_tensor_tensor(out=big32[0:96, k * 512:(k + 1) * 512], in0=big32[0:96, k * 512:(k + 1) * 512],
                                                                 scalar=gcol[0:96, G_MOUT + ch:G_MOUT + ch + 1], in1=rdt[0:96, ri, :], op0=ALU.mult, op1=ALU.mult),
                          r=[("b32", k), ("rdt", ri), "gcol"], w=[("b32", k)])
                    S_.op("pool", lambda e: e.tensor_tensor(out=mixT[0:96, ch, :], in0=big32[0:96, k * 512:(k + 1) * 512], in1=A22[0:96, ch, :], op=ALU.mult),
                          r=[("b32", k), ("A22", ch)], w=[("mixT", ch)])

            hidx = {0: hn_sq(0), 1: hn_sq(1)}
            for h in range(4):
                hn_fin(h, hidx[h])
                if h + 2 < 4:
                    hidx[h + 2] = hn_sq(h + 2)
            mem_attention(0, 8)
            out_proj(0, [96] * 8 + [128] * 2)
            post_norm_residual(G_MIXPOST)
            if stage >= 2:
                ffn(0, hook=lambda: rope_tables(blk))
            if stage >= 2.5:
                norm_to_hT(G_KV)
                slot, sk = wload([(lambda s: s[:, 0:8 * 512].rearrange("p (k n) -> p k n", k=8), kview("w_kv"), "w_kv")])
                pend = None
                for c2 in range(2):
                    ps, pk = bank()
                    for kc in range(8):
                        S_.op("pe", lambda e: e.matmul(ps[:, :], slot[:, kc * 512 + c2 * 128:kc * 512 + (c2 + 1) * 128], hT[:, kc, :], start=(kc == 0), stop=(kc == 7)),
                              r=[sk, hTk(kc)], w=[pk])
                    if pend is not None:
                        rope_evac(*pend)
                    pend = (ps, pk, [((0, 64), kpad[0:64, 2 * c2, 128:640], [("kpad", 2 * c2)]),
                                     ((64, 128), kpad[64:128, 2 * c2 + 1, 128:640], [("kpad", 2 * c2 + 1)])])
                pend_kv = pend
                for t in range(4):
                    ps, pk = bank()
                    for kc in range(8):
                        S_.op("pe", lambda e: e.matmul(ps[:, 0:256], hT[:, kc, t * 128:(t + 1) * 128], slot[:, kc * 512 + 256:kc * 512 + 512], start=(kc == 0), stop=(kc == 7)),
                              r=[sk, hTk(kc)], w=[pk])
                    if t == 0:
                        rope_evac(*pend_kv)
                    for kvh_ in range(4):
                        S_.op("act", lambda e: e.activation(out=vsh[:, 1 + t, kvh_, (kvh_ % 2) * 64:(kvh_ % 2) * 64 + 64], in_=ps[:, kvh_ * 64:(kvh_ + 1) * 64], func=AF.Copy),
                              r=[pk], w=[("vsh", 1 + t)])

            if stage >= 2.75:
                norm_to_hT(40 + G_MIXPRE, reuse_stats=(stage >= 2.5))
                slot, sk = wload([(lambda s: s[:, 0:6144].rearrange("p (k n) -> p k n", k=8), kview("w_in_b")[:, :, 0:768], "w_in_b")])
                slotM, skM = wload([(lambda s: s[:, 0:2048].rearrange("p (k n) -> p k n", k=8), kview("w_in_b")[:, :, 768:1024], "w_in_b")])
                pend = None
                for c in range(6):
                    ps, pk = bank()
                    for kc in range(8):
                        S_.op("pe", lambda e: e.matmul(ps[:, :], slot[:, kc * 768 + c * 128:kc * 768 + (c + 1) * 128], hT[:, kc, :], start=(kc == 0), stop=(kc == 7)),
                              r=[sk, hTk(kc)], w=[pk])
                    if pend is not None:
                        rope_evac(*pend)
                    pend = (ps, pk, [((0, 128), A22[:, c, :], [("A22", c)])])
                pend_q = pend
                for p in range(2):
                    ps, pk = bank()
                    for kc in range(8):
                        S_.op("pe", lambda e: e.matmul(ps[:, :], slotM[:, kc * 256 + p * 128:kc * 256 + (p + 1) * 128], hT[:, kc, :], start=(kc == 0), stop=(kc == 7)),
                              r=[skM, hTk(kc)], w=[pk])
                    S_.op("act", lambda e: e.activation(out=mqT[:, p, :], in_=ps[:, :], func=AF.Copy), r=[pk], w=[("mqT", p)])
                    if p == 0:
                        rope_evac(*pend_q)
                def swa_scores(c):
                    res = []
                    for e_, hq in enumerate(PAIRS[c]):
                        kvh = hq // 3
                        pc, pkc = bank()
                        pp, pkp = bank()
                        for qt in range(4):
                            qs = slice(qt * 128, (qt + 1) * 128)
                            S_.op("pe", lambda e: e.matmul(pc[:, qs], kpad[:, kvh, 128 + qt * 128:256 + qt * 128], A22[:, c, qs], start=True, stop=True),
                                  r=[("kpad", kvh), ("A22", c)], w=[pkc])
                            S_.op("pe", lambda e: e.matmul(pp[:, qs], kpad[:, kvh, qt * 128:128 + qt * 128], A22[:, c, qs], start=True, stop=True),
                                  r=[("kpad", kvh), ("A22", c)], w=[pkp])
                        ic = rot("pt", 8)
                        ip = rot("pt", 8)
                        S_.op("act", lambda e: e.activation(out=PT[:, ic, :], in_=pc[:, :], func=AF.Exp, scale=0.125), r=[pkc], w=[("PT", ic)])
                        S_.op("act", lambda e: e.activation(out=PT[:, ip, :], in_=pp[:, :], func=AF.Exp, scale=0.125), r=[pkp], w=[("PT", ip)])
                        S_.op("dve", lambda e: e.tensor_tensor(out=PT[:, ic, :], in0=PT[:, ic, :], in1=tri4[:], op=ALU.mult), r=[("PT", ic), "tri4"], w=[("PT", ic)])
                        mk_ = ust4f if blk == 0 else ust4
                        S_.op("pool", lambda e: e.tensor_tensor(out=PT[:, ip, :], in0=PT[:, ip, :], in1=mk_[:], op=ALU.mult),
                              r=[("PT", ip), "ust4", "ust4f"], w=[("PT", ip)])
                        res.append((kvh, ic, ip))
                    return res

                def swa_out(c, res):
                    po, pko = bank()
                    pdn, pkdn = bank()
                    for qt in range(4):
                        qs = slice(qt * 128, (qt + 1) * 128)
                        n = 0
                        for (kvh, ic, ip) in res:
                            S_.op("pe", lambda e: e.matmul(po[:, qs], vsh[:, 1 + qt, kvh, :], PT[:, ic, qs], start=(n == 0), stop=False),
                                  r=[("vsh", 1 + qt), ("PT", ic)], w=[pko])
                            n += 1
                            S_.op("pe", lambda e: e.matmul(po[:, qs], vsh[:, qt, kvh, :], PT[:, ip, qs], start=False, stop=(n == 3)),
                                  r=[("vsh", qt), ("PT", ip)], w=[pko])
                            n += 1
                    n = 0
                    for e_, (kvh, ic, ip) in enumerate(res):
                        for i_ in (ic, ip):
                            S_.op("pe", lambda e: e.matmul(pdn[:, :], onesAB[:, e_, :], PT[:, i_, :], start=(n == 0), stop=(n == 3)), r=["onesAB", ("PT", i_)], w=[pkdn])
                            n += 1
                    ri = rot("rdt", 2)
                    S_.op("act", lambda e: e.activation(out=rdt[:, ri, :], in_=pdn[:, :], func=AF.Ln, bias=esink[:, c:c + 1]),
                          r=[pkdn, "esink"], w=[("rdt", ri)])
                    S_.op("act", lambda e: e.activation(out=rdt[:, ri, :], in_=rdt[:, ri, :], func=AF.Exp, scale=-1.0), r=[("rdt", ri)], w=[("rdt", ri)])
                    S_.op("dve", lambda e: e.tensor_tensor(out=mixT[:, c, :], in0=po[:, :], in1=rdt[:, ri, :], op=ALU.mult),
                          r=[pko, ("rdt", ri)], w=[("mixT", c)])

                prev = None
                for c in range(6 if stage >= 2.9 else 0):
                    cur = swa_scores(c)
                    if prev is not None:
                        swa_out(*prev)
                    prev = (c, cur)
                if prev is not None:
                    swa_out(*prev)
            if stage >= 3:
                S_.op("pool", lambda e: e.tensor_copy(out=kpad[:, :, 0:128], in_=kpad[:, :, 512:640]), r=[("kpad", i) for i in range(4)], w=[("kpad", i) for i in range(4)])
                S_.op("pool", lambda e: e.tensor_copy(out=vsh[:, 0, :, :], in_=vsh[:, 4, :, :]), r=[("vsh", 4)], w=[("vsh", 0)])
                mem_attention(1, 6)
                out_proj(1, [128] * 8)
                post_norm_residual(40 + G_MIXPOST)
            if stage >= 4:
                ffn(1)
            store_out(blk)
        S_.finish()
        print("instructions emitted:", S_.n_instr, {k: v for k, v in S_.cnt.items()})
    return nc


def make_consts():
    cst = np.zeros((128, 512), np.float32)
    cst[:, 0:128] = np.eye(128, dtype=np.float32)
    s = np.arange(128)[:, None]
    j = np.arange(128)[None, :]
    cst[:, 128:256] = (s <= j).astype(np.float32)
    cst[:, 256:384] = (s > j).astype(np.float32)
    rm = np.zeros((64, 64), np.float32)
    for i in range(32):
        rm[32 + i, i] = -1.0
        rm[i, 32 + i] = 1.0
    cst[0:64, 384:448] = rm
    cst[64:128, 448:512] = rm
    return cst


def make_gcol(inp):
    g = np.zeros((128, NG), np.float32)

    def colz(v):
        return np.ascontiguousarray(np.asarray(v, np.float32).reshape(8, 128).T)

    for l in range(2):
        g[:, 40 * l + G_MIXPRE:40 * l + G_MIXPRE + 8] = colz(inp["g_mix_pre"][l])
        g[:, 40 * l + G_MIXPOST:40 * l + G_MIXPOST + 8] = colz(inp["g_mix_post"][l])
        g[:, 40 * l + G_FFNPRE:40 * l + G_FFNPRE + 8] = colz(inp["g_ffn_pre"][l])
        g[:, 40 * l + G_FFNPOST:40 * l + G_FFNPOST + 8] = colz(inp["g_ffn_post"][l])
        g[:, 40 * l + G_MEM:40 * l + G_MEM + 8] = colz(inp["g_mem"][l])
    g[:, G_KV:G_KV + 8] = colz(inp["g_kv"])
    g[0:96, G_MOUT:G_MOUT + 8] = np.asarray(inp["g_mlstm_out"][0], np.float32).reshape(8, 96).T
    g[:, G_BG:G_BG + 8] = np.asarray(inp["b_gates_a"][0], np.float32)[None, :]
    sk_ = np.asarray(inp["sinks_b"][0], np.float32)
    for c, (a, b) in enumerate(PAIRS):
        g[0:64, G_SINK + c] = sk_[a]
        g[64:128, G_SINK + c] = sk_[b]
    inv = (1.0 / (np.float32(10000.0) ** (np.arange(0, 64, 2, dtype=np.float32) / np.float32(64)))).astype(np.float32)
    g[:, G_INVF] = np.tile(inv, 4)
    return g


def make_in_maps(inp, ncores=8):
    shared = {
        "w_mem_kv0": inp["w_mem_kv"][0], "w_mem_kv1": inp["w_mem_kv"][1], "w_in_a": inp["w_in_a"][0],
        "w_out0": inp["w_out"][0], "w_out1": inp["w_out"][1], "w_ffn_in0": inp["w_ffn_in"][0], "w_ffn_in1": inp["w_ffn_in"][1],
        "w_ffn_out0": inp["w_ffn_out"][0], "w_ffn_out1": inp["w_ffn_out"][1], "w_kv": inp["w_kv"], "w_in_b": inp["w_in_b"][0],
    }
    shared = {k: np.ascontiguousarray(np.asarray(v, np.float32)) for k, v in shared.items()}
    shared["gcol"] = make_gcol(inp)
    shared["cst"] = make_consts()
    maps = []
    for b in range(ncores):
        m = dict(shared)
        m["x"] = np.ascontiguousarray(np.asarray(inp["x"][b], np.float32))
        m["mem"] = np.ascontiguousarray(np.asarray(inp["mem"][b], np.float32))
        m["posr"] = np.ascontiguousarray(np.broadcast_to(np.asarray(inp["positions"][b], np.int32)[None, :], (128, S)))
        maps.append(m)
    return maps


_NC_CACHE = {}


def kernel(**inputs):
    inp = {k: np.asarray(v) for k, v in inputs.items()}
    if "full" not in _NC_CACHE:
        _NC_CACHE["full"] = build()
    nc = _NC_CACHE["full"]
    maps = make_in_maps(inp, 8)
    res = run_bass_kernel_spmd(nc, maps, core_ids=list(range(8)))
    out = np.stack([np.asarray(r["out"], np.float32) for r in res.results], axis=0)
    return out
```
